# Optimizing a Trainium2 kernel written in Bass

```python
import jax, jax.numpy as jnp
from jax import lax
import numpy as np

D_MODEL = 2048
BATCH = 8
SEQ = 4096
DEPTH = 2

HEAD_DIM = 128
D_MIX = D_MODEL
N_HEADS_A = 10
N_HEADS_B = 6
N_HEADS_C = 8
N_HEADS_D = 8
IDX_HEADS = 16
IDX_DIM = 64
TOPK_MAX = 256
DILATED = ((128, 1), (512, 4), (2048, 16))
BLOCK = 128
ROPE_THETA = 10000.0
EPS = 1e-6

A_Q = N_HEADS_A * HEAD_DIM
IDX_Q = IDX_HEADS * IDX_DIM
B_W = N_HEADS_B * HEAD_DIM
C_W = N_HEADS_C * HEAD_DIM
D_W = N_HEADS_D * HEAD_DIM
SPLIT0 = (A_Q, HEAD_DIM, HEAD_DIM, IDX_Q, IDX_DIM, IDX_HEADS, B_W, B_W, B_W, D_MIX)
SPLIT1 = (C_W, C_W, C_W, D_W, D_W, D_W, N_HEADS_D, D_MIX)
IN0 = sum(SPLIT0)
IN1 = sum(SPLIT1)

kernel_name = 'hybrid_dsa_dilated_stickbreak_fox'

F32 = jnp.float32


def rmsnorm(x, g):
    xf = x.astype(F32)
    y = xf * lax.rsqrt(jnp.mean(xf * xf, axis=-1, keepdims=True) + EPS)
    return (y * g.astype(F32)).astype(x.dtype)


def rope(t, pos):
    half = t.shape[-1] // 2
    inv = ROPE_THETA ** (-jnp.arange(half, dtype=F32) / half)
    ang = pos.astype(F32)[:, None] * inv[None, :]
    cos = jnp.cos(ang)[None, :, None, :]
    sin = jnp.sin(ang)[None, :, None, :]
    tf = t.astype(F32)
    t1, t2 = tf[..., :half], tf[..., half:]
    return jnp.concatenate([t1 * cos - t2 * sin, t2 * cos + t1 * sin], axis=-1).astype(t.dtype)


def split_cols(t, sizes):
    offs, acc = [], 0
    for s in sizes[:-1]:
        acc += s
        offs.append(acc)
    return jnp.split(t, offs, axis=-1)


def blocks_to_seq(out):
    nb, bsz, q, h, d = out.shape
    return out.transpose(1, 0, 2, 3, 4).reshape(bsz, nb * q, h, d)


def dsa_attention(q, k, v, iq, ik, iw):
    S = q.shape[1]
    topk = min(TOPK_MAX, S // 4)
    key_pos = jnp.arange(S)
    ikf = ik.astype(F32)

    def block(b):
        start = b * BLOCK
        qb = lax.dynamic_slice_in_dim(q, start, BLOCK, axis=1)
        iqb = lax.dynamic_slice_in_dim(iq, start, BLOCK, axis=1).astype(F32)
        iwb = lax.dynamic_slice_in_dim(iw, start, BLOCK, axis=1).astype(F32)
        qpos = start + jnp.arange(BLOCK)
        dots = jnp.einsum('bqhd,bsd->bqhs', iqb, ikf) * (IDX_DIM ** -0.5)
        score = jnp.einsum('bqh,bqhs->bqs', iwb * (IDX_HEADS ** -0.5), jax.nn.relu(dots))
        causal = key_pos[None, :] <= qpos[:, None]
        score = jnp.where(causal[None], score, -jnp.inf)
        _, sel = lax.top_k(score, topk)
        valid = sel <= qpos[None, :, None]
        ks = jax.vmap(lambda kk, ii: kk[ii])(k, sel)
        vs = jax.vmap(lambda vv, ii: vv[ii])(v, sel)
        logits = jnp.einsum('bqhd,bqkd->bqhk', qb, ks).astype(F32) * (HEAD_DIM ** -0.5)
        logits = jnp.where(valid[:, :, None, :], logits, -jnp.inf)
        p = jax.nn.softmax(logits, axis=-1)
        return jnp.einsum('bqhk,bqkd->bqhd', p.astype(v.dtype), vs)

    return blocks_to_seq(lax.map(block, jnp.arange(S // BLOCK)))


def strided_band(q, k, v, dil, steps):
    bsz, S, H, hd = q.shape
    L = S // dil
    nb = -(-L // BLOCK)
    Lp = nb * BLOCK

    def to_blocks(t):
        t = t.reshape(bsz, L, dil, H, hd).transpose(0, 2, 1, 3, 4)
        t = jnp.pad(t, ((0, 0), (0, 0), (0, Lp - L), (0, 0), (0, 0)))
        return t.reshape(bsz, dil, nb, BLOCK, H, hd)

    def with_prev(t):
        prev = jnp.pad(t, ((0, 0), (0, 0), (1, 0), (0, 0), (0, 0), (0, 0)))[:, :, :-1]
        return jnp.concatenate([prev, t], axis=3)

    qb = to_blocks(q)
    kk = with_prev(to_blocks(k))
    vv = with_prev(to_blocks(v))
    logits = jnp.einsum('brnqhd,brnkhd->brnhqk', qb, kk).astype(F32) * (HEAD_DIM ** -0.5)
    qi = jnp.arange(BLOCK)[:, None] + BLOCK
    ki = jnp.arange(2 * BLOCK)[None, :]
    dist = qi - ki
    band = (dist >= 0) & (dist <= steps)
    blk = jnp.arange(nb)[:, None, None]
    valid = band[None] & ((ki[None] >= BLOCK) | (blk > 0))
    logits = jnp.where(valid[None, None, :, None], logits, -jnp.inf)
    m = jnp.max(logits, axis=-1, keepdims=True)
    p = jnp.exp(logits - m)
    l = jnp.sum(p, axis=-1, keepdims=True)
    o = jnp.einsum('brnhqk,brnkhd->brnhqd', p, vv.astype(F32)) / l
    lse = (m + jnp.log(l))[..., 0]
    o = o.transpose(0, 1, 2, 4, 3, 5).reshape(bsz, dil, Lp, H, hd)[:, :, :L]
    o = o.transpose(0, 2, 1, 3, 4).reshape(bsz, S, H, hd)
    lse = lse.transpose(0, 1, 2, 4, 3).reshape(bsz, dil, Lp, H)[:, :, :L]
    lse = lse.transpose(0, 2, 1, 3).reshape(bsz, S, H)
    return o, lse


def dilated_attention(q, k, v):
    outs, lses = [], []
    for window, dil in DILATED:
        o, lse = strided_band(q, k, v, dil, window // dil)
        outs.append(o)
        lses.append(lse)
    alpha = jax.nn.softmax(jnp.stack(lses, axis=0), axis=0)
    o = jnp.einsum('pbsh,pbshd->bshd', alpha, jnp.stack(outs, axis=0))
    return o.astype(q.dtype)


def stick_breaking_attention(q, k, v):
    S = q.shape[1]
    key_pos = jnp.arange(S)

    def block(b):
        start = b * BLOCK
        qb = lax.dynamic_slice_in_dim(q, start, BLOCK, axis=1)
        qpos = start + jnp.arange(BLOCK)
        z = jnp.einsum('bqhd,bshd->bhqs', qb, k).astype(F32) * (HEAD_DIM ** -0.5)
        before = (key_pos[None, :] < qpos[:, None])[None, None]
        log_1m = jnp.where(before, jax.nn.log_sigmoid(-z), 0.0)
        tail = lax.cumsum(log_1m, axis=3, reverse=True) - log_1m
        w = jnp.where(before, jnp.exp(jax.nn.log_sigmoid(z) + tail), 0.0)
        return jnp.einsum('bhqs,bshd->bqhd', w.astype(v.dtype), v)

    return blocks_to_seq(lax.map(block, jnp.arange(S // BLOCK)))


def forgetting_attention(q, k, v, log_f):
    S = q.shape[1]
    key_pos = jnp.arange(S)
    c = jnp.cumsum(log_f.astype(F32), axis=1).transpose(0, 2, 1)

    def block(b):
        start = b * BLOCK
        qb = lax.dynamic_slice_in_dim(q, start, BLOCK, axis=1)
        cb = lax.dynamic_slice_in_dim(c, start, BLOCK, axis=2)
        qpos = start + jnp.arange(BLOCK)
        logits = jnp.einsum('bqhd,bshd->bhqs', qb, k).astype(F32) * (HEAD_DIM ** -0.5)
        logits = logits + (cb[..., :, None] - c[..., None, :])
        causal = (key_pos[None, :] <= qpos[:, None])[None, None]
        p = jax.nn.softmax(jnp.where(causal, logits, -jnp.inf), axis=-1)
        return jnp.einsum('bhqs,bshd->bqhd', p.astype(v.dtype), v)

    return blocks_to_seq(lax.map(block, jnp.arange(S // BLOCK)))


def layer_even(x, norm_g, w_in, w_out):
    bsz, S, _ = x.shape
    pos = jnp.arange(S)
    h = rmsnorm(x, norm_g)
    qa, ka, va, iq, ik, iw, qb, kb, vb, gate = split_cols(h @ w_in, SPLIT0)
    qa = rope(qa.reshape(bsz, S, N_HEADS_A, HEAD_DIM), pos)
    ka = rope(ka.reshape(bsz, S, 1, HEAD_DIM), pos)[:, :, 0]
    iq = rope(iq.reshape(bsz, S, IDX_HEADS, IDX_DIM), pos)
    ik = rope(ik.reshape(bsz, S, 1, IDX_DIM), pos)[:, :, 0]
    o_a = dsa_attention(qa, ka, va, iq, ik, iw)
    qb = rope(qb.reshape(bsz, S, N_HEADS_B, HEAD_DIM), pos)
    kb = rope(kb.reshape(bsz, S, N_HEADS_B, HEAD_DIM), pos)
    vb = vb.reshape(bsz, S, N_HEADS_B, HEAD_DIM)
    o_b = dilated_attention(qb, kb, vb)
    y = jnp.concatenate([o_a.reshape(bsz, S, A_Q), o_b.reshape(bsz, S, B_W)], axis=-1)
    return x + (y * jax.nn.silu(gate)) @ w_out


def layer_odd(x, norm_g, w_in, b_f, w_out):
    bsz, S, _ = x.shape
    h = rmsnorm(x, norm_g)
    qc, kc, vc, qd, kd, vd, fl, gate = split_cols(h @ w_in, SPLIT1)
    shp_c = (bsz, S, N_HEADS_C, HEAD_DIM)
    shp_d = (bsz, S, N_HEADS_D, HEAD_DIM)
    o_c = stick_breaking_attention(qc.reshape(shp_c), kc.reshape(shp_c), vc.reshape(shp_c))
    log_f = jax.nn.log_sigmoid(fl.astype(F32) + b_f.astype(F32))
    o_d = forgetting_attention(qd.reshape(shp_d), kd.reshape(shp_d), vd.reshape(shp_d), log_f)
    y = jnp.concatenate([o_c.reshape(bsz, S, C_W), o_d.reshape(bsz, S, D_W)], axis=-1)
    return x + (y * jax.nn.silu(gate)) @ w_out


def setup_inputs(seed: int = 0) -> dict:
    key = jax.random.key(seed)
    ks = jax.random.split(key, 10)
    nrm = jax.random.normal
    return {
        'x': nrm(ks[0], (BATCH, SEQ, D_MODEL), F32),
        'norm0': 1.0 + 0.02 * nrm(ks[1], (D_MODEL,), F32),
        'w_in0': nrm(ks[2], (D_MODEL, IN0), F32) * D_MODEL ** -0.5,
        'w_out0': nrm(ks[3], (D_MIX, D_MODEL), F32) * D_MIX ** -0.5,
        'norm1': 1.0 + 0.02 * nrm(ks[4], (D_MODEL,), F32),
        'w_in1': nrm(ks[5], (D_MODEL, IN1), F32) * D_MODEL ** -0.5,
        'b_f1': 2.0 + 0.5 * nrm(ks[6], (N_HEADS_D,), F32),
        'w_out1': nrm(ks[7], (D_MIX, D_MODEL), F32) * D_MIX ** -0.5,
        'norm_f': 1.0 + 0.02 * nrm(ks[8], (D_MODEL,), F32),
    }


def reference(x, norm0, w_in0, w_out0, norm1, w_in1, b_f1, w_out1, norm_f):
    for i in range(DEPTH):
        if i % 2 == 0:
            x = layer_even(x, norm0, w_in0, w_out0)
        else:
            x = layer_odd(x, norm1, w_in1, b_f1, w_out1)
    return rmsnorm(x, norm_f)
```

```python
import contextlib
import math
import numpy as np
import ml_dtypes
import concourse.bass as bass
import concourse.mybir as mybir
from concourse.bass_utils import run_bass_kernel_spmd

F32 = mybir.dt.float32
BF16 = mybir.dt.bfloat16
AF = mybir.ActivationFunctionType
ALU = mybir.AluOpType

S = 4096
D = 2048
NB = 32
EPS = 1e-6
NEG = -3.0e38
SC128 = 128 ** -0.5
IN0 = 6992
IN1 = 8200

EPOCH = 20000


class DSem:
    def __init__(self, handle):
        self.h = handle
        self.cnt = 0


class Buf:
    def __init__(self, ap, name):
        self.ap = ap
        self.name = name
        self.w = {}
        self.r = {}
        self.ds = None

    def __getitem__(self, k):
        return self.ap[k]


class KB:
    def __init__(self, nc):
        self.nc = nc
        self.eng = {"pe": nc.tensor, "act": nc.scalar, "dve": nc.vector,
                    "pool": nc.gpsimd, "sp": nc.sync}
        self.n = {e: 0 for e in self.eng}
        self.esems = {}
        self.waited = {e: {} for e in self.eng}
        self.gstack = contextlib.ExitStack()
        self.pstack = None
        self.free_ds = []
        self.all_ds = []
        self.phase_bufs = []

    def begin_phase(self):
        self.pstack = contextlib.ExitStack()
        self.phase_bufs = []
        self.pid = getattr(self, "pid", 0) + 1

    def end_phase(self):
        self.barrier()
        for b in self.phase_bufs:
            if b.ds is not None:
                self.free_ds.append(b.ds)
                b.ds = None
        self.pstack.close()
        self.pstack = None

    def sbuf(self, name, shape, dtype, glob=False):
        st = self.gstack if glob else self.pstack
        name = name if glob else f"{name}_p{self.pid}"
        t = st.enter_context(self.nc.sbuf_tensor(name, list(shape), dtype))
        b = Buf(t, name)
        if not glob:
            self.phase_bufs.append(b)
        return b

    def psum(self, name, shape, dtype):
        t = self.pstack.enter_context(self.nc.psum_tensor(f"{name}_p{self.pid}", list(shape), dtype))
        b = Buf(t, name)
        self.phase_bufs.append(b)
        return b

    def dram(self, name, shape, dtype, kind="Internal"):
        t = self.nc.dram_tensor(name, list(shape), dtype, kind=kind).ap()
        return Buf(t, name)

    def _get_ds(self, buf):
        if buf.ds is None:
            if self.free_ds:
                buf.ds = self.free_ds.pop()
            else:
                h = self.gstack.enter_context(self.nc.semaphore(f"d{len(self.all_ds)}"))
                buf.ds = DSem(h)
                self.all_ds.append(buf.ds)
        return buf.ds

    def _esem(self, ename, epoch):
        key = (ename, epoch)
        if key not in self.esems:
            self.esems[key] = self.gstack.enter_context(
                self.nc.semaphore(f"e_{ename}_{epoch}"))
        return self.esems[key]

    def wait(self, ename, tok):
        if tok[0] == "e":
            _, src, idx = tok
            k = ("e", src)
            if self.waited[ename].get(k, 0) >= idx:
                return
            epoch = (idx - 1) // EPOCH
            self.eng[ename].wait_ge(self._esem(src, epoch), idx - epoch * EPOCH)
            self.waited[ename][k] = idx
        else:
            _, ds, val = tok
            k = ("d", id(ds))
            if self.waited[ename].get(k, 0) >= val:
                return
            self.eng[ename].wait_ge(ds.h, val)
            self.waited[ename][k] = val

    def _deps(self, ename, reads, writes, is_dma):
        for b in reads:
            for tok in b.w.values():
                if (not is_dma) and tok[0] == "e" and tok[1] == ename and ename == "pe":
                    continue
                self.wait(ename, tok)
        for b in writes:
            for tok in list(b.w.values()) + list(b.r.values()):
                if (not is_dma) and tok[0] == "e" and tok[1] == ename:
                    continue
                self.wait(ename, tok)

    @staticmethod
    def _key(tok):
        return ("e", tok[1]) if tok[0] == "e" else ("d", id(tok[1]))

    def op(self, ename, fn, reads=(), writes=()):
        self._deps(ename, reads, writes, False)
        ins = fn(self.eng[ename])
        self.n[ename] += 1
        idx = self.n[ename]
        epoch = (idx - 1) // EPOCH
        ins.then_inc(self._esem(ename, epoch), 1)
        tok = ("e", ename, idx)
        k = self._key(tok)
        for b in writes:
            b.w[k] = tok
        for b in reads:
            b.r[k] = tok
        return tok

    def dma(self, qname, out_ap, in_ap, sb, reads=(), writes=(), **kw):
        self._deps(qname, reads, writes, True)
        ds = self._get_ds(sb)
        ins = self.eng[qname].dma_start(out=out_ap, in_=in_ap, **kw)
        ds.cnt += 16
        ins.then_inc(ds.h, 16)
        tok = ("d", ds, ds.cnt)
        k = self._key(tok)
        for b in writes:
            b.w[k] = tok
        for b in reads:
            b.r[k] = tok
        return tok

    def barrier(self):
        toks = [("e", e, self.n[e]) for e in self.eng if self.n[e] > 0]
        toks += [("d", ds, ds.cnt) for ds in self.all_ds if ds.cnt > 0]
        for e in self.eng:
            for t in toks:
                if t[0] == "e" and t[1] == e:
                    continue
                self.wait(e, t)


def mm(kb, out, lhsT, rhs, start, stop, reads, writes):
    return kb.op("pe", lambda e: e.matmul(out, lhsT=lhsT, rhs=rhs, start=start, stop=stop),
                 reads=reads, writes=writes)


def act(kb, out, in_, func, reads, writes, **kw):
    return kb.op("act", lambda e: e.activation(out=out, in_=in_, func=func, **kw),
                 reads=reads, writes=writes)


def tt(kb, eng, out, in0, in1, op, reads, writes):
    return kb.op(eng, lambda e: e.tensor_tensor(out=out, in0=in0, in1=in1, op=op),
                 reads=reads, writes=writes)


class C:
    IDENT, LT, GE, LE, ONES, NEGU, NEGONES = range(7)


def cbs(cb, i, n=1):
    return cb[:, i * 128:(i + n) * 128]


def phase_inproj(kb, cb, xsrc, norm, w_in, fm_chunks, tm_pieces, cs128, cs64):
    kb.begin_phase()
    TG = 2048
    NTT = TG // 128
    hT = kb.sbuf("hT", [128, 16, TG], BF16)
    gsb = kb.sbuf("gsb", [128, 16], F32)
    xt = [kb.sbuf(f"xt{i}", [128, D], F32) for i in range(2)]
    xn = [kb.sbuf(f"xn{i}", [128, D], BF16) for i in range(2)]
    sq = kb.sbuf("sq", [128, D], BF16)
    stt = [kb.sbuf(f"stt{i}", [128, 4], F32) for i in range(2)]
    tp = [kb.psum(f"tp{i}", [128, 8, 128], BF16) for i in range(2)]
    pA = [kb.psum(f"pA{i}", [128, 512], F32) for i in range(2)]
    pB = [kb.psum(f"pB{i}", [128, 512], F32) for i in range(2)]
    pT = [kb.psum(f"pT{i}", [128, 512], F32) for i in range(2)]
    wA = [kb.sbuf(f"wA{i}", [128, 16, 128], BF16) for i in range(2)]
    wB = [kb.sbuf(f"wB{i}", [128, 16, 128], BF16) for i in range(2)]
    wT = [kb.sbuf(f"wT{i}", [128, 16, 512], BF16) for i in range(2)]
    t1 = [kb.sbuf(f"t1_{i}", [128, 512], F32) for i in range(2)]
    t2 = [kb.sbuf(f"t2_{i}", [128, 512], F32) for i in range(2)]
    ob = [kb.sbuf(f"ob{i}", [128, 512], BF16) for i in range(3)]
    o32 = [kb.sbuf(f"o32_{i}", [128, 512], F32) for i in range(2)]
    otm = [kb.sbuf(f"otm{i}", [128, 512], BF16) for i in range(2)]
    o16 = [kb.sbuf(f"o16_{i}", [128, 16], F32) for i in range(2)]
    csA = kb.sbuf("csA", [128, 2, TG], F32)
    csB = kb.sbuf("csB", [128, 2, TG], F32) if cs64 is not None else None

    kb.dma("sp", gsb[:], norm.ap.rearrange("(k p) -> p k", p=128), gsb, reads=[norm],
           writes=[gsb], allow_slow_non_contiguous=True)
    ident = cbs(cb, C.IDENT)
    wsrc = w_in.ap.rearrange("(k p) n -> p k n", p=128)

    def load_w(dst, segs):
        for (do, so, n) in segs:
            for k0 in range(0, 16, 8):
                kb.dma("pool", dst[:, k0:k0 + 8, do:do + n], wsrc[:, k0:k0 + 8, so:so + n], dst,
                       reads=[w_in], writes=[dst])

    cnt = {"a": 0, "o": 0, "t": 0, "o32": 0, "tm": 0, "otm": 0}
    for G in range(S // TG):
        g0 = G * TG
        if cs128 is not None:
            kb.dma("sp", csA[:], cs128.ap[:, :, g0:g0 + TG].rearrange("c p t -> p c t"), csA,
                   reads=[cs128], writes=[csA])
        if cs64 is not None:
            kb.dma("sp", csB[:], cs64.ap[:, :, g0:g0 + TG].rearrange("c p t -> p c t"), csB,
                   reads=[cs64], writes=[csB])
        for i in range(NTT):
            b = i % 2
            r0 = g0 + i * 128
            kb.dma("sp", xt[b][:], xsrc.ap[r0:r0 + 128, :], xt[b], reads=[xsrc], writes=[xt[b]])
            act(kb, sq[:], xt[b][:], AF.Square, [xt[b]], [sq, stt[b]], accum_out=stt[b][:, 0:1])
            kb.op("dve", lambda e: e.tensor_scalar(out=stt[b][:, 1:2], in0=stt[b][:, 0:1],
                                                   scalar1=1.0 / D, scalar2=EPS, op0=ALU.mult,
                                                   op1=ALU.add), reads=[stt[b]], writes=[stt[b]])
            act(kb, stt[b][:, 2:3], stt[b][:, 1:2], AF.Ln, [stt[b]], [stt[b]])
            act(kb, stt[b][:, 3:4], stt[b][:, 2:3], AF.Exp, [stt[b]], [stt[b]], scale=-0.5)
            kb.op("dve", lambda e: e.tensor_scalar(out=xn[b][:], in0=xt[b][:],
                                                   scalar1=stt[b][:, 3:4], scalar2=None,
                                                   op0=ALU.mult), reads=[xt[b], stt[b]],
                  writes=[xn[b]])
            for half in range(2):
                for k in range(8):
                    kk = half * 8 + k
                    kb.op("pe", lambda e: e.transpose(out=tp[half][:, k, :],
                                                      in_=xn[b][:, kk * 128:(kk + 1) * 128],
                                                      identity=ident),
                          reads=[xn[b], cb], writes=[tp[half]])
                tt(kb, "dve", hT[:, half * 8:(half + 1) * 8, i * 128:(i + 1) * 128],
                   tp[half][:, :, :],
                   gsb[:, half * 8:(half + 1) * 8].unsqueeze(2).broadcast_to([128, 8, 128]),
                   ALU.mult, [tp[half], gsb], [hT])
        nfm = len(fm_chunks)
        if nfm:
            load_w(wA[0], fm_chunks[0]["segA"])
            if fm_chunks[0]["segB"]:
                load_w(wB[0], fm_chunks[0]["segB"])
        for ci, ch in enumerate(fm_chunks):
            wa, wb = wA[ci % 2], wB[ci % 2]
            if ci + 1 < nfm:
                nx = fm_chunks[ci + 1]
                load_w(wA[(ci + 1) % 2], nx["segA"])
                if nx["segB"]:
                    load_w(wB[(ci + 1) % 2], nx["segB"])
            nco = ch["ncols"]
            for sub in range(TG // 512):
                tok0 = g0 + sub * 512
                sl = slice(sub * 512, (sub + 1) * 512)
                pa = pA[cnt["a"] % 2]
                pb = pB[cnt["a"] % 2]
                cnt["a"] += 1
                for k in range(16):
                    mm(kb, pa[0:nco, :], wa[:, k, 0:nco], hT[:, k, sl], k == 0, k == 15,
                       [wa, hT], [pa])
                if ch["segB"]:
                    for k in range(16):
                        mm(kb, pb[0:nco, :], wb[:, k, 0:nco], hT[:, k, sl], k == 0, k == 15,
                           [wb, hT], [pb])
                kind = ch["kind"]
                if kind == "copy32":
                    o = o32[cnt["o32"] % 2]
                    cnt["o32"] += 1
                    act(kb, o[0:nco, :], pa[0:nco, :], AF.Copy, [pa], [o])
                else:
                    o = ob[cnt["o"] % 3]
                    cnt["o"] += 1
                    if kind == "rope":
                        cst = csA if ch["cs"] == 128 else csB
                        a1, a2 = t1[cnt["t"] % 2], t2[cnt["t"] % 2]
                        cnt["t"] += 1
                        sc = float(ch["scale"])
                        kb.op("dve", lambda e: e.scalar_tensor_tensor(
                            out=a1[0:nco, :], in0=pa[0:nco, :], scalar=sc, in1=cst[0:nco, 0, sl],
                            op0=ALU.mult, op1=ALU.mult), reads=[pa, cst], writes=[a1])
                        kb.op("dve", lambda e: e.scalar_tensor_tensor(
                            out=a2[0:nco, :], in0=pb[0:nco, :], scalar=sc, in1=cst[0:nco, 1, sl],
                            op0=ALU.mult, op1=ALU.mult), reads=[pb, cst], writes=[a2])
                        tt(kb, "pool", o[0:nco, :], a1[0:nco, :], a2[0:nco, :], ALU.add,
                           [a1, a2], [o])
                    elif kind == "copy":
                        act(kb, o[0:nco, :], pa[0:nco, :], AF.Copy, [pa], [o],
                            scale=float(ch["scale"]))
                    elif kind == "silu":
                        act(kb, o[0:nco, :], pa[0:nco, :], AF.Silu, [pa], [o])
                dbuf, dap = ch["dest"](tok0)
                kb.dma("sp", dap, o[0:nco, :], o, reads=[o], writes=[dbuf])
        ntm = len(tm_pieces)
        if ntm:
            load_w(wT[0], tm_pieces[0]["seg"])
        for pi, pc in enumerate(tm_pieces):
            wt = wT[pi % 2]
            if pi + 1 < ntm:
                load_w(wT[(pi + 1) % 2], tm_pieces[pi + 1]["seg"])
            nco = pc["ncols"]
            for i in range(NTT):
                r0 = g0 + i * 128
                pt_ = pT[cnt["tm"] % 2]
                cnt["tm"] += 1
                for k in range(16):
                    mm(kb, pt_[:, 0:nco], hT[:, k, i * 128:(i + 1) * 128], wt[:, k, 0:nco],
                       k == 0, k == 15, [hT, wt], [pt_])
                for (off, n, dfn, dt_) in pc["dests"]:
                    if dt_ == "f32":
                        o = o16[cnt["otm"] % 2]
                    else:
                        o = otm[cnt["otm"] % 2]
                    cnt["otm"] += 1
                    act(kb, o[:, 0:n], pt_[:, off:off + n], AF.Copy, [pt_], [o])
                    dbuf, dap = dfn(r0)
                    kb.dma("sp", dap, o[:, 0:n], o, reads=[o], writes=[dbuf])
    kb.end_phase()


def attn_epilogue(kb, OT, DEN, gt, rden, ob, ygT, row0, t0, ncol=512):
    if DEN is not None:
        kb.op("dve", lambda e: e.reciprocal(out=rden[:, 0:ncol], in_=DEN[:, 0:ncol]),
              reads=[DEN], writes=[rden])
        tt(kb, "pool", rden[:, 0:ncol], rden[:, 0:ncol], gt[:, 0:ncol], ALU.mult,
           [rden, gt], [rden])
        tt(kb, "dve", ob[:, 0:ncol], OT[:, 0:ncol], rden[:, 0:ncol], ALU.mult, [OT, rden], [ob])
    else:
        tt(kb, "dve", ob[:, 0:ncol], OT[:, 0:ncol], gt[:, 0:ncol], ALU.mult, [OT, gt], [ob])
    kb.dma("sp", ygT.ap[row0:row0 + 128, t0:t0 + ncol], ob[:, 0:ncol], ob, reads=[ob],
           writes=[ygT])


def phase_dsa(kb, cb, qaT, kaT, va, iqT, ikT, iw, gT, ygT):
    kb.begin_phase()
    KT = kb.sbuf("KT", [128, S], BF16)
    VA = kb.sbuf("VA", [128, NB, 128], BF16)
    IK = kb.sbuf("IK", [128, S], BF16)
    iqg = kb.sbuf("iqg", [128, 8, 512], BF16)
    iwg = kb.sbuf("iwg", [128, 4, 16], F32)
    score = [kb.sbuf(f"score{i}", [128, S], F32) for i in range(2)]
    work = kb.sbuf("work", [128, S], F32)
    rl = [kb.sbuf(f"rl{i}", [128, 512], F32) for i in range(2)]
    m8 = [kb.sbuf(f"m8_{i}", [128, 8], F32) for i in range(2)]
    mbf = [kb.sbuf(f"mbf{i}", [128, S], BF16) for i in range(2)]
    maskT = kb.sbuf("maskT", [128, NB, 512], BF16)
    QT = [kb.sbuf(f"QT{i}", [128, 512], BF16) for i in range(2)]
    PT = [kb.sbuf(f"PT{i}", [128, 512], BF16) for i in range(3)]
    PM = [kb.sbuf(f"PM{i}", [128, 512], BF16) for i in range(3)]
    gt = [kb.sbuf(f"gt{i}", [128, 512], BF16) for i in range(2)]
    rden = kb.sbuf("rden", [128, 512], F32)
    ob = [kb.sbuf(f"obd{i}", [128, 512], BF16) for i in range(2)]
    pd = [kb.psum(f"pd{i}", [128, 512], F32) for i in range(2)]
    ptp = kb.psum("ptp", [128, 8, 128], BF16)
    pS = [kb.psum(f"pS{i}", [128, 512], F32) for i in range(2)]
    OT = kb.psum("OT", [128, 512], F32)
    DEN = kb.psum("DEN", [128, 512], F32)
    ident = cbs(cb, C.IDENT)
    ones = cbs(cb, C.ONES)

    kb.dma("sp", KT[:], kaT.ap[:, :], KT, reads=[kaT], writes=[KT])
    kb.dma("sp", IK[:], ikT.ap[:, :], IK, reads=[ikT], writes=[IK])
    vsrc = va.ap.rearrange("(b p) c -> p b c", p=128)
    for b0 in range(0, NB, 8):
        kb.dma("sp", VA[:, b0:b0 + 8, :], vsrc[:, b0:b0 + 8, :], VA, reads=[va], writes=[VA])

    cnt = {"pd": 0, "s": 0, "q": 0, "ob": 0}
    for g in range(8):
        t0 = g * 512
        kb.dma("sp", iqg[:], iqT.ap[:, :, t0:t0 + 512].rearrange("c p t -> p c t"), iqg,
               reads=[iqT], writes=[iqg])
        kb.dma("sp", iwg[:], iw.ap[t0:t0 + 512, :].rearrange("(j p) h -> p j h", p=128), iwg,
               reads=[iw], writes=[iwg])
        for j in range(4):
            qb = 4 * g + j
            n = (qb + 1) * 128
            sc = score[j % 2]
            mb = mbf[j % 2]
            nch = (n + 511) // 512
            for chn in range(nch):
                w = min(512, n - chn * 512)
                cs = slice(chn * 512, chn * 512 + w)
                for h in range(16):
                    c, half = h // 2, h % 2
                    ps_ = slice(half * 64, (half + 1) * 64)
                    p = pd[cnt["pd"] % 2]
                    r = rl[cnt["pd"] % 2]
                    cnt["pd"] += 1
                    mm(kb, p[:, 0:w], iqg[ps_, c, j * 128:(j + 1) * 128], IK[ps_, cs], True, True,
                       [iqg, IK], [p])
                    act(kb, r[:, 0:w], p[:, 0:w], AF.Relu, [p], [r])
                    if h == 0:
                        kb.op("dve", lambda e: e.tensor_scalar(
                            out=sc[:, cs], in0=r[:, 0:w], scalar1=iwg[:, j, 0:1], scalar2=None,
                            op0=ALU.mult), reads=[r, iwg], writes=[sc])
                    else:
                        kb.op("dve", lambda e: e.scalar_tensor_tensor(
                            out=sc[:, cs], in0=r[:, 0:w], scalar=iwg[:, j, h:h + 1],
                            in1=sc[:, cs], op0=ALU.mult, op1=ALU.add), reads=[r, iwg, sc],
                            writes=[sc])
            dsl = slice(qb * 128, (qb + 1) * 128)
            kb.op("pool", lambda e: e.affine_select(
                out=sc[:, dsl], in_=sc[:, dsl], pattern=[[-1, 128]], compare_op=ALU.is_ge,
                fill=NEG, base=0, channel_multiplier=1), reads=[sc], writes=[sc])
            if qb >= 2:
                for rnd in range(32):
                    m = m8[rnd % 2]
                    src = sc if rnd == 0 else work
                    kb.op("dve", lambda e: e.max(out=m[:, 0:8], in_=src[:, 0:n]), reads=[src],
                          writes=[m])
                    if rnd < 31:
                        kb.op("dve", lambda e: e.match_replace(
                            out=work[:, 0:n], in_to_replace=m[:, 0:8], in_values=src[:, 0:n],
                            imm_value=NEG), reads=[src, m], writes=[work])
                mlast = m8[31 % 2]
                kb.op("dve", lambda e: e.tensor_scalar(
                    out=mb[:, 0:n], in0=sc[:, 0:n], scalar1=mlast[:, 7:8], scalar2=None,
                    op0=ALU.is_ge), reads=[sc, mlast], writes=[mb])
            else:
                kb.op("dve", lambda e: e.tensor_scalar(
                    out=mb[:, 0:n], in0=sc[:, 0:n], scalar1=-1.0e30, scalar2=None,
                    op0=ALU.is_ge), reads=[sc], writes=[mb])
            for k0 in range(0, qb + 1, 8):
                nk = min(8, qb + 1 - k0)
                for kk in range(nk):
                    kb.op("pe", lambda e: e.transpose(
                        out=ptp[:, kk, :], in_=mb[:, (k0 + kk) * 128:(k0 + kk + 1) * 128],
                        identity=ident), reads=[mb, cb], writes=[ptp])
                act(kb, maskT[:, k0:k0 + nk, j * 128:(j + 1) * 128], ptp[:, 0:nk, :], AF.Copy,
                    [ptp], [maskT])
        nkb = 4 * (g + 1)
        for h in range(10):
            q = QT[cnt["q"] % 2]
            gg = gt[cnt["q"] % 2]
            cnt["q"] += 1
            kb.dma("sp", q[:], qaT.ap[h, :, t0:t0 + 512], q, reads=[qaT], writes=[q])
            kb.dma("sp", gg[:], gT.ap[h * 128:(h + 1) * 128, t0:t0 + 512], gg, reads=[gT],
                   writes=[gg])

            def qk(kbi):
                c0 = max(0, kbi - 4 * g) * 128
                p = pS[kbi % 2]
                mm(kb, p[:, c0:512], KT[:, kbi * 128:(kbi + 1) * 128], q[:, c0:512], True, True,
                   [KT, q], [p])

            qk(0)
            for kbi in range(nkb):
                if kbi + 1 < nkb:
                    qk(kbi + 1)
                c0 = max(0, kbi - 4 * g) * 128
                p = pS[kbi % 2]
                pt_ = PT[kbi % 3]
                pm = PM[kbi % 3]
                act(kb, pt_[:, c0:512], p[:, c0:512], AF.Exp, [p], [pt_])
                tt(kb, "pool", pm[:, c0:512], pt_[:, c0:512], maskT[:, kbi, c0:512], ALU.mult,
                   [pt_, maskT], [pm])
                mm(kb, OT[:, c0:512], VA[:, kbi, :], pm[:, c0:512], kbi == 0, kbi == nkb - 1,
                   [VA, pm], [OT])
                mm(kb, DEN[:, c0:512], ones, pm[:, c0:512], kbi == 0, kbi == nkb - 1,
                   [cb, pm], [DEN])
            o = ob[cnt["ob"] % 2]
            cnt["ob"] += 1
            attn_epilogue(kb, OT, DEN, gg, rden, o, ygT, h * 128, t0)
    kb.end_phase()


def phase_dilated(kb, cb, qbT, kbT, vb, gT, ygT):
    kb.begin_phase()
    QT = [kb.sbuf(f"dQ{i}", [128, S], BF16) for i in range(2)]
    KT = [kb.sbuf(f"dK{i}", [128, S], BF16) for i in range(2)]
    VC = [kb.sbuf(f"dV{i}", [128, NB, 128], BF16) for i in range(2)]
    acc = kb.sbuf("acc", [128, S], F32)
    dacc = kb.sbuf("dacc", [128, S], F32)
    gt = kb.sbuf("dgt", [128, S], BF16)
    ob = kb.sbuf("dob", [128, S], BF16)
    PT = [kb.sbuf(f"dPT{i}", [128, 2, 128], BF16) for i in range(3)]
    PM = [kb.sbuf(f"dPM{i}", [128, 2, 128], BF16) for i in range(3)]
    pS = [kb.psum(f"dpS{i}", [128, 512], F32) for i in range(2)]
    pO = [kb.psum(f"dpO{i}", [128, 512], F32) for i in range(2)]
    pD = [kb.psum(f"dpD{i}", [128, 512], F32) for i in range(2)]
    ones = cbs(cb, C.ONES)
    band = cb[:, C.GE * 128:(C.GE + 2) * 128].rearrange("p (a c) -> p a c", a=2)
    u = 0
    vi = 0
    for h in range(6):
        q, k = QT[h % 2], KT[h % 2]
        kb.dma("sp", q[:], qbT.ap[h, :, :], q, reads=[qbT], writes=[q])
        kb.dma("sp", k[:], kbT.ap[h, :, :], k, reads=[kbT], writes=[k])
        kb.dma("sp", gt[:], gT.ap[1280 + h * 128:1280 + (h + 1) * 128, :], gt, reads=[gT],
               writes=[gt])
        for pi, dil in enumerate((1, 4, 16)):
            nb = NB // dil
            v = VC[vi % 2]
            vi += 1
            vsrc = vb.ap[:, h * 128:(h + 1) * 128].rearrange("(n i d) c -> i d n c", d=dil, i=128)
            vdst = v[:].rearrange("p (d n) c -> p d n c", d=dil)
            for r in range(dil):
                for n0 in range(0, nb, 8):
                    n1 = min(nb, n0 + 8)
                    kb.dma("sp", vdst[:, r, n0:n1, :], vsrc[:, r, n0:n1, :], v, reads=[vb],
                           writes=[v])
            for r in range(dil):
                for n in range(nb):
                    def cols(m):
                        s0 = m * 128 * dil + r
                        return slice(s0, s0 + 127 * dil + 1, dil)
                    st = pS[u % 2]
                    po = pO[u % 2]
                    pdn = pD[u % 2]
                    pt_ = PT[u % 3]
                    pm = PM[u % 3]
                    u += 1
                    s0 = 0 if n > 0 else 1
                    stv = st[:, 0:256].rearrange("p (a c) -> p a c", a=2)
                    for sl_ in range(s0, 2):
                        m = n - 1 + sl_
                        mm(kb, stv[:, sl_, :], k[:, cols(m)], q[:, cols(n)], True, True, [k, q], [st])
                    act(kb, pt_[:, s0:2, :], stv[:, s0:2, :], AF.Exp, [st], [pt_])
                    tt(kb, "pool", pm[:, s0:2, :], pt_[:, s0:2, :], band[:, s0:2, :], ALU.mult,
                       [pt_, cb], [pm])
                    for sl_ in range(s0, 2):
                        m = n - 1 + sl_
                        mm(kb, po[:, 0:128], v[:, r * nb + m, :], pm[:, sl_, :], sl_ == s0, sl_ == 1,
                           [v, pm], [po])
                    for sl_ in range(s0, 2):
                        mm(kb, pdn[:, 0:128], ones, pm[:, sl_, :], sl_ == s0, sl_ == 1, [cb, pm],
                           [pdn])
                    cs = cols(n)
                    if pi == 0:
                        kb.op("dve", lambda e: e.tensor_copy(out=acc[:, cs], in_=po[:, 0:128]),
                              reads=[po], writes=[acc])
                        kb.op("dve", lambda e: e.tensor_copy(out=dacc[:, cs], in_=pdn[:, 0:128]),
                              reads=[pdn], writes=[dacc])
                    else:
                        tt(kb, "dve", acc[:, cs], po[:, 0:128], acc[:, cs], ALU.add, [po, acc], [acc])
                        tt(kb, "dve", dacc[:, cs], pdn[:, 0:128], dacc[:, cs], ALU.add,
                           [pdn, dacc], [dacc])
        kb.op("dve", lambda e: e.reciprocal(out=dacc[:], in_=dacc[:]), reads=[dacc], writes=[dacc])
        tt(kb, "dve", acc[:], acc[:], dacc[:], ALU.mult, [acc, dacc], [acc])
        tt(kb, "dve", ob[:], acc[:], gt[:], ALU.mult, [acc, gt], [ob])
        kb.dma("sp", ygT.ap[1280 + h * 128:1280 + (h + 1) * 128, :], ob[:], ob, reads=[ob],
               writes=[ygT])
    kb.end_phase()


def phase_sb(kb, cb, qcT, kcT, vc, gT, ygT):
    kb.begin_phase()
    KT = [kb.sbuf(f"sK{i}", [128, S], BF16) for i in range(2)]
    VV = [kb.sbuf(f"sV{i}", [128, NB, 128], BF16) for i in range(2)]
    QT = [kb.sbuf(f"sQ{i}", [128, 512], BF16) for i in range(2)]
    gt = [kb.sbuf(f"sg{i}", [128, 512], BF16) for i in range(2)]
    E = [kb.sbuf(f"sE{i}", [128, 512], F32) for i in range(2)]
    LP = [kb.sbuf(f"sL{i}", [128, 512], BF16) for i in range(3)]
    RS = kb.sbuf("sRS", [128, 512], BF16)
    PT = [kb.sbuf(f"sP{i}", [128, 512], BF16) for i in range(3)]
    ob = [kb.sbuf(f"sob{i}", [128, 512], BF16) for i in range(2)]
    pZ = [kb.psum(f"pZ{i}", [128, 512], F32) for i in range(2)]
    pW = [kb.psum(f"pW{i}", [128, 512], F32) for i in range(2)]
    OT = [kb.psum(f"sOT{i}", [128, 512], F32) for i in range(2)]
    negU = cbs(cb, C.NEGU)
    negones = cbs(cb, C.NEGONES)
    lt = cbs(cb, C.LT)
    u = 0
    for h in range(8):
        k, v = KT[h % 2], VV[h % 2]
        kb.dma("sp", k[:], kcT.ap[h, :, :], k, reads=[kcT], writes=[k])
        vsrc = vc.ap[:, h * 128:(h + 1) * 128].rearrange("(b p) c -> p b c", p=128)
        for b0 in range(0, NB, 8):
            kb.dma("sp", v[:, b0:b0 + 8, :], vsrc[:, b0:b0 + 8, :], v, reads=[vc], writes=[v])
        for g in range(8):
            t0 = g * 512
            q = QT[u % 2]
            gg = gt[u % 2]
            ot = OT[u % 2]
            o = ob[u % 2]
            u += 1
            kb.dma("sp", q[:], qcT.ap[h, :, t0:t0 + 512], q, reads=[qcT], writes=[q])
            kb.dma("sp", gg[:], gT.ap[h * 128:(h + 1) * 128, t0:t0 + 512], gg, reads=[gT],
                   writes=[gg])
            nkb = 4 * (g + 1)
            order = list(range(nkb - 1, -1, -1))

            def zmm(kbi, idx):
                c0 = max(0, kbi - 4 * g) * 128
                mm(kb, pZ[idx % 2][:, c0:512], k[:, kbi * 128:(kbi + 1) * 128], q[:, c0:512],
                   True, True, [k, q], [pZ[idx % 2]])

            zmm(order[0], 0)
            for idx, kbi in enumerate(order):
                if idx + 1 < nkb:
                    zmm(order[idx + 1], idx + 1)
                c0 = max(0, kbi - 4 * g) * 128
                z = pZ[idx % 2]
                w = pW[idx % 2]
                e_ = E[idx % 2]
                lp = LP[idx % 3]
                pt_ = PT[idx % 3]
                diag = kbi >= 4 * g
                act(kb, e_[:, c0:512], z[:, c0:512], AF.Exp, [z], [e_])
                act(kb, lp[:, c0:512], e_[:, c0:512], AF.Ln, [e_], [lp], bias=1.0)
                if diag:
                    tt(kb, "pool", lp[:, c0:c0 + 128], lp[:, c0:c0 + 128], lt, ALU.mult,
                       [lp, cb], [lp])
                mm(kb, w[:, c0:512], k[:, kbi * 128:(kbi + 1) * 128], q[:, c0:512], True, False,
                   [k, q], [w])
                mm(kb, w[:, c0:512], negU, lp[:, c0:512], False, idx == 0, [cb, lp], [w])
                if idx > 0:
                    mm(kb, w[:, c0:512], negones, RS[:, c0:512], False, True, [cb, RS], [w])
                act(kb, pt_[:, c0:512], w[:, c0:512], AF.Exp, [w], [pt_])
                if diag:
                    tt(kb, "pool", pt_[:, c0:c0 + 128], pt_[:, c0:c0 + 128], lt, ALU.mult,
                       [pt_, cb], [pt_])
                if idx == 0:
                    kb.op("dve", lambda e: e.memset(RS[:], 0.0), writes=[RS])
                if idx + 1 < nkb:
                    tt(kb, "dve", RS[:, c0:512], RS[:, c0:512], lp[:, c0:512], ALU.add, [RS, lp],
                       [RS])
                mm(kb, ot[:, c0:512], v[:, kbi, :], pt_[:, c0:512], idx == 0, idx == nkb - 1,
                   [v, pt_], [ot])
            attn_epilogue(kb, ot, None, gg, None, o, ygT, h * 128, t0)
    kb.end_phase()


def phase_fox_prep(kb, flT, b_f, fA, fB):
    kb.begin_phase()
    fl = kb.sbuf("fl", [8, S], F32)
    bf = kb.sbuf("bf", [8, 2], F32)
    e_ = kb.sbuf("fe", [8, S], F32)
    ones = kb.sbuf("fones", [8, S], F32)
    cc = kb.sbuf("fcc", [8, S], F32)
    rr = kb.sbuf("frr", [8, S], F32)
    A = kb.sbuf("fAs", [8, 6, S], BF16)
    B = kb.sbuf("fBs", [8, 6, S], BF16)
    kb.dma("sp", fl[:], flT.ap[:, :], fl, reads=[flT], writes=[fl])
    kb.dma("sp", bf[:, 0:1], b_f.ap.rearrange("(h o) -> h o", o=1), bf, reads=[b_f], writes=[bf])
    kb.op("dve", lambda e: e.tensor_scalar(out=bf[:, 1:2], in0=bf[:, 0:1], scalar1=-1.0,
                                           scalar2=None, op0=ALU.mult), reads=[bf], writes=[bf])
    act(kb, e_[:], fl[:], AF.Exp, [fl, bf], [e_], scale=-1.0, bias=bf[:, 1:2])
    act(kb, e_[:], e_[:], AF.Ln, [e_], [e_], bias=1.0)
    kb.op("dve", lambda e: e.memset(ones[:], 1.0), writes=[ones])
    kb.op("dve", lambda e: e.tensor_tensor_scan(out=cc[:], data0=ones[:], data1=e_[:], initial=0.0,
                                                op0=ALU.mult, op1=ALU.add), reads=[ones, e_],
          writes=[cc])
    cur = cc
    for i in range(3):
        kb.op("dve", lambda e: e.tensor_copy(out=A[:, i, :], in_=cur[:]), reads=[cur], writes=[A])
        kb.op("dve", lambda e: e.tensor_scalar(out=B[:, 3 + i, :], in0=A[:, i, :], scalar1=-1.0,
                                               scalar2=None, op0=ALU.mult), reads=[A], writes=[B])
        if i < 2:
            nxt = rr if cur is cc else cc
            tt(kb, "dve", nxt[:], cur[:], A[:, i, :], ALU.subtract, [cur, A], [nxt])
            cur = nxt
    kb.op("dve", lambda e: e.memset(A[:, 3:6, :], 1.0), writes=[A])
    kb.op("dve", lambda e: e.memset(B[:, 0:3, :], 1.0), writes=[B])
    kb.dma("sp", fA.ap[:, :, :], A[:], A, reads=[A], writes=[fA])
    kb.dma("sp", fB.ap[:, :, :], B[:], B, reads=[B], writes=[fB])
    kb.end_phase()


def phase_fox(kb, cb, qdT, kdT, vd, fA, fB, gT, ygT):
    kb.begin_phase()
    KT = [kb.sbuf(f"fK{i}", [128, S], BF16) for i in range(2)]
    VV = [kb.sbuf(f"fV{i}", [128, NB, 128], BF16) for i in range(2)]
    AA = [kb.sbuf(f"fA{i}", [6, S], BF16) for i in range(2)]
    BB = [kb.sbuf(f"fB{i}", [6, S], BF16) for i in range(2)]
    QT = [kb.sbuf(f"fQ{i}", [128, 512], BF16) for i in range(2)]
    gt = [kb.sbuf(f"fg{i}", [128, 512], BF16) for i in range(2)]
    PT = [kb.sbuf(f"fP{i}", [128, 512], BF16) for i in range(3)]
    rden = kb.sbuf("frden", [128, 512], F32)
    ob = [kb.sbuf(f"fob{i}", [128, 512], BF16) for i in range(2)]
    pS = [kb.psum(f"fpS{i}", [128, 512], F32) for i in range(3)]
    OT = [kb.psum(f"fOT{i}", [128, 512], F32) for i in range(2)]
    DEN = [kb.psum(f"fDEN{i}", [128, 512], F32) for i in range(2)]
    ones = cbs(cb, C.ONES)
    le = cbs(cb, C.LE)
    u = 0
    for h in range(8):
        k, v, A, B = KT[h % 2], VV[h % 2], AA[h % 2], BB[h % 2]
        kb.dma("sp", k[:], kdT.ap[h, :, :], k, reads=[kdT], writes=[k])
        kb.dma("sp", A[:], fA.ap[h, :, :], A, reads=[fA], writes=[A])
        kb.dma("sp", B[:], fB.ap[h, :, :], B, reads=[fB], writes=[B])
        vsrc = vd.ap[:, h * 128:(h + 1) * 128].rearrange("(b p) c -> p b c", p=128)
        for b0 in range(0, NB, 8):
            kb.dma("sp", v[:, b0:b0 + 8, :], vsrc[:, b0:b0 + 8, :], v, reads=[vd], writes=[v])
        for g in range(8):
            t0 = g * 512
            q = QT[u % 2]
            gg = gt[u % 2]
            ot = OT[u % 2]
            dn = DEN[u % 2]
            o = ob[u % 2]
            u += 1
            kb.dma("sp", q[:], qdT.ap[h, :, t0:t0 + 512], q, reads=[qdT], writes=[q])
            kb.dma("sp", gg[:], gT.ap[1024 + h * 128:1024 + (h + 1) * 128, t0:t0 + 512], gg,
                   reads=[gT], writes=[gg])
            nkb = 4 * (g + 1)

            def qk(kbi):
                c0 = max(0, kbi - 4 * g) * 128
                p = pS[kbi % 3]
                mm(kb, p[:, c0:512], k[:, kbi * 128:(kbi + 1) * 128], q[:, c0:512], True, False,
                   [k, q], [p])
                mm(kb, p[:, c0:512], A[:, kbi * 128:(kbi + 1) * 128], B[:, t0 + c0:t0 + 512],
                   False, True, [A, B], [p])

            qk(0)
            for kbi in range(nkb):
                if kbi + 1 < nkb:
                    qk(kbi + 1)
                c0 = max(0, kbi - 4 * g) * 128
                p = pS[kbi % 3]
                pt_ = PT[kbi % 3]
                act(kb, pt_[:, c0:512], p[:, c0:512], AF.Exp, [p], [pt_])
                if kbi >= 4 * g:
                    tt(kb, "pool", pt_[:, c0:c0 + 128], pt_[:, c0:c0 + 128], le, ALU.mult,
                       [pt_, cb], [pt_])
                mm(kb, ot[:, c0:512], v[:, kbi, :], pt_[:, c0:512], kbi == 0, kbi == nkb - 1,
                   [v, pt_], [ot])
                mm(kb, dn[:, c0:512], ones, pt_[:, c0:512], kbi == 0, kbi == nkb - 1,
                   [cb, pt_], [dn])
            attn_epilogue(kb, ot, dn, gg, rden, o, ygT, 1024 + h * 128, t0)
    kb.end_phase()


def phase_outproj(kb, ygT, w_out, xres, xdst, norm_f):
    kb.begin_phase()
    W = kb.sbuf("oW", [128, 16, D], BF16)
    yg = [kb.sbuf(f"oyg{i}", [128, 16, 512], BF16) for i in range(2)]
    xr = [kb.sbuf(f"oxr{i}", [128, D], F32) for i in range(2)]
    xw = [kb.sbuf(f"oxw{i}", [128, D], F32) for i in range(2)]
    pp = [kb.psum(f"opp{i}", [128, 512], F32) for i in range(4)]
    if norm_f is not None:
        nf = kb.sbuf("onf", [128, D], F32)
        sq = kb.sbuf("osq", [128, D], BF16)
        stt = [kb.sbuf(f"ost{i}", [128, 4], F32) for i in range(2)]
        kb.dma("sp", nf[:], norm_f.ap.partition_broadcast(128), nf, reads=[norm_f], writes=[nf])
    wsrc = w_out.ap.rearrange("(k p) n -> p k n", p=128)
    for k0 in range(0, 16, 4):
        kb.dma("pool", W[:, k0:k0 + 4, :], wsrc[:, k0:k0 + 4, :], W, reads=[w_out], writes=[W])
    ysrc = ygT.ap.rearrange("(k p) t -> p k t", p=128)
    ti = 0
    for g in range(8):
        y = yg[g % 2]
        for k0 in range(0, 16, 8):
            kb.dma("sp", y[:, k0:k0 + 8, :], ysrc[:, k0:k0 + 8, g * 512:(g + 1) * 512], y,
                   reads=[ygT], writes=[y])
        for j in range(4):
            r0 = g * 512 + j * 128
            xi, xo = xr[ti % 2], xw[ti % 2]
            kb.dma("sp", xi[:], xres.ap[r0:r0 + 128, :], xi, reads=[xres], writes=[xi])
            for n in range(4):
                p = pp[n]
                for k in range(16):
                    mm(kb, p[:, :], y[:, k, j * 128:(j + 1) * 128], W[:, k, n * 512:(n + 1) * 512],
                       k == 0, k == 15, [y, W], [p])
                tt(kb, "dve", xo[:, n * 512:(n + 1) * 512], p[:, :], xi[:, n * 512:(n + 1) * 512],
                   ALU.add, [p, xi], [xo])
            if norm_f is None:
                kb.dma("sp", xdst.ap[r0:r0 + 128, :], xo[:], xo, reads=[xo], writes=[xdst])
            else:
                st_ = stt[ti % 2]
                act(kb, sq[:], xo[:], AF.Square, [xo], [sq, st_], accum_out=st_[:, 0:1])
                kb.op("dve", lambda e: e.tensor_scalar(out=st_[:, 1:2], in0=st_[:, 0:1],
                                                       scalar1=1.0 / D, scalar2=EPS, op0=ALU.mult,
                                                       op1=ALU.add), reads=[st_], writes=[st_])
                act(kb, st_[:, 2:3], st_[:, 1:2], AF.Ln, [st_], [st_])
                act(kb, st_[:, 3:4], st_[:, 2:3], AF.Exp, [st_], [st_], scale=-0.5)
                kb.op("dve", lambda e: e.scalar_tensor_tensor(
                    out=xi[:], in0=xo[:], scalar=st_[:, 3:4], in1=nf[:], op0=ALU.mult,
                    op1=ALU.mult), reads=[xo, st_, nf], writes=[xi])
                kb.dma("sp", xdst.ap[r0:r0 + 128, :], xi[:], xi, reads=[xi], writes=[xdst])
            ti += 1
    kb.end_phase()


def rope_segs(c0, n, half):
    segs = []
    for hs in range(0, n, 2 * half):
        segs.append((hs, c0 + hs + half, half))
        segs.append((hs + half, c0 + hs, half))
    return segs


def build(phases=None, debug=False):
    nc = bass.Bass("TRN2", target_bir_lowering=False)
    kb = KB(nc)
    P = (lambda p: True) if phases is None else (lambda p: p in phases)

    def ein(name, shape, dt=F32):
        return kb.dram(name, shape, dt, kind="ExternalInput")

    def scr(name, shape, dt):
        return kb.dram(name, shape, dt, kind="ExternalOutput")

    x = ein("x", [S, D])
    norm0 = ein("norm0", [D])
    w_in0 = ein("w_in0", [D, IN0])
    w_out0 = ein("w_out0", [D, D])
    norm1 = ein("norm1", [D])
    w_in1 = ein("w_in1", [D, IN1])
    b_f1 = ein("b_f1", [8])
    w_out1 = ein("w_out1", [D, D])
    norm_f = ein("norm_f", [D])
    cs128 = ein("cs128", [2, 128, S])
    cs64 = ein("cs64", [2, 128, S])
    cbd = ein("cb", [128, 7 * 128], BF16)
    out = kb.dram("out", [S, D], F32, kind="ExternalOutput")

    qaT = scr("qaT", [10, 128, S], BF16)
    kaT = scr("kaT", [128, S], BF16)
    va = scr("va", [S, 128], BF16)
    iqT = scr("iqT", [8, 128, S], BF16)
    ikT = scr("ikT", [128, S], BF16)
    iw = scr("iw", [S, 16], F32)
    qbT = scr("qbT", [6, 128, S], BF16)
    kbT = scr("kbT", [6, 128, S], BF16)
    vb = scr("vb", [S, 768], BF16)
    g0T = scr("g0T", [D, S], BF16)
    yg0T = scr("yg0T", [D, S], BF16)
    x1 = scr("x1", [S, D], F32)
    qcT = scr("qcT", [8, 128, S], BF16)
    kcT = scr("kcT", [8, 128, S], BF16)
    vc = scr("vc", [S, 1024], BF16)
    qdT = scr("qdT", [8, 128, S], BF16)
    kdT = scr("kdT", [8, 128, S], BF16)
    vd = scr("vd", [S, 1024], BF16)
    flT = scr("flT", [8, S], F32)
    fA = scr("fA", [8, 6, S], BF16)
    fB = scr("fB", [8, 6, S], BF16)
    g1T = scr("g1T", [D, S], BF16)
    yg1T = scr("yg1T", [D, S], BF16)

    cb = kb.sbuf("cb_sb", [128, 7 * 128], BF16, glob=True)
    kb.dma("sp", cb[:], cbd.ap[:, :], cb, reads=[cbd], writes=[cb])

    def fm(c0, n, kind, dest, scale=1.0, cs=None, dup=False):
        half = 64 if cs == 128 else 32
        if dup:
            segA = [(0, c0, n), (n, c0, n)]
            segB = [(d, s, m) for (d, s, m) in rope_segs(c0, n, half)] + \
                   [(d + n, s, m) for (d, s, m) in rope_segs(c0, n, half)]
            n = 2 * n
        else:
            segA = [(0, c0, n)]
            segB = rope_segs(c0, n, half) if kind == "rope" else None
        return dict(segA=segA, segB=segB, ncols=n, kind=kind, scale=scale, dest=dest, cs=cs)

    def dest3(buf, h):
        return lambda tok0: (buf, buf.ap[h, :, tok0:tok0 + 512])

    def dest2(buf, r0, n=128):
        return lambda tok0: (buf, buf.ap[r0:r0 + n, tok0:tok0 + 512])

    def tdest(buf, c0, n):
        return lambda r0: (buf, buf.ap[r0:r0 + 128, c0:c0 + n])

    if P("in0"):
        ch = []
        for h in range(10):
            ch.append(fm(128 * h, 128, "rope", dest3(qaT, h), SC128, 128))
        ch.append(fm(1280, 128, "rope", dest2(kaT, 0), 1.0, 128))
        for c in range(8):
            ch.append(fm(1536 + 128 * c, 128, "rope", dest3(iqT, c), 1.0, 64))
        ch.append(fm(2560, 64, "rope", dest2(ikT, 0), 1.0, 64, dup=True))
        for h in range(6):
            ch.append(fm(2640 + 128 * h, 128, "rope", dest3(qbT, h), SC128, 128))
        for h in range(6):
            ch.append(fm(3408 + 128 * h, 128, "rope", dest3(kbT, h), 1.0, 128))
        for c in range(16):
            ch.append(fm(4944 + 128 * c, 128, "silu", dest2(g0T, 128 * c)))
        tm = [
            dict(seg=[(0, 1408, 128), (128, 4176, 384)], ncols=512,
                 dests=[(0, 128, tdest(va, 0, 128), "bf"), (128, 384, tdest(vb, 0, 384), "bf")]),
            dict(seg=[(0, 4560, 384), (384, 2624, 16)], ncols=400,
                 dests=[(0, 384, tdest(vb, 384, 384), "bf"), (384, 16, tdest(iw, 0, 16), "f32")]),
        ]
        phase_inproj(kb, cb, x, norm0, w_in0, ch, tm, cs128, cs64)
    if P("dsa"):
        phase_dsa(kb, cb, qaT, kaT, va, iqT, ikT, iw, g0T, yg0T)
    if P("dil"):
        phase_dilated(kb, cb, qbT, kbT, vb, g0T, yg0T)
    if P("out0"):
        phase_outproj(kb, yg0T, w_out0, x, x1, None)
    if P("in1"):
        ch = []
        for h in range(8):
            ch.append(fm(128 * h, 128, "copy", dest3(qcT, h), 1.0))
        for h in range(8):
            ch.append(fm(1024 + 128 * h, 128, "copy", dest3(kcT, h), SC128))
        for h in range(8):
            ch.append(fm(3072 + 128 * h, 128, "copy", dest3(qdT, h), 1.0))
        for h in range(8):
            ch.append(fm(4096 + 128 * h, 128, "copy", dest3(kdT, h), SC128))
        ch.append(fm(6144, 8, "copy32", dest2(flT, 0, 8)))
        for c in range(16):
            ch.append(fm(6152 + 128 * c, 128, "silu", dest2(g1T, 128 * c)))
        tm = []
        for i in range(2):
            tm.append(dict(seg=[(0, 2048 + 512 * i, 512)], ncols=512,
                           dests=[(0, 512, tdest(vc, 512 * i, 512), "bf")]))
        for i in range(2):
            tm.append(dict(seg=[(0, 5120 + 512 * i, 512)], ncols=512,
                           dests=[(0, 512, tdest(vd, 512 * i, 512), "bf")]))
        phase_inproj(kb, cb, x1, norm1, w_in1, ch, tm, None, None)
    if P("sb"):
        phase_sb(kb, cb, qcT, kcT, vc, g1T, yg1T)
    if P("fox"):
        phase_fox_prep(kb, flT, b_f1, fA, fB)
        phase_fox(kb, cb, qdT, kdT, vd, fA, fB, g1T, yg1T)
    if P("out1"):
        phase_outproj(kb, yg1T, w_out1, x1, out, norm_f)
    kb.barrier()
    return nc, kb


def host_consts():
    pos = np.arange(S, dtype=np.float32)

    def tables(hd, rows):
        half = hd // 2
        inv = (np.float32(10000.0) ** (-np.arange(half, dtype=np.float32) / np.float32(half))).astype(np.float32)
        ang = pos[None, :] * inv[:, None]
        cos = np.cos(ang).astype(np.float32)
        sin = np.sin(ang).astype(np.float32)
        c = np.zeros((2, rows, S), np.float32)
        for p in range(rows):
            d = p % hd
            c[0, p] = cos[d % half]
            c[1, p] = -sin[d % half] if d < half else sin[d % half]
        return c

    cs128 = tables(128, 128)
    cs64 = tables(64, 128)
    i = np.arange(128)[:, None]
    j = np.arange(128)[None, :]
    blocks = [
        (i == j), (i < j), (i >= j), (i <= j), np.ones((128, 128), bool),
    ]
    cbv = [b.astype(np.float32) for b in blocks]
    cbv.append(-(i >= j).astype(np.float32))
    cbv.append(-np.ones((128, 128), np.float32))
    cb = np.concatenate(cbv, axis=1).astype(ml_dtypes.bfloat16)
    return cs128, cs64, cb


_CACHE = {}


def kernel(x, norm0, w_in0, w_out0, norm1, w_in1, b_f1, w_out1, norm_f):
    if "nc" not in _CACHE:
        _CACHE["nc"] = build()[0]
        _CACHE["consts"] = host_consts()
    nc = _CACHE["nc"]
    cs128, cs64, cb = _CACHE["consts"]
    f = lambda a: np.ascontiguousarray(np.asarray(a, dtype=np.float32))
    shared = dict(norm0=f(norm0), w_in0=f(w_in0), w_out0=f(w_out0), norm1=f(norm1),
                  w_in1=f(w_in1), b_f1=f(b_f1), w_out1=f(w_out1), norm_f=f(norm_f),
                  cs128=cs128, cs64=cs64, cb=cb)
    x = np.asarray(x, dtype=np.float32)
    in_maps = [dict(shared, x=np.ascontiguousarray(x[i])) for i in range(8)]
    res = run_bass_kernel_spmd(nc, in_maps, core_ids=list(range(8)))
    return np.stack([np.asarray(r["out"], dtype=np.float32) for r in res.results], axis=0)
```

```python
import contextlib
import math
import numpy as np
import ml_dtypes
import concourse.bass as bass
import concourse.mybir as mybir
from concourse.bass_utils import run_bass_kernel_spmd

F32 = mybir.dt.float32
BF16 = mybir.dt.bfloat16
AF = mybir.ActivationFunctionType
ALU = mybir.AluOpType

S = 4096
D = 2048
NB = 32
EPS = 1e-6
NEG = -3.0e38
SC128 = 128 ** -0.5
IN0 = 6992
IN1 = 8200

EPOCH = 20000


class DSem:
    def __init__(self, handle):
        self.h = handle
        self.cnt = 0


class Buf:
    def __init__(self, ap, name):
        self.ap = ap
        self.name = name
        self.w = {}
        self.r = {}
        self.ds = None

    def __getitem__(self, k):
        return self.ap[k]


class KB:
    def __init__(self, nc):
        self.nc = nc
        self.eng = {"pe": nc.tensor, "act": nc.scalar, "dve": nc.vector,
                    "pool": nc.gpsimd, "sp": nc.sync}
        self.n = {e: 0 for e in self.eng}
        self.esems = {}
        self.waited = {e: {} for e in self.eng}
        self.gstack = contextlib.ExitStack()
        self.pstack = None
        self.free_ds = []
        self.all_ds = []
        self.phase_bufs = []

    def begin_phase(self):
        self.pstack = contextlib.ExitStack()
        self.phase_bufs = []
        self.pid = getattr(self, "pid", 0) + 1

    def end_phase(self):
        self.barrier()
        for b in self.phase_bufs:
            if b.ds is not None:
                self.free_ds.append(b.ds)
                b.ds = None
        self.pstack.close()
        self.pstack = None

    def sbuf(self, name, shape, dtype, glob=False):
        st = self.gstack if glob else self.pstack
        name = name if glob else f"{name}_p{self.pid}"
        t = st.enter_context(self.nc.sbuf_tensor(name, list(shape), dtype))
        b = Buf(t, name)
        if not glob:
            self.phase_bufs.append(b)
        return b

    def psum(self, name, shape, dtype):
        t = self.pstack.enter_context(self.nc.psum_tensor(f"{name}_p{self.pid}", list(shape), dtype))
        b = Buf(t, name)
        self.phase_bufs.append(b)
        return b

    def dram(self, name, shape, dtype, kind="Internal"):
        t = self.nc.dram_tensor(name, list(shape), dtype, kind=kind).ap()
        return Buf(t, name)

    def _get_ds(self, buf):
        if buf.ds is None:
            if self.free_ds:
                buf.ds = self.free_ds.pop()
            else:
                h = self.gstack.enter_context(self.nc.semaphore(f"d{len(self.all_ds)}"))
                buf.ds = DSem(h)
                self.all_ds.append(buf.ds)
        return buf.ds

    def _esem(self, ename, epoch):
        key = (ename, epoch)
        if key not in self.esems:
            self.esems[key] = self.gstack.enter_context(
                self.nc.semaphore(f"e_{ename}_{epoch}"))
        return self.esems[key]

    def wait(self, ename, tok):
        if tok[0] == "e":
            _, src, idx = tok
            k = ("e", src)
            if self.waited[ename].get(k, 0) >= idx:
                return
            epoch = (idx - 1) // EPOCH
            self.eng[ename].wait_ge(self._esem(src, epoch), idx - epoch * EPOCH)
            self.waited[ename][k] = idx
        else:
            _, ds, val = tok
            k = ("d", id(ds))
            if self.waited[ename].get(k, 0) >= val:
                return
            self.eng[ename].wait_ge(ds.h, val)
            self.waited[ename][k] = val

    def _deps(self, ename, reads, writes, is_dma):
        for b in reads:
            for tok in b.w.values():
                if (not is_dma) and tok[0] == "e" and tok[1] == ename and ename == "pe":
                    continue
                self.wait(ename, tok)
        for b in writes:
            for tok in list(b.w.values()) + list(b.r.values()):
                if (not is_dma) and tok[0] == "e" and tok[1] == ename:
                    continue
                self.wait(ename, tok)

    @staticmethod
    def _key(tok):
        return ("e", tok[1]) if tok[0] == "e" else ("d", id(tok[1]))

    def op(self, ename, fn, reads=(), writes=()):
        self._deps(ename, reads, writes, False)
        ins = fn(self.eng[ename])
        self.n[ename] += 1
        idx = self.n[ename]
        epoch = (idx - 1) // EPOCH
        ins.then_inc(self._esem(ename, epoch), 1)
        tok = ("e", ename, idx)
        k = self._key(tok)
        for b in writes:
            b.w[k] = tok
        for b in reads:
            b.r[k] = tok
        return tok

    def dma(self, qname, out_ap, in_ap, sb, reads=(), writes=(), **kw):
        self._deps(qname, reads, writes, True)
        ds = self._get_ds(sb)
        ins = self.eng[qname].dma_start(out=out_ap, in_=in_ap, **kw)
        ds.cnt += 16
        ins.then_inc(ds.h, 16)
        tok = ("d", ds, ds.cnt)
        k = self._key(tok)
        for b in writes:
            b.w[k] = tok
        for b in reads:
            b.r[k] = tok
        return tok

    def barrier(self):
        toks = [("e", e, self.n[e]) for e in self.eng if self.n[e] > 0]
        toks += [("d", ds, ds.cnt) for ds in self.all_ds if ds.cnt > 0]
        for e in self.eng:
            for t in toks:
                if t[0] == "e" and t[1] == e:
                    continue
                self.wait(e, t)


def mm(kb, out, lhsT, rhs, start, stop, reads, writes):
    return kb.op("pe", lambda e: e.matmul(out, lhsT=lhsT, rhs=rhs, start=start, stop=stop),
                 reads=reads, writes=writes)


def act(kb, out, in_, func, reads, writes, **kw):
    return kb.op("act", lambda e: e.activation(out=out, in_=in_, func=func, **kw),
                 reads=reads, writes=writes)


def tt(kb, eng, out, in0, in1, op, reads, writes):
    return kb.op(eng, lambda e: e.tensor_tensor(out=out, in0=in0, in1=in1, op=op),
                 reads=reads, writes=writes)


class C:
    IDENT, LT, GE, LE, ONES, NEGU, NEGONES = range(7)


def cbs(cb, i, n=1):
    return cb[:, i * 128:(i + n) * 128]


def phase_inproj(kb, cb, xsrc, norm, w_in, fm_chunks, tm_pieces, cs128, cs64):
    kb.begin_phase()
    TG = 2048
    NTT = TG // 128
    hT = kb.sbuf("hT", [128, 16, TG], BF16)
    gsb = kb.sbuf("gsb", [128, 16], F32)
    xt = [kb.sbuf(f"xt{i}", [128, D], F32) for i in range(2)]
    xn = [kb.sbuf(f"xn{i}", [128, D], BF16) for i in range(2)]
    sq = kb.sbuf("sq", [128, D], BF16)
    stt = [kb.sbuf(f"stt{i}", [128, 4], F32) for i in range(2)]
    tp = [kb.psum(f"tp{i}", [128, 8, 128], BF16) for i in range(2)]
    pA = [kb.psum(f"pA{i}", [128, 512], F32) for i in range(2)]
    pB = [kb.psum(f"pB{i}", [128, 512], F32) for i in range(2)]
    pT = [kb.psum(f"pT{i}", [128, 512], F32) for i in range(2)]
    wA = [kb.sbuf(f"wA{i}", [128, 16, 128], BF16) for i in range(2)]
    wB = [kb.sbuf(f"wB{i}", [128, 16, 128], BF16) for i in range(2)]
    wT = [kb.sbuf(f"wT{i}", [128, 16, 512], BF16) for i in range(2)]
    t1 = [kb.sbuf(f"t1_{i}", [128, 512], F32) for i in range(2)]
    t2 = [kb.sbuf(f"t2_{i}", [128, 512], F32) for i in range(2)]
    ob = [kb.sbuf(f"ob{i}", [128, 512], BF16) for i in range(3)]
    o32 = [kb.sbuf(f"o32_{i}", [128, 512], F32) for i in range(2)]
    otm = [kb.sbuf(f"otm{i}", [128, 512], BF16) for i in range(2)]
    o16 = [kb.sbuf(f"o16_{i}", [128, 16], F32) for i in range(2)]
    csA = kb.sbuf("csA", [128, 2, TG], F32)
    csB = kb.sbuf("csB", [128, 2, TG], F32) if cs64 is not None else None

    kb.dma("sp", gsb[:], norm.ap.rearrange("(k p) -> p k", p=128), gsb, reads=[norm],
           writes=[gsb], allow_slow_non_contiguous=True)
    ident = cbs(cb, C.IDENT)
    wsrc = w_in.ap.rearrange("(k p) n -> p k n", p=128)

    def load_w(dst, segs):
        for (do, so, n) in segs:
            for k0 in range(0, 16, 8):
                kb.dma("pool", dst[:, k0:k0 + 8, do:do + n], wsrc[:, k0:k0 + 8, so:so + n], dst,
                       reads=[w_in], writes=[dst])

    cnt = {"a": 0, "o": 0, "t": 0, "o32": 0, "tm": 0, "otm": 0}
    for G in range(S // TG):
        g0 = G * TG
        if cs128 is not None:
            kb.dma("sp", csA[:], cs128.ap[:, :, g0:g0 + TG].rearrange("c p t -> p c t"), csA,
                   reads=[cs128], writes=[csA])
        if cs64 is not None:
            kb.dma("sp", csB[:], cs64.ap[:, :, g0:g0 + TG].rearrange("c p t -> p c t"), csB,
                   reads=[cs64], writes=[csB])
        for i in range(NTT):
            b = i % 2
            r0 = g0 + i * 128
            kb.dma("sp", xt[b][:], xsrc.ap[r0:r0 + 128, :], xt[b], reads=[xsrc], writes=[xt[b]])
            act(kb, sq[:], xt[b][:], AF.Square, [xt[b]], [sq, stt[b]], accum_out=stt[b][:, 0:1])
            kb.op("dve", lambda e: e.tensor_scalar(out=stt[b][:, 1:2], in0=stt[b][:, 0:1],
                                                   scalar1=1.0 / D, scalar2=EPS, op0=ALU.mult,
                                                   op1=ALU.add), reads=[stt[b]], writes=[stt[b]])
            act(kb, stt[b][:, 2:3], stt[b][:, 1:2], AF.Ln, [stt[b]], [stt[b]])
            act(kb, stt[b][:, 3:4], stt[b][:, 2:3], AF.Exp, [stt[b]], [stt[b]], scale=-0.5)
            kb.op("dve", lambda e: e.tensor_scalar(out=xn[b][:], in0=xt[b][:],
                                                   scalar1=stt[b][:, 3:4], scalar2=None,
                                                   op0=ALU.mult), reads=[xt[b], stt[b]],
                  writes=[xn[b]])
            for half in range(2):
                for k in range(8):
                    kk = half * 8 + k
                    kb.op("pe", lambda e: e.transpose(out=tp[half][:, k, :],
                                                      in_=xn[b][:, kk * 128:(kk + 1) * 128],
                                                      identity=ident),
                          reads=[xn[b], cb], writes=[tp[half]])
                tt(kb, "dve", hT[:, half * 8:(half + 1) * 8, i * 128:(i + 1) * 128],
                   tp[half][:, :, :],
                   gsb[:, half * 8:(half + 1) * 8].unsqueeze(2).broadcast_to([128, 8, 128]),
                   ALU.mult, [tp[half], gsb], [hT])
        nfm = len(fm_chunks)
        if nfm:
            load_w(wA[0], fm_chunks[0]["segA"])
            if fm_chunks[0]["segB"]:
                load_w(wB[0], fm_chunks[0]["segB"])
        for ci, ch in enumerate(fm_chunks):
            wa, wb = wA[ci % 2], wB[ci % 2]
            if ci + 1 < nfm:
                nx = fm_chunks[ci + 1]
                load_w(wA[(ci + 1) % 2], nx["segA"])
                if nx["segB"]:
                    load_w(wB[(ci + 1) % 2], nx["segB"])
            nco = ch["ncols"]
            for sub in range(TG // 512):
                tok0 = g0 + sub * 512
                sl = slice(sub * 512, (sub + 1) * 512)
                pa = pA[cnt["a"] % 2]
                pb = pB[cnt["a"] % 2]
                cnt["a"] += 1
                for k in range(16):
                    mm(kb, pa[0:nco, :], wa[:, k, 0:nco], hT[:, k, sl], k == 0, k == 15,
                       [wa, hT], [pa])
                if ch["segB"]:
                    for k in range(16):
                        mm(kb, pb[0:nco, :], wb[:, k, 0:nco], hT[:, k, sl], k == 0, k == 15,
                           [wb, hT], [pb])
                kind = ch["kind"]
                if kind == "copy32":
                    o = o32[cnt["o32"] % 2]
                    cnt["o32"] += 1
                    act(kb, o[0:nco, :], pa[0:nco, :], AF.Copy, [pa], [o])
                else:
                    o = ob[cnt["o"] % 3]
                    cnt["o"] += 1
                    if kind == "rope":
                        cst = csA if ch["cs"] == 128 else csB
                        a1, a2 = t1[cnt["t"] % 2], t2[cnt["t"] % 2]
                        cnt["t"] += 1
                        sc = float(ch["scale"])
                        kb.op("dve", lambda e: e.scalar_tensor_tensor(
                            out=a1[0:nco, :], in0=pa[0:nco, :], scalar=sc, in1=cst[0:nco, 0, sl],
                            op0=ALU.mult, op1=ALU.mult), reads=[pa, cst], writes=[a1])
                        kb.op("dve", lambda e: e.scalar_tensor_tensor(
                            out=a2[0:nco, :], in0=pb[0:nco, :], scalar=sc, in1=cst[0:nco, 1, sl],
                            op0=ALU.mult, op1=ALU.mult), reads=[pb, cst], writes=[a2])
                        tt(kb, "pool", o[0:nco, :], a1[0:nco, :], a2[0:nco, :], ALU.add,
                           [a1, a2], [o])
                    elif kind == "copy":
                        act(kb, o[0:nco, :], pa[0:nco, :], AF.Copy, [pa], [o],
                            scale=float(ch["scale"]))
                    elif kind == "silu":
                        act(kb, o[0:nco, :], pa[0:nco, :], AF.Silu, [pa], [o])
                dbuf, dap = ch["dest"](tok0)
                kb.dma("sp", dap, o[0:nco, :], o, reads=[o], writes=[dbuf])
        ntm = len(tm_pieces)
        if ntm:
            load_w(wT[0], tm_pieces[0]["seg"])
        for pi, pc in enumerate(tm_pieces):
            wt = wT[pi % 2]
            if pi + 1 < ntm:
                load_w(wT[(pi + 1) % 2], tm_pieces[pi + 1]["seg"])
            nco = pc["ncols"]
            for i in range(NTT):
                r0 = g0 + i * 128
                pt_ = pT[cnt["tm"] % 2]
                cnt["tm"] += 1
                for k in range(16):
                    mm(kb, pt_[:, 0:nco], hT[:, k, i * 128:(i + 1) * 128], wt[:, k, 0:nco],
                       k == 0, k == 15, [hT, wt], [pt_])
                for (off, n, dfn, dt_) in pc["dests"]:
                    if dt_ == "f32":
                        o = o16[cnt["otm"] % 2]
                    else:
                        o = otm[cnt["otm"] % 2]
                    cnt["otm"] += 1
                    act(kb, o[:, 0:n], pt_[:, off:off + n], AF.Copy, [pt_], [o])
                    dbuf, dap = dfn(r0)
                    kb.dma("sp", dap, o[:, 0:n], o, reads=[o], writes=[dbuf])
    kb.end_phase()


def attn_epilogue(kb, OT, DEN, gt, rden, ob, ygT, row0, t0, ncol=512):
    if DEN is not None:
        act(kb, rden[:, 0:ncol], DEN[:, 0:ncol], AF.Ln, [DEN], [rden])
        act(kb, rden[:, 0:ncol], rden[:, 0:ncol], AF.Exp, [rden], [rden], scale=-1.0)
        tt(kb, "pool", rden[:, 0:ncol], rden[:, 0:ncol], gt[:, 0:ncol], ALU.mult,
           [rden, gt], [rden])
        tt(kb, "dve", ob[:, 0:ncol], OT[:, 0:ncol], rden[:, 0:ncol], ALU.mult, [OT, rden], [ob])
    else:
        tt(kb, "dve", ob[:, 0:ncol], OT[:, 0:ncol], gt[:, 0:ncol], ALU.mult, [OT, gt], [ob])
    kb.dma("sp", ygT.ap[row0:row0 + 128, t0:t0 + ncol], ob[:, 0:ncol], ob, reads=[ob],
           writes=[ygT])


NIT = 28


def phase_dsa(kb, cb, qaT, kaT, va, iqT, ikT, iw, gT, ygT):
    kb.begin_phase()
    KT = kb.sbuf("KT", [128, S], BF16)
    VA = kb.sbuf("VA", [128, NB, 128], BF16)
    IK = kb.sbuf("IK", [128, S], BF16)
    iqg = [kb.sbuf(f"iqg{i}", [128, 8, 512], BF16) for i in range(2)]
    iwg = [kb.sbuf(f"iwg{i}", [128, 4, 16], F32) for i in range(2)]
    score = [kb.sbuf(f"score{i}", [128, S], F32) for i in range(2)]
    junk = kb.sbuf("junk", [128, S], BF16)
    rl = [kb.sbuf(f"rl{i}", [128, 512], F32) for i in range(2)]
    bs = [kb.sbuf(f"bs{i}", [128, 4], F32) for i in range(2)]
    wt = [kb.sbuf(f"wt{i}", [128, NIT], F32) for i in range(2)]
    md = [kb.sbuf(f"md{i}", [128, NIT + 2], F32) for i in range(2)]
    cn = [kb.sbuf(f"cn{i}", [128, NIT], F32) for i in range(2)]
    tq = [kb.sbuf(f"tq{i}", [128, NIT], F32) for i in range(2)]
    pw = kb.sbuf("pw", [128, NIT], F32)
    mbf = [kb.sbuf(f"mbf{i}", [128, S], BF16) for i in range(2)]
    maskT = [kb.sbuf(f"maskT{i}", [128, NB, 512], BF16) for i in range(2)]
    QT = [kb.sbuf(f"QT{i}", [128, 512], BF16) for i in range(2)]
    PT = [kb.sbuf(f"PT{i}", [128, 512], BF16) for i in range(3)]
    PM = [kb.sbuf(f"PM{i}", [128, 512], BF16) for i in range(3)]
    gt = [kb.sbuf(f"gt{i}", [128, 512], BF16) for i in range(2)]
    rden = [kb.sbuf(f"rden{i}", [128, 512], F32) for i in range(2)]
    ob = [kb.sbuf(f"obd{i}", [128, 512], BF16) for i in range(2)]
    pd = [kb.psum(f"pd{i}", [128, 512], F32) for i in range(2)]
    ptp = kb.psum("ptp", [128, 8, 128], BF16)
    pS = [kb.psum(f"pS{i}", [128, 512], F32) for i in range(2)]
    OT = kb.psum("OT", [128, 512], F32)
    DEN = kb.psum("DEN", [128, 512], F32)
    ident = cbs(cb, C.IDENT)
    ones = cbs(cb, C.ONES)

    kb.dma("sp", KT[:], kaT.ap[:, :], KT, reads=[kaT], writes=[KT])
    kb.dma("sp", IK[:], ikT.ap[:, :], IK, reads=[ikT], writes=[IK])
    vsrc = va.ap.rearrange("(b p) c -> p b c", p=128)
    for b0 in range(0, NB, 8):
        kb.dma("sp", VA[:, b0:b0 + 8, :], vsrc[:, b0:b0 + 8, :], VA, reads=[va], writes=[VA])
    for i in range(NIT):
        kb.op("dve", lambda e: e.memset(pw[:, i:i + 1], (0.5 ** (i + 1)) * 1.000001), writes=[pw])

    cnt = {"pd": 0, "q": 0, "ob": 0, "jb": 0}

    def idx_tasks(g):
        tasks = []
        t0 = g * 512
        iq_, iw_ = iqg[g % 2], iwg[g % 2]
        mT = maskT[g % 2]

        def loads():
            kb.dma("sp", iq_[:], iqT.ap[:, :, t0:t0 + 512].rearrange("c p t -> p c t"), iq_,
                   reads=[iqT], writes=[iq_])
            kb.dma("sp", iw_[:], iw.ap[t0:t0 + 512, :].rearrange("(j p) h -> p j h", p=128), iw_,
                   reads=[iw], writes=[iw_])
        tasks.append(loads)
        for j in range(4):
            qb = 4 * g + j
            n = (qb + 1) * 128
            jb = (4 * g + j) % 2
            sc, mb = score[jb], mbf[jb]
            b_, w_, m_, c_, t_ = bs[jb], wt[jb], md[jb], cn[jb], tq[jb]
            nch = (n + 511) // 512
            for chn in range(nch):
                w = min(512, n - chn * 512)
                cs = slice(chn * 512, chn * 512 + w)
                for h in range(16):
                    def idx_unit(h=h, w=w, cs=cs, j=j, sc=sc):
                        c, half = h // 2, h % 2
                        ps_ = slice(half * 64, (half + 1) * 64)
                        p = pd[cnt["pd"] % 2]
                        r = rl[cnt["pd"] % 2]
                        cnt["pd"] += 1
                        mm(kb, p[:, 0:w], iq_[ps_, c, j * 128:(j + 1) * 128], IK[ps_, cs], True,
                           True, [iq_, IK], [p])
                        act(kb, r[:, 0:w], p[:, 0:w], AF.Relu, [p], [r])
                        if h == 0:
                            kb.op("dve", lambda e: e.tensor_scalar(
                                out=sc[:, cs], in0=r[:, 0:w], scalar1=iw_[:, j, 0:1], scalar2=None,
                                op0=ALU.mult), reads=[r, iw_], writes=[sc])
                        else:
                            kb.op("dve", lambda e: e.scalar_tensor_tensor(
                                out=sc[:, cs], in0=r[:, 0:w], scalar=iw_[:, j, h:h + 1],
                                in1=sc[:, cs], op0=ALU.mult, op1=ALU.add), reads=[r, iw_, sc],
                                writes=[sc])
                    tasks.append(idx_unit)

            def causal(qb=qb, sc=sc):
                dsl = slice(qb * 128, (qb + 1) * 128)
                kb.op("pool", lambda e: e.affine_select(
                    out=sc[:, dsl], in_=sc[:, dsl], pattern=[[-1, 128]], compare_op=ALU.is_ge,
                    fill=NEG, base=0, channel_multiplier=1), reads=[sc], writes=[sc])
            tasks.append(causal)
            if qb >= 2:
                def bracket(n=n, sc=sc, b_=b_, w_=w_, m_=m_, c_=c_):
                    kb.op("dve", lambda e: e.tensor_reduce(out=b_[:, 0:1], in_=sc[:, 0:n],
                                                           axis=mybir.AxisListType.X, op=ALU.max),
                          reads=[sc], writes=[b_])
                    kb.op("dve", lambda e: e.tensor_reduce(out=b_[:, 1:2], in_=sc[:, 0:n - 128],
                                                           axis=mybir.AxisListType.X, op=ALU.min),
                          reads=[sc], writes=[b_])
                    tt(kb, "dve", b_[:, 2:3], b_[:, 0:1], b_[:, 1:2], ALU.subtract, [b_], [b_])
                    kb.op("dve", lambda e: e.tensor_scalar(out=w_[:, :], in0=pw[:, :],
                                                           scalar1=b_[:, 2:3], scalar2=None,
                                                           op0=ALU.mult), reads=[pw, b_], writes=[w_])
                    kb.op("dve", lambda e: e.memset(c_[:, :], 0.0), writes=[c_])
                    tt(kb, "dve", m_[:, 0:1], b_[:, 1:2], w_[:, 0:1], ALU.add, [b_, w_], [m_])
                tasks.append(bracket)
                for it in range(NIT):
                    def bis(it=it, n=n, sc=sc, w_=w_, m_=m_, c_=c_, t_=t_):
                        kb.op("dve", lambda e: e.tensor_scalar(
                            out=junk[:, 0:n], in0=sc[:, 0:n], scalar1=m_[:, it:it + 1], scalar2=0.0,
                            op0=ALU.is_ge, op1=ALU.add, accum_out=c_[:, it:it + 1]),
                            reads=[sc, m_], writes=[junk, c_])
                        kb.op("dve", lambda e: e.tensor_scalar(
                            out=t_[:, it:it + 1], in0=c_[:, it:it + 1], scalar1=255.5, scalar2=0.5,
                            op0=ALU.is_ge, op1=ALU.subtract), reads=[c_], writes=[t_])
                        kb.op("dve", lambda e: e.scalar_tensor_tensor(
                            out=m_[:, it + 1:it + 2], in0=t_[:, it:it + 1], scalar=w_[:, it:it + 1],
                            in1=m_[:, it:it + 1], op0=ALU.mult, op1=ALU.add), reads=[t_, w_, m_],
                            writes=[m_])
                    tasks.append(bis)

                def mk(n=n, sc=sc, mb=mb, w_=w_, m_=m_):
                    kb.op("dve", lambda e: e.scalar_tensor_tensor(
                        out=m_[:, NIT + 1:NIT + 2], in0=w_[:, NIT - 1:NIT], scalar=-0.5,
                        in1=m_[:, NIT:NIT + 1], op0=ALU.mult, op1=ALU.add), reads=[w_, m_],
                        writes=[m_])
                    kb.op("dve", lambda e: e.tensor_scalar(
                        out=mb[:, 0:n], in0=sc[:, 0:n], scalar1=m_[:, NIT + 1:NIT + 2],
                        scalar2=None, op0=ALU.is_ge), reads=[sc, m_], writes=[mb])
                tasks.append(mk)
            else:
                def mk(n=n, sc=sc, mb=mb):
                    kb.op("dve", lambda e: e.tensor_scalar(
                        out=mb[:, 0:n], in0=sc[:, 0:n], scalar1=-1.0e30, scalar2=None,
                        op0=ALU.is_ge), reads=[sc], writes=[mb])
                tasks.append(mk)
            for k0 in range(0, qb + 1, 8):
                def tr(k0=k0, qb=qb, mb=mb, j=j):
                    nk = min(8, qb + 1 - k0)
                    for kk in range(nk):
                        kb.op("pe", lambda e: e.transpose(
                            out=ptp[:, kk, :], in_=mb[:, (k0 + kk) * 128:(k0 + kk + 1) * 128],
                            identity=ident), reads=[mb, cb], writes=[ptp])
                    act(kb, mT[:, k0:k0 + nk, j * 128:(j + 1) * 128], ptp[:, 0:nk, :], AF.Copy,
                        [ptp], [mT])
                tasks.append(tr)
        return tasks

    for t in idx_tasks(0):
        t()
    for g in range(8):
        t0 = g * 512
        mT = maskT[g % 2]
        nxt = idx_tasks(g + 1) if g + 1 < 8 else []
        nkb = 4 * (g + 1)
        total_units = 10 * nkb
        done_tasks = 0
        unit_i = 0
        for h in range(10):
            q = QT[cnt["q"] % 2]
            gg = gt[cnt["q"] % 2]
            rd = rden[cnt["q"] % 2]
            cnt["q"] += 1
            kb.dma("sp", q[:], qaT.ap[h, :, t0:t0 + 512], q, reads=[qaT], writes=[q])
            kb.dma("sp", gg[:], gT.ap[h * 128:(h + 1) * 128, t0:t0 + 512], gg, reads=[gT],
                   writes=[gg])

            def qk(kbi):
                c0 = max(0, kbi - 4 * g) * 128
                p = pS[kbi % 2]
                mm(kb, p[:, c0:512], KT[:, kbi * 128:(kbi + 1) * 128], q[:, c0:512], True, True,
                   [KT, q], [p])

            qk(0)
            for kbi in range(nkb):
                if kbi + 1 < nkb:
                    qk(kbi + 1)
                c0 = max(0, kbi - 4 * g) * 128
                p = pS[kbi % 2]
                pt_ = PT[kbi % 3]
                pm = PM[kbi % 3]
                act(kb, pt_[:, c0:512], p[:, c0:512], AF.Exp, [p], [pt_])
                tt(kb, "pool", pm[:, c0:512], pt_[:, c0:512], mT[:, kbi, c0:512], ALU.mult,
                   [pt_, mT], [pm])
                mm(kb, OT[:, c0:512], VA[:, kbi, :], pm[:, c0:512], kbi == 0, kbi == nkb - 1,
                   [VA, pm], [OT])
                mm(kb, DEN[:, c0:512], ones, pm[:, c0:512], kbi == 0, kbi == nkb - 1,
                   [cb, pm], [DEN])
                unit_i += 1
                target = (len(nxt) * unit_i) // total_units
                while done_tasks < target:
                    nxt[done_tasks]()
                    done_tasks += 1
            o = ob[cnt["ob"] % 2]
            cnt["ob"] += 1
            attn_epilogue(kb, OT, DEN, gg, rd, o, ygT, h * 128, t0)
        while done_tasks < len(nxt):
            nxt[done_tasks]()
            done_tasks += 1
    kb.end_phase()


def phase_dilated(kb, cb, qbT, kbT, vb, gT, ygT):
    kb.begin_phase()
    QT = [kb.sbuf(f"dQ{i}", [128, S], BF16) for i in range(2)]
    KT = [kb.sbuf(f"dK{i}", [128, S], BF16) for i in range(2)]
    VC = [kb.sbuf(f"dV{i}", [128, NB, 128], BF16) for i in range(2)]
    acc = kb.sbuf("acc", [128, S], F32)
    dacc = kb.sbuf("dacc", [128, S], F32)
    gt = kb.sbuf("dgt", [128, S], BF16)
    ob = kb.sbuf("dob", [128, S], BF16)
    PT = [kb.sbuf(f"dPT{i}", [128, 2, 128], BF16) for i in range(3)]
    PM = [kb.sbuf(f"dPM{i}", [128, 2, 128], BF16) for i in range(3)]
    pS = [kb.psum(f"dpS{i}", [128, 512], F32) for i in range(2)]
    pO = [kb.psum(f"dpO{i}", [128, 512], F32) for i in range(2)]
    pD = [kb.psum(f"dpD{i}", [128, 512], F32) for i in range(2)]
    ones = cbs(cb, C.ONES)
    band = cb[:, C.GE * 128:(C.GE + 2) * 128].rearrange("p (a c) -> p a c", a=2)
    u = 0
    vi = 0
    for h in range(6):
        q, k = QT[h % 2], KT[h % 2]
        kb.dma("sp", q[:], qbT.ap[h, :, :], q, reads=[qbT], writes=[q])
        kb.dma("sp", k[:], kbT.ap[h, :, :], k, reads=[kbT], writes=[k])
        kb.dma("sp", gt[:], gT.ap[1280 + h * 128:1280 + (h + 1) * 128, :], gt, reads=[gT],
               writes=[gt])
        for pi, dil in enumerate((1, 4, 16)):
            nb = NB // dil
            v = VC[vi % 2]
            vi += 1
            vsrc = vb.ap[:, h * 128:(h + 1) * 128].rearrange("(n i d) c -> i d n c", d=dil, i=128)
            vdst = v[:].rearrange("p (d n) c -> p d n c", d=dil)
            for r in range(dil):
                for n0 in range(0, nb, 8):
                    n1 = min(nb, n0 + 8)
                    kb.dma("sp", vdst[:, r, n0:n1, :], vsrc[:, r, n0:n1, :], v, reads=[vb],
                           writes=[v])
            units = [(r, n) for r in range(dil) for n in range(nb)]

            def cols(r, m):
                s0_ = m * 128 * dil + r
                return slice(s0_, s0_ + 127 * dil + 1, dil)

            def qk(ui, r, n):
                st = pS[ui % 2]
                s0 = 0 if n > 0 else 1
                stv = st[:, 0:256].rearrange("p (a c) -> p a c", a=2)
                for sl_ in range(s0, 2):
                    m = n - 1 + sl_
                    mm(kb, stv[:, sl_, :], k[:, cols(r, m)], q[:, cols(r, n)], True, True, [k, q],
                       [st])

            qk(u, *units[0])
            for li, (r, n) in enumerate(units):
                if li + 1 < len(units):
                    qk(u + 1, *units[li + 1])
                st = pS[u % 2]
                po = pO[u % 2]
                pdn = pD[u % 2]
                pt_ = PT[u % 3]
                pm = PM[u % 3]
                u += 1
                s0 = 0 if n > 0 else 1
                stv = st[:, 0:256].rearrange("p (a c) -> p a c", a=2)
                act(kb, pt_[:, s0:2, :], stv[:, s0:2, :], AF.Exp, [st], [pt_])
                tt(kb, "pool", pm[:, s0:2, :], pt_[:, s0:2, :], band[:, s0:2, :], ALU.mult,
                   [pt_, cb], [pm])
                for sl_ in range(s0, 2):
                    m = n - 1 + sl_
                    mm(kb, po[:, 0:128], v[:, r * nb + m, :], pm[:, sl_, :], sl_ == s0, sl_ == 1,
                       [v, pm], [po])
                for sl_ in range(s0, 2):
                    mm(kb, pdn[:, 0:128], ones, pm[:, sl_, :], sl_ == s0, sl_ == 1, [cb, pm],
                       [pdn])
                cs = cols(r, n)
                if pi == 0:
                    kb.op("dve", lambda e: e.tensor_copy(out=acc[:, cs], in_=po[:, 0:128]),
                          reads=[po], writes=[acc])
                    kb.op("dve", lambda e: e.tensor_copy(out=dacc[:, cs], in_=pdn[:, 0:128]),
                          reads=[pdn], writes=[dacc])
                else:
                    tt(kb, "dve", acc[:, cs], po[:, 0:128], acc[:, cs], ALU.add, [po, acc], [acc])
                    tt(kb, "dve", dacc[:, cs], pdn[:, 0:128], dacc[:, cs], ALU.add,
                       [pdn, dacc], [dacc])
        act(kb, dacc[:], dacc[:], AF.Ln, [dacc], [dacc])
        act(kb, dacc[:], dacc[:], AF.Exp, [dacc], [dacc], scale=-1.0)
        tt(kb, "dve", acc[:], acc[:], dacc[:], ALU.mult, [acc, dacc], [acc])
        tt(kb, "dve", ob[:], acc[:], gt[:], ALU.mult, [acc, gt], [ob])
        kb.dma("sp", ygT.ap[1280 + h * 128:1280 + (h + 1) * 128, :], ob[:], ob, reads=[ob],
               writes=[ygT])
    kb.end_phase()


def phase_sb(kb, cb, qcT, kcT, vc, gT, ygT):
    kb.begin_phase()
    KT = [kb.sbuf(f"sK{i}", [128, S], BF16) for i in range(2)]
    VV = [kb.sbuf(f"sV{i}", [128, NB, 128], BF16) for i in range(2)]
    QT = [kb.sbuf(f"sQ{i}", [128, 512], BF16) for i in range(2)]
    gt = [kb.sbuf(f"sg{i}", [128, 512], BF16) for i in range(2)]
    E = [kb.sbuf(f"sE{i}", [128, 512], F32) for i in range(2)]
    LP = [kb.sbuf(f"sL{i}", [128, 512], BF16) for i in range(3)]
    RS = kb.sbuf("sRS", [128, 512], BF16)
    PT = [kb.sbuf(f"sP{i}", [128, 512], BF16) for i in range(3)]
    ob = [kb.sbuf(f"sob{i}", [128, 512], BF16) for i in range(2)]
    pZ = [kb.psum(f"pZ{i}", [128, 512], F32) for i in range(2)]
    pW = [kb.psum(f"pW{i}", [128, 512], F32) for i in range(2)]
    OT = [kb.psum(f"sOT{i}", [128, 512], F32) for i in range(2)]
    negU = cbs(cb, C.NEGU)
    negones = cbs(cb, C.NEGONES)
    lt = cbs(cb, C.LT)
    u = 0
    for h in range(8):
        k, v = KT[h % 2], VV[h % 2]
        kb.dma("sp", k[:], kcT.ap[h, :, :], k, reads=[kcT], writes=[k])
        vsrc = vc.ap[:, h * 128:(h + 1) * 128].rearrange("(b p) c -> p b c", p=128)
        for b0 in range(0, NB, 8):
            kb.dma("sp", v[:, b0:b0 + 8, :], vsrc[:, b0:b0 + 8, :], v, reads=[vc], writes=[v])
        for g in range(8):
            t0 = g * 512
            q = QT[u % 2]
            gg = gt[u % 2]
            ot = OT[u % 2]
            o = ob[u % 2]
            u += 1
            kb.dma("sp", q[:], qcT.ap[h, :, t0:t0 + 512], q, reads=[qcT], writes=[q])
            kb.dma("sp", gg[:], gT.ap[h * 128:(h + 1) * 128, t0:t0 + 512], gg, reads=[gT],
                   writes=[gg])
            nkb = 4 * (g + 1)
            order = list(range(nkb - 1, -1, -1))

            def zmm(kbi, idx):
                c0 = max(0, kbi - 4 * g) * 128
                mm(kb, pZ[idx % 2][:, c0:512], k[:, kbi * 128:(kbi + 1) * 128], q[:, c0:512],
                   True, True, [k, q], [pZ[idx % 2]])

            def stage2(idx, kbi):
                c0 = max(0, kbi - 4 * g) * 128
                w = pW[idx % 2]
                pt_ = PT[idx % 3]
                act(kb, pt_[:, c0:512], w[:, c0:512], AF.Exp, [w], [pt_])
                if kbi >= 4 * g:
                    tt(kb, "pool", pt_[:, c0:c0 + 128], pt_[:, c0:c0 + 128], lt, ALU.mult,
                       [pt_, cb], [pt_])
                mm(kb, ot[:, c0:512], v[:, kbi, :], pt_[:, c0:512], idx == 0, idx == nkb - 1,
                   [v, pt_], [ot])

            zmm(order[0], 0)
            prev = None
            for idx, kbi in enumerate(order):
                c0 = max(0, kbi - 4 * g) * 128
                z = pZ[idx % 2]
                w = pW[idx % 2]
                e_ = E[idx % 2]
                lp = LP[idx % 3]
                diag = kbi >= 4 * g
                act(kb, e_[:, c0:512], z[:, c0:512], AF.Exp, [z], [e_])
                act(kb, lp[:, c0:512], e_[:, c0:512], AF.Ln, [e_], [lp], bias=1.0)
                if diag:
                    tt(kb, "pool", lp[:, c0:c0 + 128], lp[:, c0:c0 + 128], lt, ALU.mult,
                       [lp, cb], [lp])
                if idx + 1 < nkb:
                    zmm(order[idx + 1], idx + 1)
                mm(kb, w[:, c0:512], k[:, kbi * 128:(kbi + 1) * 128], q[:, c0:512], True, False,
                   [k, q], [w])
                mm(kb, w[:, c0:512], negU, lp[:, c0:512], False, idx == 0, [cb, lp], [w])
                if idx > 0:
                    mm(kb, w[:, c0:512], negones, RS[:, c0:512], False, True, [cb, RS], [w])
                if idx == 0:
                    kb.op("dve", lambda e: e.memset(RS[:], 0.0), writes=[RS])
                if idx + 1 < nkb:
                    tt(kb, "dve", RS[:, c0:512], RS[:, c0:512], lp[:, c0:512], ALU.add, [RS, lp],
                       [RS])
                if prev is not None:
                    stage2(*prev)
                prev = (idx, kbi)
            stage2(*prev)
            attn_epilogue(kb, ot, None, gg, None, o, ygT, h * 128, t0)
    kb.end_phase()


def phase_fox_prep(kb, flT, b_f, fA, fB):
    kb.begin_phase()
    fl = kb.sbuf("fl", [8, S], F32)
    bf = kb.sbuf("bf", [8, 2], F32)
    e_ = kb.sbuf("fe", [8, S], F32)
    ones = kb.sbuf("fones", [8, S], F32)
    cc = kb.sbuf("fcc", [8, S], F32)
    rr = kb.sbuf("frr", [8, S], F32)
    A = kb.sbuf("fAs", [8, 6, S], BF16)
    B = kb.sbuf("fBs", [8, 6, S], BF16)
    kb.dma("sp", fl[:], flT.ap[:, :], fl, reads=[flT], writes=[fl])
    kb.dma("sp", bf[:, 0:1], b_f.ap.rearrange("(h o) -> h o", o=1), bf, reads=[b_f], writes=[bf])
    kb.op("dve", lambda e: e.tensor_scalar(out=bf[:, 1:2], in0=bf[:, 0:1], scalar1=-1.0,
                                           scalar2=None, op0=ALU.mult), reads=[bf], writes=[bf])
    act(kb, e_[:], fl[:], AF.Exp, [fl, bf], [e_], scale=-1.0, bias=bf[:, 1:2])
    act(kb, e_[:], e_[:], AF.Ln, [e_], [e_], bias=1.0)
    kb.op("dve", lambda e: e.memset(ones[:], 1.0), writes=[ones])
    kb.op("dve", lambda e: e.tensor_tensor_scan(out=cc[:], data0=ones[:], data1=e_[:], initial=0.0,
                                                op0=ALU.mult, op1=ALU.add), reads=[ones, e_],
          writes=[cc])
    cur = cc
    for i in range(3):
        kb.op("dve", lambda e: e.tensor_copy(out=A[:, i, :], in_=cur[:]), reads=[cur], writes=[A])
        kb.op("dve", lambda e: e.tensor_scalar(out=B[:, 3 + i, :], in0=A[:, i, :], scalar1=-1.0,
                                               scalar2=None, op0=ALU.mult), reads=[A], writes=[B])
        if i < 2:
            nxt = rr if cur is cc else cc
            tt(kb, "dve", nxt[:], cur[:], A[:, i, :], ALU.subtract, [cur, A], [nxt])
            cur = nxt
    kb.op("dve", lambda e: e.memset(A[:, 3:6, :], 1.0), writes=[A])
    kb.op("dve", lambda e: e.memset(B[:, 0:3, :], 1.0), writes=[B])
    kb.dma("sp", fA.ap[:, :, :], A[:], A, reads=[A], writes=[fA])
    kb.dma("sp", fB.ap[:, :, :], B[:], B, reads=[B], writes=[fB])
    kb.end_phase()


def phase_fox(kb, cb, qdT, kdT, vd, fA, fB, gT, ygT):
    kb.begin_phase()
    KT = [kb.sbuf(f"fK{i}", [128, S], BF16) for i in range(2)]
    VV = [kb.sbuf(f"fV{i}", [128, NB, 128], BF16) for i in range(2)]
    AA = [kb.sbuf(f"fA{i}", [6, S], BF16) for i in range(2)]
    BB = [kb.sbuf(f"fB{i}", [6, S], BF16) for i in range(2)]
    QT = [kb.sbuf(f"fQ{i}", [128, 512], BF16) for i in range(2)]
    gt = [kb.sbuf(f"fg{i}", [128, 512], BF16) for i in range(2)]
    PT = [kb.sbuf(f"fP{i}", [128, 512], BF16) for i in range(3)]
    rden = kb.sbuf("frden", [128, 512], F32)
    ob = [kb.sbuf(f"fob{i}", [128, 512], BF16) for i in range(2)]
    pS = [kb.psum(f"fpS{i}", [128, 512], F32) for i in range(3)]
    OT = [kb.psum(f"fOT{i}", [128, 512], F32) for i in range(2)]
    DEN = [kb.psum(f"fDEN{i}", [128, 512], F32) for i in range(2)]
    ones = cbs(cb, C.ONES)
    le = cbs(cb, C.LE)
    u = 0
    for h in range(8):
        k, v, A, B = KT[h % 2], VV[h % 2], AA[h % 2], BB[h % 2]
        kb.dma("sp", k[:], kdT.ap[h, :, :], k, reads=[kdT], writes=[k])
        kb.dma("sp", A[:], fA.ap[h, :, :], A, reads=[fA], writes=[A])
        kb.dma("sp", B[:], fB.ap[h, :, :], B, reads=[fB], writes=[B])
        vsrc = vd.ap[:, h * 128:(h + 1) * 128].rearrange("(b p) c -> p b c", p=128)
        for b0 in range(0, NB, 8):
            kb.dma("sp", v[:, b0:b0 + 8, :], vsrc[:, b0:b0 + 8, :], v, reads=[vd], writes=[v])
        for g in range(8):
            t0 = g * 512
            q = QT[u % 2]
            gg = gt[u % 2]
            ot = OT[u % 2]
            dn = DEN[u % 2]
            o = ob[u % 2]
            u += 1
            kb.dma("sp", q[:], qdT.ap[h, :, t0:t0 + 512], q, reads=[qdT], writes=[q])
            kb.dma("sp", gg[:], gT.ap[1024 + h * 128:1024 + (h + 1) * 128, t0:t0 + 512], gg,
                   reads=[gT], writes=[gg])
            nkb = 4 * (g + 1)

            def qk(kbi):
                c0 = max(0, kbi - 4 * g) * 128
                p = pS[kbi % 3]
                mm(kb, p[:, c0:512], k[:, kbi * 128:(kbi + 1) * 128], q[:, c0:512], True, False,
                   [k, q], [p])
                mm(kb, p[:, c0:512], A[:, kbi * 128:(kbi + 1) * 128], B[:, t0 + c0:t0 + 512],
                   False, True, [A, B], [p])

            qk(0)
            for kbi in range(nkb):
                if kbi + 1 < nkb:
                    qk(kbi + 1)
                c0 = max(0, kbi - 4 * g) * 128
                p = pS[kbi % 3]
                pt_ = PT[kbi % 3]
                act(kb, pt_[:, c0:512], p[:, c0:512], AF.Exp, [p], [pt_])
                if kbi >= 4 * g:
                    tt(kb, "pool", pt_[:, c0:c0 + 128], pt_[:, c0:c0 + 128], le, ALU.mult,
                       [pt_, cb], [pt_])
                mm(kb, ot[:, c0:512], v[:, kbi, :], pt_[:, c0:512], kbi == 0, kbi == nkb - 1,
                   [v, pt_], [ot])
                mm(kb, dn[:, c0:512], ones, pt_[:, c0:512], kbi == 0, kbi == nkb - 1,
                   [cb, pt_], [dn])
            attn_epilogue(kb, ot, dn, gg, rden, o, ygT, 1024 + h * 128, t0)
    kb.end_phase()


def phase_outproj(kb, ygT, w_out, xres, xdst, norm_f):
    kb.begin_phase()
    W = kb.sbuf("oW", [128, 16, D], BF16)
    yg = [kb.sbuf(f"oyg{i}", [128, 16, 512], BF16) for i in range(2)]
    xr = [kb.sbuf(f"oxr{i}", [128, D], F32) for i in range(2)]
    xw = [kb.sbuf(f"oxw{i}", [128, D], F32) for i in range(2)]
    pp = [kb.psum(f"opp{i}", [128, 512], F32) for i in range(4)]
    if norm_f is not None:
        nf = kb.sbuf("onf", [128, D], F32)
        sq = kb.sbuf("osq", [128, D], BF16)
        stt = [kb.sbuf(f"ost{i}", [128, 4], F32) for i in range(2)]
        kb.dma("sp", nf[:], norm_f.ap.partition_broadcast(128), nf, reads=[norm_f], writes=[nf])
    wsrc = w_out.ap.rearrange("(k p) n -> p k n", p=128)
    for k0 in range(0, 16, 4):
        kb.dma("pool", W[:, k0:k0 + 4, :], wsrc[:, k0:k0 + 4, :], W, reads=[w_out], writes=[W])
    ysrc = ygT.ap.rearrange("(k p) t -> p k t", p=128)
    ti = 0
    for g in range(8):
        y = yg[g % 2]
        for k0 in range(0, 16, 8):
            kb.dma("sp", y[:, k0:k0 + 8, :], ysrc[:, k0:k0 + 8, g * 512:(g + 1) * 512], y,
                   reads=[ygT], writes=[y])
        for j in range(4):
            r0 = g * 512 + j * 128
            xi, xo = xr[ti % 2], xw[ti % 2]
            kb.dma("sp", xi[:], xres.ap[r0:r0 + 128, :], xi, reads=[xres], writes=[xi])
            for n in range(4):
                p = pp[n]
                for k in range(16):
                    mm(kb, p[:, :], y[:, k, j * 128:(j + 1) * 128], W[:, k, n * 512:(n + 1) * 512],
                       k == 0, k == 15, [y, W], [p])
                tt(kb, "dve", xo[:, n * 512:(n + 1) * 512], p[:, :], xi[:, n * 512:(n + 1) * 512],
                   ALU.add, [p, xi], [xo])
            if norm_f is None:
                kb.dma("sp", xdst.ap[r0:r0 + 128, :], xo[:], xo, reads=[xo], writes=[xdst])
            else:
                st_ = stt[ti % 2]
                act(kb, sq[:], xo[:], AF.Square, [xo], [sq, st_], accum_out=st_[:, 0:1])
                kb.op("dve", lambda e: e.tensor_scalar(out=st_[:, 1:2], in0=st_[:, 0:1],
                                                       scalar1=1.0 / D, scalar2=EPS, op0=ALU.mult,
                                                       op1=ALU.add), reads=[st_], writes=[st_])
                act(kb, st_[:, 2:3], st_[:, 1:2], AF.Ln, [st_], [st_])
                act(kb, st_[:, 3:4], st_[:, 2:3], AF.Exp, [st_], [st_], scale=-0.5)
                kb.op("dve", lambda e: e.scalar_tensor_tensor(
                    out=xi[:], in0=xo[:], scalar=st_[:, 3:4], in1=nf[:], op0=ALU.mult,
                    op1=ALU.mult), reads=[xo, st_, nf], writes=[xi])
                kb.dma("sp", xdst.ap[r0:r0 + 128, :], xi[:], xi, reads=[xi], writes=[xdst])
            ti += 1
    kb.end_phase()


def rope_segs(c0, n, half):
    segs = []
    for hs in range(0, n, 2 * half):
        segs.append((hs, c0 + hs + half, half))
        segs.append((hs + half, c0 + hs, half))
    return segs


def build(phases=None, debug=False):
    nc = bass.Bass("TRN2", target_bir_lowering=False)
    kb = KB(nc)
    P = (lambda p: True) if phases is None else (lambda p: p in phases)

    def ein(name, shape, dt=F32):
        return kb.dram(name, shape, dt, kind="ExternalInput")

    def scr(name, shape, dt):
        return kb.dram(name, shape, dt, kind="ExternalOutput")

    x = ein("x", [S, D])
    norm0 = ein("norm0", [D])
    w_in0 = ein("w_in0", [D, IN0])
    w_out0 = ein("w_out0", [D, D])
    norm1 = ein("norm1", [D])
    w_in1 = ein("w_in1", [D, IN1])
    b_f1 = ein("b_f1", [8])
    w_out1 = ein("w_out1", [D, D])
    norm_f = ein("norm_f", [D])
    cs128 = ein("cs128", [2, 128, S])
    cs64 = ein("cs64", [2, 128, S])
    cbd = ein("cb", [128, 7 * 128], BF16)
    out = kb.dram("out", [S, D], F32, kind="ExternalOutput")

    qaT = scr("qaT", [10, 128, S], BF16)
    kaT = scr("kaT", [128, S], BF16)
    va = scr("va", [S, 128], BF16)
    iqT = scr("iqT", [8, 128, S], BF16)
    ikT = scr("ikT", [128, S], BF16)
    iw = scr("iw", [S, 16], F32)
    qbT = scr("qbT", [6, 128, S], BF16)
    kbT = scr("kbT", [6, 128, S], BF16)
    vb = scr("vb", [S, 768], BF16)
    g0T = scr("g0T", [D, S], BF16)
    yg0T = scr("yg0T", [D, S], BF16)
    x1 = scr("x1", [S, D], F32)
    qcT = scr("qcT", [8, 128, S], BF16)
    kcT = scr("kcT", [8, 128, S], BF16)
    vc = scr("vc", [S, 1024], BF16)
    qdT = scr("qdT", [8, 128, S], BF16)
    kdT = scr("kdT", [8, 128, S], BF16)
    vd = scr("vd", [S, 1024], BF16)
    flT = scr("flT", [8, S], F32)
    fA = scr("fA", [8, 6, S], BF16)
    fB = scr("fB", [8, 6, S], BF16)
    g1T = scr("g1T", [D, S], BF16)
    yg1T = scr("yg1T", [D, S], BF16)

    cb = kb.sbuf("cb_sb", [128, 7 * 128], BF16, glob=True)
    kb.dma("sp", cb[:], cbd.ap[:, :], cb, reads=[cbd], writes=[cb])

    def fm(c0, n, kind, dest, scale=1.0, cs=None, dup=False):
        half = 64 if cs == 128 else 32
        if dup:
            segA = [(0, c0, n), (n, c0, n)]
            segB = [(d, s, m) for (d, s, m) in rope_segs(c0, n, half)] + \
                   [(d + n, s, m) for (d, s, m) in rope_segs(c0, n, half)]
            n = 2 * n
        else:
            segA = [(0, c0, n)]
            segB = rope_segs(c0, n, half) if kind == "rope" else None
        return dict(segA=segA, segB=segB, ncols=n, kind=kind, scale=scale, dest=dest, cs=cs)

    def dest3(buf, h):
        return lambda tok0: (buf, buf.ap[h, :, tok0:tok0 + 512])

    def dest2(buf, r0, n=128):
        return lambda tok0: (buf, buf.ap[r0:r0 + n, tok0:tok0 + 512])

    def tdest(buf, c0, n):
        return lambda r0: (buf, buf.ap[r0:r0 + 128, c0:c0 + n])

    if P("in0"):
        ch = []
        for h in range(10):
            ch.append(fm(128 * h, 128, "rope", dest3(qaT, h), SC128, 128))
        ch.append(fm(1280, 128, "rope", dest2(kaT, 0), 1.0, 128))
        for c in range(8):
            ch.append(fm(1536 + 128 * c, 128, "rope", dest3(iqT, c), 1.0, 64))
        ch.append(fm(2560, 64, "rope", dest2(ikT, 0), 1.0, 64, dup=True))
        for h in range(6):
            ch.append(fm(2640 + 128 * h, 128, "rope", dest3(qbT, h), SC128, 128))
        for h in range(6):
            ch.append(fm(3408 + 128 * h, 128, "rope", dest3(kbT, h), 1.0, 128))
        for c in range(16):
            ch.append(fm(4944 + 128 * c, 128, "silu", dest2(g0T, 128 * c)))
        tm = [
            dict(seg=[(0, 1408, 128), (128, 4176, 384)], ncols=512,
                 dests=[(0, 128, tdest(va, 0, 128), "bf"), (128, 384, tdest(vb, 0, 384), "bf")]),
            dict(seg=[(0, 4560, 384), (384, 2624, 16)], ncols=400,
                 dests=[(0, 384, tdest(vb, 384, 384), "bf"), (384, 16, tdest(iw, 0, 16), "f32")]),
        ]
        phase_inproj(kb, cb, x, norm0, w_in0, ch, tm, cs128, cs64)
    if P("dsa"):
        phase_dsa(kb, cb, qaT, kaT, va, iqT, ikT, iw, g0T, yg0T)
    if P("dil"):
        phase_dilated(kb, cb, qbT, kbT, vb, g0T, yg0T)
    if P("out0"):
        phase_outproj(kb, yg0T, w_out0, x, x1, None)
    if P("in1"):
        ch = []
        for h in range(8):
            ch.append(fm(128 * h, 128, "copy", dest3(qcT, h), 1.0))
        for h in range(8):
            ch.append(fm(1024 + 128 * h, 128, "copy", dest3(kcT, h), SC128))
        for h in range(8):
            ch.append(fm(3072 + 128 * h, 128, "copy", dest3(qdT, h), 1.0))
        for h in range(8):
            ch.append(fm(4096 + 128 * h, 128, "copy", dest3(kdT, h), SC128))
        ch.append(fm(6144, 8, "copy32", dest2(flT, 0, 8)))
        for c in range(16):
            ch.append(fm(6152 + 128 * c, 128, "silu", dest2(g1T, 128 * c)))
        tm = []
        for i in range(2):
            tm.append(dict(seg=[(0, 2048 + 512 * i, 512)], ncols=512,
                           dests=[(0, 512, tdest(vc, 512 * i, 512), "bf")]))
        for i in range(2):
            tm.append(dict(seg=[(0, 5120 + 512 * i, 512)], ncols=512,
                           dests=[(0, 512, tdest(vd, 512 * i, 512), "bf")]))
        phase_inproj(kb, cb, x1, norm1, w_in1, ch, tm, None, None)
    if P("sb"):
        phase_sb(kb, cb, qcT, kcT, vc, g1T, yg1T)
    if P("fox"):
        phase_fox_prep(kb, flT, b_f1, fA, fB)
        phase_fox(kb, cb, qdT, kdT, vd, fA, fB, g1T, yg1T)
    if P("out1"):
        phase_outproj(kb, yg1T, w_out1, x1, out, norm_f)
    kb.barrier()
    return nc, kb


def host_consts():
    pos = np.arange(S, dtype=np.float32)

    def tables(hd, rows):
        half = hd // 2
        inv = (np.float32(10000.0) ** (-np.arange(half, dtype=np.float32) / np.float32(half))).astype(np.float32)
        ang = pos[None, :] * inv[:, None]
        cos = np.cos(ang).astype(np.float32)
        sin = np.sin(ang).astype(np.float32)
        c = np.zeros((2, rows, S), np.float32)
        for p in range(rows):
            d = p % hd
            c[0, p] = cos[d % half]
            c[1, p] = -sin[d % half] if d < half else sin[d % half]
        return c

    cs128 = tables(128, 128)
    cs64 = tables(64, 128)
    i = np.arange(128)[:, None]
    j = np.arange(128)[None, :]
    blocks = [
        (i == j), (i < j), (i >= j), (i <= j), np.ones((128, 128), bool),
    ]
    cbv = [b.astype(np.float32) for b in blocks]
    cbv.append(-(i >= j).astype(np.float32))
    cbv.append(-np.ones((128, 128), np.float32))
    cb = np.concatenate(cbv, axis=1).astype(ml_dtypes.bfloat16)
    return cs128, cs64, cb


_CACHE = {}


def kernel(x, norm0, w_in0, w_out0, norm1, w_in1, b_f1, w_out1, norm_f):
    if "nc" not in _CACHE:
        _CACHE["nc"] = build()[0]
        _CACHE["consts"] = host_consts()
    nc = _CACHE["nc"]
    cs128, cs64, cb = _CACHE["consts"]
    f = lambda a: np.ascontiguousarray(np.asarray(a, dtype=np.float32))
    shared = dict(norm0=f(norm0), w_in0=f(w_in0), w_out0=f(w_out0), norm1=f(norm1),
                  w_in1=f(w_in1), b_f1=f(b_f1), w_out1=f(w_out1), norm_f=f(norm_f),
                  cs128=cs128, cs64=cs64, cb=cb)
    x = np.asarray(x, dtype=np.float32)
    in_maps = [dict(shared, x=np.ascontiguousarray(x[i])) for i in range(8)]
    res = run_bass_kernel_spmd(nc, in_maps, core_ids=list(range(8)))
    return np.stack([np.asarray(r["out"], dtype=np.float32) for r in res.results], axis=0)
```

```python
import contextlib
import math
import numpy as np
import ml_dtypes
import concourse.bass as bass
import concourse.mybir as mybir
from concourse.bass_utils import run_bass_kernel_spmd

F32 = mybir.dt.float32
BF16 = mybir.dt.bfloat16
AF = mybir.ActivationFunctionType
ALU = mybir.AluOpType

S = 4096
D = 2048
NB = 32
EPS = 1e-6
NEG = -3.0e38
SC128 = 128 ** -0.5
IN0 = 6992
IN1 = 8200

EPOCH = 20000


class DSem:
    def __init__(self, handle):
        self.h = handle
        self.cnt = 0


class Buf:
    def __init__(self, ap, name):
        self.ap = ap
        self.name = name
        self.w = {}
        self.r = {}
        self.ds = None

    def __getitem__(self, k):
        return self.ap[k]


class KB:
    def __init__(self, nc):
        self.nc = nc
        self.eng = {"pe": nc.tensor, "act": nc.scalar, "dve": nc.vector,
                    "pool": nc.gpsimd, "sp": nc.sync}
        self.n = {e: 0 for e in self.eng}
        self.esems = {}
        self.waited = {e: {} for e in self.eng}
        self.gstack = contextlib.ExitStack()
        self.pstack = None
        self.free_ds = []
        self.all_ds = []
        self.phase_bufs = []

    def begin_phase(self):
        self.pstack = contextlib.ExitStack()
        self.phase_bufs = []
        self.pid = getattr(self, "pid", 0) + 1

    def end_phase(self):
        self.barrier()
        for b in self.phase_bufs:
            if b.ds is not None:
                self.free_ds.append(b.ds)
                b.ds = None
        self.pstack.close()
        self.pstack = None

    def sbuf(self, name, shape, dtype, glob=False):
        st = self.gstack if glob else self.pstack
        name = name if glob else f"{name}_p{self.pid}"
        t = st.enter_context(self.nc.sbuf_tensor(name, list(shape), dtype))
        b = Buf(t, name)
        if not glob:
            self.phase_bufs.append(b)
        return b

    def psum(self, name, shape, dtype):
        t = self.pstack.enter_context(self.nc.psum_tensor(f"{name}_p{self.pid}", list(shape), dtype))
        b = Buf(t, name)
        self.phase_bufs.append(b)
        return b

    def dram(self, name, shape, dtype, kind="Internal"):
        t = self.nc.dram_tensor(name, list(shape), dtype, kind=kind).ap()
        return Buf(t, name)

    def _get_ds(self, buf):
        if buf.ds is None:
            if self.free_ds:
                buf.ds = self.free_ds.pop()
            else:
                h = self.gstack.enter_context(self.nc.semaphore(f"d{len(self.all_ds)}"))
                buf.ds = DSem(h)
                self.all_ds.append(buf.ds)
        return buf.ds

    def _esem(self, ename, epoch):
        key = (ename, epoch)
        if key not in self.esems:
            self.esems[key] = self.gstack.enter_context(
                self.nc.semaphore(f"e_{ename}_{epoch}"))
        return self.esems[key]

    def wait(self, ename, tok):
        if tok[0] == "e":
            _, src, idx = tok
            k = ("e", src)
            if self.waited[ename].get(k, 0) >= idx:
                return
            epoch = (idx - 1) // EPOCH
            self.eng[ename].wait_ge(self._esem(src, epoch), idx - epoch * EPOCH)
            self.waited[ename][k] = idx
        else:
            _, ds, val = tok
            k = ("d", id(ds))
            if self.waited[ename].get(k, 0) >= val:
                return
            self.eng[ename].wait_ge(ds.h, val)
            self.waited[ename][k] = val

    def _deps(self, ename, reads, writes, is_dma):
        for b in reads:
            for tok in b.w.values():
                if (not is_dma) and tok[0] == "e" and tok[1] == ename and ename == "pe":
                    continue
                self.wait(ename, tok)
        for b in writes:
            for tok in list(b.w.values()) + list(b.r.values()):
                if (not is_dma) and tok[0] == "e" and tok[1] == ename:
                    continue
                self.wait(ename, tok)

    @staticmethod
    def _key(tok):
        return ("e", tok[1]) if tok[0] == "e" else ("d", id(tok[1]))

    def op(self, ename, fn, reads=(), writes=()):
        self._deps(ename, reads, writes, False)
        ins = fn(self.eng[ename])
        self.n[ename] += 1
        idx = self.n[ename]
        epoch = (idx - 1) // EPOCH
        ins.then_inc(self._esem(ename, epoch), 1)
        tok = ("e", ename, idx)
        k = self._key(tok)
        for b in writes:
            b.w[k] = tok
        for b in reads:
            b.r[k] = tok
        return tok

    def dma(self, qname, out_ap, in_ap, sb, reads=(), writes=(), **kw):
        self._deps(qname, reads, writes, True)
        ds = self._get_ds(sb)
        ins = self.eng[qname].dma_start(out=out_ap, in_=in_ap, **kw)
        ds.cnt += 16
        ins.then_inc(ds.h, 16)
        tok = ("d", ds, ds.cnt)
        k = self._key(tok)
        for b in writes:
            b.w[k] = tok
        for b in reads:
            b.r[k] = tok
        return tok

    def barrier(self):
        toks = [("e", e, self.n[e]) for e in self.eng if self.n[e] > 0]
        toks += [("d", ds, ds.cnt) for ds in self.all_ds if ds.cnt > 0]
        for e in self.eng:
            for t in toks:
                if t[0] == "e" and t[1] == e:
                    continue
                self.wait(e, t)


def mm(kb, out, lhsT, rhs, start, stop, reads, writes):
    return kb.op("pe", lambda e: e.matmul(out, lhsT=lhsT, rhs=rhs, start=start, stop=stop),
                 reads=reads, writes=writes)


def act(kb, out, in_, func, reads, writes, **kw):
    return kb.op("act", lambda e: e.activation(out=out, in_=in_, func=func, **kw),
                 reads=reads, writes=writes)


def tt(kb, eng, out, in0, in1, op, reads, writes):
    return kb.op(eng, lambda e: e.tensor_tensor(out=out, in0=in0, in1=in1, op=op),
                 reads=reads, writes=writes)


class C:
    IDENT, LT, GE, LE, ONES, NEGU, NEGONES = range(7)


def cbs(cb, i, n=1):
    return cb[:, i * 128:(i + n) * 128]


def phase_inproj(kb, cb, xsrc, norm, w_in, fm_chunks, tm_pieces, cs128, cs64):
    kb.begin_phase()
    TG = 2048
    NTT = TG // 128
    hT = kb.sbuf("hT", [128, 16, TG], BF16)
    gsb = kb.sbuf("gsb", [128, 16], F32)
    xt = [kb.sbuf(f"xt{i}", [128, D], F32) for i in range(2)]
    xn = [kb.sbuf(f"xn{i}", [128, D], BF16) for i in range(2)]
    sq = kb.sbuf("sq", [128, D], BF16)
    stt = [kb.sbuf(f"stt{i}", [128, 4], F32) for i in range(2)]
    tp = [kb.psum(f"tp{i}", [128, 8, 128], BF16) for i in range(2)]
    pA = [kb.psum(f"pA{i}", [128, 512], F32) for i in range(2)]
    pB = [kb.psum(f"pB{i}", [128, 512], F32) for i in range(2)]
    pT = [kb.psum(f"pT{i}", [128, 512], F32) for i in range(2)]
    wA = [kb.sbuf(f"wA{i}", [128, 16, 128], BF16) for i in range(2)]
    wB = [kb.sbuf(f"wB{i}", [128, 16, 128], BF16) for i in range(2)]
    wT = [kb.sbuf(f"wT{i}", [128, 16, 512], BF16) for i in range(2)]
    t1 = [kb.sbuf(f"t1_{i}", [128, 512], F32) for i in range(2)]
    t2 = [kb.sbuf(f"t2_{i}", [128, 512], F32) for i in range(2)]
    ob = [kb.sbuf(f"ob{i}", [128, 512], BF16) for i in range(3)]
    o32 = [kb.sbuf(f"o32_{i}", [128, 512], F32) for i in range(2)]
    otm = [kb.sbuf(f"otm{i}", [128, 512], BF16) for i in range(2)]
    o16 = [kb.sbuf(f"o16_{i}", [128, 16], F32) for i in range(2)]
    csA = kb.sbuf("csA", [128, 2, TG], F32)
    csB = kb.sbuf("csB", [128, 2, TG], F32) if cs64 is not None else None

    kb.dma("sp", gsb[:], norm.ap.rearrange("(k p) -> p k", p=128), gsb, reads=[norm],
           writes=[gsb], allow_slow_non_contiguous=True)
    ident = cbs(cb, C.IDENT)
    wsrc = w_in.ap.rearrange("(k p) n -> p k n", p=128)

    def load_w(dst, segs):
        for (do, so, n) in segs:
            for k0 in range(0, 16, 8):
                kb.dma("pool", dst[:, k0:k0 + 8, do:do + n], wsrc[:, k0:k0 + 8, so:so + n], dst,
                       reads=[w_in], writes=[dst])

    cnt = {"a": 0, "o": 0, "t": 0, "o32": 0, "tm": 0, "otm": 0}
    for G in range(S // TG):
        g0 = G * TG
        if cs128 is not None:
            kb.dma("sp", csA[:], cs128.ap[:, :, g0:g0 + TG].rearrange("c p t -> p c t"), csA,
                   reads=[cs128], writes=[csA])
        if cs64 is not None:
            kb.dma("sp", csB[:], cs64.ap[:, :, g0:g0 + TG].rearrange("c p t -> p c t"), csB,
                   reads=[cs64], writes=[csB])
        for i in range(NTT):
            b = i % 2
            r0 = g0 + i * 128
            kb.dma("sp", xt[b][:], xsrc.ap[r0:r0 + 128, :], xt[b], reads=[xsrc], writes=[xt[b]])
            act(kb, sq[:], xt[b][:], AF.Square, [xt[b]], [sq, stt[b]], accum_out=stt[b][:, 0:1])
            kb.op("dve", lambda e: e.tensor_scalar(out=stt[b][:, 1:2], in0=stt[b][:, 0:1],
                                                   scalar1=1.0 / D, scalar2=EPS, op0=ALU.mult,
                                                   op1=ALU.add), reads=[stt[b]], writes=[stt[b]])
            act(kb, stt[b][:, 2:3], stt[b][:, 1:2], AF.Ln, [stt[b]], [stt[b]])
            act(kb, stt[b][:, 3:4], stt[b][:, 2:3], AF.Exp, [stt[b]], [stt[b]], scale=-0.5)
            kb.op("dve", lambda e: e.tensor_scalar(out=xn[b][:], in0=xt[b][:],
                                                   scalar1=stt[b][:, 3:4], scalar2=None,
                                                   op0=ALU.mult), reads=[xt[b], stt[b]],
                  writes=[xn[b]])
            for half in range(2):
                for k in range(8):
                    kk = half * 8 + k
                    kb.op("pe", lambda e: e.transpose(out=tp[half][:, k, :],
                                                      in_=xn[b][:, kk * 128:(kk + 1) * 128],
                                                      identity=ident),
                          reads=[xn[b], cb], writes=[tp[half]])
                tt(kb, "dve", hT[:, half * 8:(half + 1) * 8, i * 128:(i + 1) * 128],
                   tp[half][:, :, :],
                   gsb[:, half * 8:(half + 1) * 8].unsqueeze(2).broadcast_to([128, 8, 128]),
                   ALU.mult, [tp[half], gsb], [hT])
        nfm = len(fm_chunks)
        if nfm:
            load_w(wA[0], fm_chunks[0]["segA"])
            if fm_chunks[0]["segB"]:
                load_w(wB[0], fm_chunks[0]["segB"])
        for ci, ch in enumerate(fm_chunks):
            wa, wb = wA[ci % 2], wB[ci % 2]
            if ci + 1 < nfm:
                nx = fm_chunks[ci + 1]
                load_w(wA[(ci + 1) % 2], nx["segA"])
                if nx["segB"]:
                    load_w(wB[(ci + 1) % 2], nx["segB"])
            nco = ch["ncols"]
            for sub in range(TG // 512):
                tok0 = g0 + sub * 512
                sl = slice(sub * 512, (sub + 1) * 512)
                pa = pA[cnt["a"] % 2]
                pb = pB[cnt["a"] % 2]
                cnt["a"] += 1
                for k in range(16):
                    mm(kb, pa[0:nco, :], wa[:, k, 0:nco], hT[:, k, sl], k == 0, k == 15,
                       [wa, hT], [pa])
                if ch["segB"]:
                    for k in range(16):
                        mm(kb, pb[0:nco, :], wb[:, k, 0:nco], hT[:, k, sl], k == 0, k == 15,
                           [wb, hT], [pb])
                kind = ch["kind"]
                if kind == "copy32":
                    o = o32[cnt["o32"] % 2]
                    cnt["o32"] += 1
                    act(kb, o[0:nco, :], pa[0:nco, :], AF.Copy, [pa], [o])
                else:
                    o = ob[cnt["o"] % 3]
                    cnt["o"] += 1
                    if kind == "rope":
                        cst = csA if ch["cs"] == 128 else csB
                        a1, a2 = t1[cnt["t"] % 2], t2[cnt["t"] % 2]
                        cnt["t"] += 1
                        sc = float(ch["scale"])
                        kb.op("dve", lambda e: e.scalar_tensor_tensor(
                            out=a1[0:nco, :], in0=pa[0:nco, :], scalar=sc, in1=cst[0:nco, 0, sl],
                            op0=ALU.mult, op1=ALU.mult), reads=[pa, cst], writes=[a1])
                        kb.op("dve", lambda e: e.scalar_tensor_tensor(
                            out=a2[0:nco, :], in0=pb[0:nco, :], scalar=sc, in1=cst[0:nco, 1, sl],
                            op0=ALU.mult, op1=ALU.mult), reads=[pb, cst], writes=[a2])
                        tt(kb, "pool", o[0:nco, :], a1[0:nco, :], a2[0:nco, :], ALU.add,
                           [a1, a2], [o])
                    elif kind == "copy":
                        act(kb, o[0:nco, :], pa[0:nco, :], AF.Copy, [pa], [o],
                            scale=float(ch["scale"]))
                    elif kind == "silu":
                        act(kb, o[0:nco, :], pa[0:nco, :], AF.Silu, [pa], [o])
                dbuf, dap = ch["dest"](tok0)
                kb.dma("sp", dap, o[0:nco, :], o, reads=[o], writes=[dbuf])
        ntm = len(tm_pieces)
        if ntm:
            load_w(wT[0], tm_pieces[0]["seg"])
        for pi, pc in enumerate(tm_pieces):
            wt = wT[pi % 2]
            if pi + 1 < ntm:
                load_w(wT[(pi + 1) % 2], tm_pieces[pi + 1]["seg"])
            nco = pc["ncols"]
            for i in range(NTT):
                r0 = g0 + i * 128
                pt_ = pT[cnt["tm"] % 2]
                cnt["tm"] += 1
                for k in range(16):
                    mm(kb, pt_[:, 0:nco], hT[:, k, i * 128:(i + 1) * 128], wt[:, k, 0:nco],
                       k == 0, k == 15, [hT, wt], [pt_])
                for (off, n, dfn, dt_) in pc["dests"]:
                    if dt_ == "f32":
                        o = o16[cnt["otm"] % 2]
                    else:
                        o = otm[cnt["otm"] % 2]
                    cnt["otm"] += 1
                    act(kb, o[:, 0:n], pt_[:, off:off + n], AF.Copy, [pt_], [o])
                    dbuf, dap = dfn(r0)
                    kb.dma("sp", dap, o[:, 0:n], o, reads=[o], writes=[dbuf])
    kb.end_phase()


def attn_epilogue(kb, OT, DEN, gt, rden, ob, ygT, row0, t0, ncol=512):
    if DEN is not None:
        act(kb, rden[:, 0:ncol], DEN[:, 0:ncol], AF.Ln, [DEN], [rden])
        act(kb, rden[:, 0:ncol], rden[:, 0:ncol], AF.Exp, [rden], [rden], scale=-1.0)
        tt(kb, "pool", rden[:, 0:ncol], rden[:, 0:ncol], gt[:, 0:ncol], ALU.mult,
           [rden, gt], [rden])
        tt(kb, "dve", ob[:, 0:ncol], OT[:, 0:ncol], rden[:, 0:ncol], ALU.mult, [OT, rden], [ob])
    else:
        tt(kb, "dve", ob[:, 0:ncol], OT[:, 0:ncol], gt[:, 0:ncol], ALU.mult, [OT, gt], [ob])
    kb.dma("sp", ygT.ap[row0:row0 + 128, t0:t0 + ncol], ob[:, 0:ncol], ob, reads=[ob],
           writes=[ygT])


NIT = 22


def phase_dsa(kb, cb, qaT, kaT, va, iqT, ikT, iw, gT, ygT):
    kb.begin_phase()
    KT = kb.sbuf("KT", [128, S], BF16)
    VA = kb.sbuf("VA", [128, NB, 128], BF16)
    IK = kb.sbuf("IK", [128, S], BF16)
    iqg = [kb.sbuf(f"iqg{i}", [128, 8, 512], BF16) for i in range(2)]
    iwg = [kb.sbuf(f"iwg{i}", [128, 4, 16], F32) for i in range(2)]
    score = [kb.sbuf(f"score{i}", [128, S], F32) for i in range(2)]
    junk = kb.sbuf("junk", [128, S], BF16)
    rl = [kb.sbuf(f"rl{i}", [128, 512], F32) for i in range(3)]
    bs = [kb.sbuf(f"bs{i}", [128, 4], F32) for i in range(2)]
    wt = [kb.sbuf(f"wt{i}", [128, NIT], F32) for i in range(2)]
    md = [kb.sbuf(f"md{i}", [128, NIT + 2], F32) for i in range(2)]
    cn = [kb.sbuf(f"cn{i}", [128, NIT], F32) for i in range(2)]
    tq = [kb.sbuf(f"tq{i}", [128, NIT], F32) for i in range(2)]
    pw = kb.sbuf("pw", [128, NIT], F32)
    mbf = [kb.sbuf(f"mbf{i}", [128, S], BF16) for i in range(2)]
    maskT = [kb.sbuf(f"maskT{i}", [128, NB, 512], BF16) for i in range(2)]
    QT = [kb.sbuf(f"QT{i}", [128, 512], BF16) for i in range(2)]
    PT = [kb.sbuf(f"PT{i}", [128, 512], BF16) for i in range(3)]
    PM = [kb.sbuf(f"PM{i}", [128, 512], BF16) for i in range(4)]
    gt = [kb.sbuf(f"gt{i}", [128, 512], BF16) for i in range(2)]
    rden = [kb.sbuf(f"rden{i}", [128, 512], F32) for i in range(2)]
    ob = [kb.sbuf(f"obd{i}", [128, 512], BF16) for i in range(2)]
    pd = [kb.psum(f"pd{i}", [128, 512], F32) for i in range(2)]
    ptp = kb.psum("ptp", [128, 8, 128], BF16)
    pS = [kb.psum(f"pS{i}", [128, 512], F32) for i in range(3)]
    OT = kb.psum("OT", [128, 512], F32)
    DEN = kb.psum("DEN", [128, 512], F32)
    ident = cbs(cb, C.IDENT)
    ones = cbs(cb, C.ONES)

    kb.dma("sp", KT[:], kaT.ap[:, :], KT, reads=[kaT], writes=[KT])
    kb.dma("sp", IK[:], ikT.ap[:, :], IK, reads=[ikT], writes=[IK])
    vsrc = va.ap.rearrange("(b p) c -> p b c", p=128)
    for b0 in range(0, NB, 8):
        kb.dma("sp", VA[:, b0:b0 + 8, :], vsrc[:, b0:b0 + 8, :], VA, reads=[va], writes=[VA])
    for i in range(NIT):
        kb.op("dve", lambda e: e.memset(pw[:, i:i + 1], (0.5 ** (i + 1)) * 1.000001), writes=[pw])

    cnt = {"pd": 0, "q": 0, "ob": 0, "jb": 0}

    def idx_tasks(g):
        tasks = []
        t0 = g * 512
        iq_, iw_ = iqg[g % 2], iwg[g % 2]
        mT = maskT[g % 2]

        def loads():
            kb.dma("sp", iq_[:], iqT.ap[:, :, t0:t0 + 512].rearrange("c p t -> p c t"), iq_,
                   reads=[iqT], writes=[iq_])
            kb.dma("sp", iw_[:], iw.ap[t0:t0 + 512, :].rearrange("(j p) h -> p j h", p=128), iw_,
                   reads=[iw], writes=[iw_])
        tasks.append(loads)
        for j in range(4):
            qb = 4 * g + j
            n = (qb + 1) * 128
            jb = (4 * g + j) % 2
            sc, mb = score[jb], mbf[jb]
            b_, w_, m_, c_, t_ = bs[jb], wt[jb], md[jb], cn[jb], tq[jb]
            nch = (n + 511) // 512
            for chn in range(nch):
                w = min(512, n - chn * 512)
                cs = slice(chn * 512, chn * 512 + w)
                for h in range(16):
                    def idx_unit(h=h, w=w, cs=cs, j=j, sc=sc):
                        c, half = h // 2, h % 2
                        ps_ = slice(half * 64, (half + 1) * 64)
                        p = pd[cnt["pd"] % 2]
                        r = rl[cnt["pd"] % 3]
                        cnt["pd"] += 1
                        mm(kb, p[:, 0:w], iq_[ps_, c, j * 128:(j + 1) * 128], IK[ps_, cs], True,
                           True, [iq_, IK], [p])
                        act(kb, r[:, 0:w], p[:, 0:w], AF.Relu, [p], [r])
                        if h == 0:
                            kb.op("dve", lambda e: e.tensor_scalar(
                                out=sc[:, cs], in0=r[:, 0:w], scalar1=iw_[:, j, 0:1], scalar2=None,
                                op0=ALU.mult), reads=[r, iw_], writes=[sc])
                        else:
                            kb.op("dve", lambda e: e.scalar_tensor_tensor(
                                out=sc[:, cs], in0=r[:, 0:w], scalar=iw_[:, j, h:h + 1],
                                in1=sc[:, cs], op0=ALU.mult, op1=ALU.add), reads=[r, iw_, sc],
                                writes=[sc])
                    tasks.append(idx_unit)

            def causal(qb=qb, sc=sc):
                dsl = slice(qb * 128, (qb + 1) * 128)
                kb.op("pool", lambda e: e.affine_select(
                    out=sc[:, dsl], in_=sc[:, dsl], pattern=[[-1, 128]], compare_op=ALU.is_ge,
                    fill=NEG, base=0, channel_multiplier=1), reads=[sc], writes=[sc])
            tasks.append(causal)
            if qb >= 2:
                def bracket(n=n, sc=sc, b_=b_, w_=w_, m_=m_, c_=c_):
                    kb.op("dve", lambda e: e.tensor_reduce(out=b_[:, 0:1], in_=sc[:, 0:n],
                                                           axis=mybir.AxisListType.X, op=ALU.max),
                          reads=[sc], writes=[b_])
                    kb.op("dve", lambda e: e.tensor_reduce(out=b_[:, 1:2], in_=sc[:, 0:n - 128],
                                                           axis=mybir.AxisListType.X, op=ALU.min),
                          reads=[sc], writes=[b_])
                    tt(kb, "dve", b_[:, 2:3], b_[:, 0:1], b_[:, 1:2], ALU.subtract, [b_], [b_])
                    kb.op("dve", lambda e: e.tensor_scalar(out=w_[:, :], in0=pw[:, :],
                                                           scalar1=b_[:, 2:3], scalar2=None,
                                                           op0=ALU.mult), reads=[pw, b_], writes=[w_])
                    kb.op("dve", lambda e: e.memset(c_[:, :], 0.0), writes=[c_])
                    tt(kb, "dve", m_[:, 0:1], b_[:, 1:2], w_[:, 0:1], ALU.add, [b_, w_], [m_])
                tasks.append(bracket)
                for it in range(NIT):
                    def bis(it=it, n=n, sc=sc, w_=w_, m_=m_, c_=c_, t_=t_):
                        kb.op("dve", lambda e: e.tensor_scalar(
                            out=junk[:, 0:n], in0=sc[:, 0:n], scalar1=m_[:, it:it + 1], scalar2=0.0,
                            op0=ALU.is_ge, op1=ALU.add, accum_out=c_[:, it:it + 1]),
                            reads=[sc, m_], writes=[junk, c_])
                        kb.op("dve", lambda e: e.tensor_scalar(
                            out=t_[:, it:it + 1], in0=c_[:, it:it + 1], scalar1=255.5, scalar2=0.5,
                            op0=ALU.is_ge, op1=ALU.subtract), reads=[c_], writes=[t_])
                        kb.op("dve", lambda e: e.scalar_tensor_tensor(
                            out=m_[:, it + 1:it + 2], in0=t_[:, it:it + 1], scalar=w_[:, it:it + 1],
                            in1=m_[:, it:it + 1], op0=ALU.mult, op1=ALU.add), reads=[t_, w_, m_],
                            writes=[m_])
                    tasks.append(bis)

                def mk(n=n, sc=sc, mb=mb, w_=w_, m_=m_):
                    kb.op("dve", lambda e: e.scalar_tensor_tensor(
                        out=m_[:, NIT + 1:NIT + 2], in0=w_[:, NIT - 1:NIT], scalar=-0.5,
                        in1=m_[:, NIT:NIT + 1], op0=ALU.mult, op1=ALU.add), reads=[w_, m_],
                        writes=[m_])
                    kb.op("dve", lambda e: e.tensor_scalar(
                        out=mb[:, 0:n], in0=sc[:, 0:n], scalar1=m_[:, NIT + 1:NIT + 2],
                        scalar2=None, op0=ALU.is_ge), reads=[sc, m_], writes=[mb])
                tasks.append(mk)
            else:
                def mk(n=n, sc=sc, mb=mb):
                    kb.op("dve", lambda e: e.tensor_scalar(
                        out=mb[:, 0:n], in0=sc[:, 0:n], scalar1=-1.0e30, scalar2=None,
                        op0=ALU.is_ge), reads=[sc], writes=[mb])
                tasks.append(mk)
            for k0 in range(0, qb + 1, 8):
                def tr(k0=k0, qb=qb, mb=mb, j=j):
                    nk = min(8, qb + 1 - k0)
                    for kk in range(nk):
                        kb.op("pe", lambda e: e.transpose(
                            out=ptp[:, kk, :], in_=mb[:, (k0 + kk) * 128:(k0 + kk + 1) * 128],
                            identity=ident), reads=[mb, cb], writes=[ptp])
                    act(kb, mT[:, k0:k0 + nk, j * 128:(j + 1) * 128], ptp[:, 0:nk, :], AF.Copy,
                        [ptp], [mT])
                tasks.append(tr)
        return tasks

    for t in idx_tasks(0):
        t()
    for g in range(8):
        t0 = g * 512
        mT = maskT[g % 2]
        nxt = idx_tasks(g + 1) if g + 1 < 8 else []
        nkb = 4 * (g + 1)
        total_units = 10 * nkb
        done_tasks = 0
        unit_i = 0
        for h in range(10):
            q = QT[cnt["q"] % 2]
            gg = gt[cnt["q"] % 2]
            rd = rden[cnt["q"] % 2]
            cnt["q"] += 1
            kb.dma("sp", q[:], qaT.ap[h, :, t0:t0 + 512], q, reads=[qaT], writes=[q])
            kb.dma("sp", gg[:], gT.ap[h * 128:(h + 1) * 128, t0:t0 + 512], gg, reads=[gT],
                   writes=[gg])

            def qk(kbi):
                c0 = max(0, kbi - 4 * g) * 128
                p = pS[kbi % 3]
                mm(kb, p[:, c0:512], KT[:, kbi * 128:(kbi + 1) * 128], q[:, c0:512], True, True,
                   [KT, q], [p])

            def pv(kbi):
                c0 = max(0, kbi - 4 * g) * 128
                pm = PM[kbi % 4]
                mm(kb, OT[:, c0:512], VA[:, kbi, :], pm[:, c0:512], kbi == 0, kbi == nkb - 1,
                   [VA, pm], [OT])
                mm(kb, DEN[:, c0:512], ones, pm[:, c0:512], kbi == 0, kbi == nkb - 1,
                   [cb, pm], [DEN])

            qk(0)
            if nkb > 1:
                qk(1)
            for kbi in range(nkb):
                if kbi + 2 < nkb:
                    qk(kbi + 2)
                c0 = max(0, kbi - 4 * g) * 128
                p = pS[kbi % 3]
                pt_ = PT[kbi % 3]
                pm = PM[kbi % 4]
                act(kb, pt_[:, c0:512], p[:, c0:512], AF.Exp, [p], [pt_])
                tt(kb, "pool", pm[:, c0:512], pt_[:, c0:512], mT[:, kbi, c0:512], ALU.mult,
                   [pt_, mT], [pm])
                if kbi >= 1:
                    pv(kbi - 1)
                unit_i += 1
                target = (len(nxt) * unit_i) // total_units
                while done_tasks < target:
                    nxt[done_tasks]()
                    done_tasks += 1
            pv(nkb - 1)
            o = ob[cnt["ob"] % 2]
            cnt["ob"] += 1
            attn_epilogue(kb, OT, DEN, gg, rd, o, ygT, h * 128, t0)
        while done_tasks < len(nxt):
            nxt[done_tasks]()
            done_tasks += 1
    kb.end_phase()


def phase_dilated(kb, cb, qbT, kbT, vb, gT, ygT):
    kb.begin_phase()
    QT = [kb.sbuf(f"dQ{i}", [128, S], BF16) for i in range(2)]
    KT = [kb.sbuf(f"dK{i}", [128, S], BF16) for i in range(2)]
    VC = [kb.sbuf(f"dV{i}", [128, NB, 128], BF16) for i in range(2)]
    acc = kb.sbuf("acc", [128, S], F32)
    dacc = kb.sbuf("dacc", [128, S], F32)
    gt = kb.sbuf("dgt", [128, S], BF16)
    ob = kb.sbuf("dob", [128, S], BF16)
    PT = [kb.sbuf(f"dPT{i}", [128, 2, 128], BF16) for i in range(3)]
    PM = [kb.sbuf(f"dPM{i}", [128, 2, 128], BF16) for i in range(3)]
    pS = [kb.psum(f"dpS{i}", [128, 512], F32) for i in range(2)]
    pO = [kb.psum(f"dpO{i}", [128, 512], F32) for i in range(2)]
    pD = [kb.psum(f"dpD{i}", [128, 512], F32) for i in range(2)]
    ones = cbs(cb, C.ONES)
    band = cb[:, C.GE * 128:(C.GE + 2) * 128].rearrange("p (a c) -> p a c", a=2)
    u = 0
    vi = 0
    for h in range(6):
        q, k = QT[h % 2], KT[h % 2]
        kb.dma("sp", q[:], qbT.ap[h, :, :], q, reads=[qbT], writes=[q])
        kb.dma("sp", k[:], kbT.ap[h, :, :], k, reads=[kbT], writes=[k])
        kb.dma("sp", gt[:], gT.ap[1280 + h * 128:1280 + (h + 1) * 128, :], gt, reads=[gT],
               writes=[gt])
        for pi, dil in enumerate((1, 4, 16)):
            nb = NB // dil
            v = VC[vi % 2]
            vi += 1
            vsrc = vb.ap[:, h * 128:(h + 1) * 128].rearrange("(n i d) c -> i d n c", d=dil, i=128)
            vdst = v[:].rearrange("p (d n) c -> p d n c", d=dil)
            for r in range(dil):
                for n0 in range(0, nb, 8):
                    n1 = min(nb, n0 + 8)
                    kb.dma("sp", vdst[:, r, n0:n1, :], vsrc[:, r, n0:n1, :], v, reads=[vb],
                           writes=[v])
            units = [(r, n) for r in range(dil) for n in range(nb)]

            def cols(r, m):
                s0_ = m * 128 * dil + r
                return slice(s0_, s0_ + 127 * dil + 1, dil)

            def qk(ui, r, n):
                st = pS[ui % 2]
                s0 = 0 if n > 0 else 1
                stv = st[:, 0:256].rearrange("p (a c) -> p a c", a=2)
                for sl_ in range(s0, 2):
                    m = n - 1 + sl_
                    mm(kb, stv[:, sl_, :], k[:, cols(r, m)], q[:, cols(r, n)], True, True, [k, q],
                       [st])

            qk(u, *units[0])
            for li, (r, n) in enumerate(units):
                if li + 1 < len(units):
                    qk(u + 1, *units[li + 1])
                st = pS[u % 2]
                po = pO[u % 2]
                pdn = pD[u % 2]
                pt_ = PT[u % 3]
                pm = PM[u % 3]
                u += 1
                s0 = 0 if n > 0 else 1
                stv = st[:, 0:256].rearrange("p (a c) -> p a c", a=2)
                act(kb, pt_[:, s0:2, :], stv[:, s0:2, :], AF.Exp, [st], [pt_])
                tt(kb, "pool", pm[:, s0:2, :], pt_[:, s0:2, :], band[:, s0:2, :], ALU.mult,
                   [pt_, cb], [pm])
                for sl_ in range(s0, 2):
                    m = n - 1 + sl_
                    mm(kb, po[:, 0:128], v[:, r * nb + m, :], pm[:, sl_, :], sl_ == s0, sl_ == 1,
                       [v, pm], [po])
                for sl_ in range(s0, 2):
                    mm(kb, pdn[:, 0:128], ones, pm[:, sl_, :], sl_ == s0, sl_ == 1, [cb, pm],
                       [pdn])
                cs = cols(r, n)
                if pi == 0:
                    kb.op("dve", lambda e: e.tensor_copy(out=acc[:, cs], in_=po[:, 0:128]),
                          reads=[po], writes=[acc])
                    kb.op("dve", lambda e: e.tensor_copy(out=dacc[:, cs], in_=pdn[:, 0:128]),
                          reads=[pdn], writes=[dacc])
                else:
                    tt(kb, "dve", acc[:, cs], po[:, 0:128], acc[:, cs], ALU.add, [po, acc], [acc])
                    tt(kb, "dve", dacc[:, cs], pdn[:, 0:128], dacc[:, cs], ALU.add,
                       [pdn, dacc], [dacc])
        act(kb, dacc[:], dacc[:], AF.Ln, [dacc], [dacc])
        act(kb, dacc[:], dacc[:], AF.Exp, [dacc], [dacc], scale=-1.0)
        tt(kb, "dve", acc[:], acc[:], dacc[:], ALU.mult, [acc, dacc], [acc])
        tt(kb, "dve", ob[:], acc[:], gt[:], ALU.mult, [acc, gt], [ob])
        kb.dma("sp", ygT.ap[1280 + h * 128:1280 + (h + 1) * 128, :], ob[:], ob, reads=[ob],
               writes=[ygT])
    kb.end_phase()


def phase_sb(kb, cb, qcT, kcT, vc, gT, ygT):
    kb.begin_phase()
    KT = [kb.sbuf(f"sK{i}", [128, S], BF16) for i in range(2)]
    VV = [kb.sbuf(f"sV{i}", [128, NB, 128], BF16) for i in range(2)]
    QT = [kb.sbuf(f"sQ{i}", [128, 512], BF16) for i in range(2)]
    gt = [kb.sbuf(f"sg{i}", [128, 512], BF16) for i in range(2)]
    E = [kb.sbuf(f"sE{i}", [128, 512], F32) for i in range(2)]
    LP = [kb.sbuf(f"sL{i}", [128, 512], BF16) for i in range(3)]
    RS = kb.sbuf("sRS", [128, 512], BF16)
    PT = [kb.sbuf(f"sP{i}", [128, 512], BF16) for i in range(4)]
    ob = [kb.sbuf(f"sob{i}", [128, 512], BF16) for i in range(2)]
    pZ = [kb.psum(f"pZ{i}", [128, 512], F32) for i in range(3)]
    pW = [kb.psum(f"pW{i}", [128, 512], F32) for i in range(2)]
    OT = [kb.psum(f"sOT{i}", [128, 512], F32) for i in range(2)]
    negU = cbs(cb, C.NEGU)
    negones = cbs(cb, C.NEGONES)
    lt = cbs(cb, C.LT)
    u = 0
    for h in range(8):
        k, v = KT[h % 2], VV[h % 2]
        kb.dma("sp", k[:], kcT.ap[h, :, :], k, reads=[kcT], writes=[k])
        vsrc = vc.ap[:, h * 128:(h + 1) * 128].rearrange("(b p) c -> p b c", p=128)
        for b0 in range(0, NB, 8):
            kb.dma("sp", v[:, b0:b0 + 8, :], vsrc[:, b0:b0 + 8, :], v, reads=[vc], writes=[v])
        for g in range(8):
            t0 = g * 512
            q = QT[u % 2]
            gg = gt[u % 2]
            ot = OT[u % 2]
            o = ob[u % 2]
            u += 1
            kb.dma("sp", q[:], qcT.ap[h, :, t0:t0 + 512], q, reads=[qcT], writes=[q])
            kb.dma("sp", gg[:], gT.ap[h * 128:(h + 1) * 128, t0:t0 + 512], gg, reads=[gT],
                   writes=[gg])
            nkb = 4 * (g + 1)
            order = list(range(nkb - 1, -1, -1))

            def zmm(kbi, idx):
                c0 = max(0, kbi - 4 * g) * 128
                mm(kb, pZ[idx % 3][:, c0:512], k[:, kbi * 128:(kbi + 1) * 128], q[:, c0:512],
                   True, True, [k, q], [pZ[idx % 3]])

            def c0_of(kbi):
                return max(0, kbi - 4 * g) * 128

            def st_el(idx):
                kbi = order[idx]
                c0 = c0_of(kbi)
                z, e_, lp = pZ[idx % 3], E[idx % 2], LP[idx % 3]
                act(kb, e_[:, c0:512], z[:, c0:512], AF.Exp, [z], [e_])
                act(kb, lp[:, c0:512], e_[:, c0:512], AF.Ln, [e_], [lp], bias=1.0)
                if kbi >= 4 * g:
                    tt(kb, "pool", lp[:, c0:c0 + 128], lp[:, c0:c0 + 128], lt, ALU.mult,
                       [lp, cb], [lp])

            def st_w(idx):
                kbi = order[idx]
                c0 = c0_of(kbi)
                w, lp = pW[idx % 2], LP[idx % 3]
                mm(kb, w[:, c0:512], k[:, kbi * 128:(kbi + 1) * 128], q[:, c0:512], True, False,
                   [k, q], [w])
                mm(kb, w[:, c0:512], negU, lp[:, c0:512], False, idx == 0, [cb, lp], [w])
                if idx > 0:
                    mm(kb, w[:, c0:512], negones, RS[:, c0:512], False, True, [cb, RS], [w])
                if idx == 0:
                    kb.op("dve", lambda e: e.memset(RS[:], 0.0), writes=[RS])
                if idx + 1 < nkb:
                    tt(kb, "dve", RS[:, c0:512], RS[:, c0:512], lp[:, c0:512], ALU.add, [RS, lp],
                       [RS])

            def st_p(idx):
                kbi = order[idx]
                c0 = c0_of(kbi)
                w, pt_ = pW[idx % 2], PT[idx % 4]
                act(kb, pt_[:, c0:512], w[:, c0:512], AF.Exp, [w], [pt_])
                if kbi >= 4 * g:
                    tt(kb, "pool", pt_[:, c0:c0 + 128], pt_[:, c0:c0 + 128], lt, ALU.mult,
                       [pt_, cb], [pt_])

            def st_pv(idx):
                kbi = order[idx]
                c0 = c0_of(kbi)
                pt_ = PT[idx % 4]
                mm(kb, ot[:, c0:512], v[:, kbi, :], pt_[:, c0:512], idx == 0, idx == nkb - 1,
                   [v, pt_], [ot])

            zmm(order[0], 0)
            if nkb > 1:
                zmm(order[1], 1)
            for i in range(nkb + 3):
                if i < nkb:
                    st_el(i)
                if i + 2 < nkb:
                    zmm(order[i + 2], i + 2)
                if 0 <= i - 1 < nkb:
                    st_w(i - 1)
                if 0 <= i - 2 < nkb:
                    st_p(i - 2)
                if 0 <= i - 3 < nkb:
                    st_pv(i - 3)
            attn_epilogue(kb, ot, None, gg, None, o, ygT, h * 128, t0)
    kb.end_phase()


def phase_fox_prep(kb, flT, b_f, fA, fB):
    kb.begin_phase()
    fl = kb.sbuf("fl", [8, S], F32)
    bf = kb.sbuf("bf", [8, 2], F32)
    e_ = kb.sbuf("fe", [8, S], F32)
    ones = kb.sbuf("fones", [8, S], F32)
    cc = kb.sbuf("fcc", [8, S], F32)
    rr = kb.sbuf("frr", [8, S], F32)
    A = kb.sbuf("fAs", [8, 6, S], BF16)
    B = kb.sbuf("fBs", [8, 6, S], BF16)
    kb.dma("sp", fl[:], flT.ap[:, :], fl, reads=[flT], writes=[fl])
    kb.dma("sp", bf[:, 0:1], b_f.ap.rearrange("(h o) -> h o", o=1), bf, reads=[b_f], writes=[bf])
    kb.op("dve", lambda e: e.tensor_scalar(out=bf[:, 1:2], in0=bf[:, 0:1], scalar1=-1.0,
                                           scalar2=None, op0=ALU.mult), reads=[bf], writes=[bf])
    act(kb, e_[:], fl[:], AF.Exp, [fl, bf], [e_], scale=-1.0, bias=bf[:, 1:2])
    act(kb, e_[:], e_[:], AF.Ln, [e_], [e_], bias=1.0)
    kb.op("dve", lambda e: e.memset(ones[:], 1.0), writes=[ones])
    kb.op("dve", lambda e: e.tensor_tensor_scan(out=cc[:], data0=ones[:], data1=e_[:], initial=0.0,
                                                op0=ALU.mult, op1=ALU.add), reads=[ones, e_],
          writes=[cc])
    cur = cc
    for i in range(3):
        kb.op("dve", lambda e: e.tensor_copy(out=A[:, i, :], in_=cur[:]), reads=[cur], writes=[A])
        kb.op("dve", lambda e: e.tensor_scalar(out=B[:, 3 + i, :], in0=A[:, i, :], scalar1=-1.0,
                                               scalar2=None, op0=ALU.mult), reads=[A], writes=[B])
        if i < 2:
            nxt = rr if cur is cc else cc
            tt(kb, "dve", nxt[:], cur[:], A[:, i, :], ALU.subtract, [cur, A], [nxt])
            cur = nxt
    kb.op("dve", lambda e: e.memset(A[:, 3:6, :], 1.0), writes=[A])
    kb.op("dve", lambda e: e.memset(B[:, 0:3, :], 1.0), writes=[B])
    kb.dma("sp", fA.ap[:, :, :], A[:], A, reads=[A], writes=[fA])
    kb.dma("sp", fB.ap[:, :, :], B[:], B, reads=[B], writes=[fB])
    kb.end_phase()


def phase_fox(kb, cb, qdT, kdT, vd, fA, fB, gT, ygT):
    kb.begin_phase()
    KT = [kb.sbuf(f"fK{i}", [128, S], BF16) for i in range(2)]
    VV = [kb.sbuf(f"fV{i}", [128, NB, 128], BF16) for i in range(2)]
    AA = [kb.sbuf(f"fA{i}", [6, S], BF16) for i in range(2)]
    BB = [kb.sbuf(f"fB{i}", [6, S], BF16) for i in range(2)]
    QT = [kb.sbuf(f"fQ{i}", [128, 512], BF16) for i in range(2)]
    gt = [kb.sbuf(f"fg{i}", [128, 512], BF16) for i in range(2)]
    PT = [kb.sbuf(f"fP{i}", [128, 512], BF16) for i in range(4)]
    rden = kb.sbuf("frden", [128, 512], F32)
    ob = [kb.sbuf(f"fob{i}", [128, 512], BF16) for i in range(2)]
    pS = [kb.psum(f"fpS{i}", [128, 512], F32) for i in range(3)]
    OT = [kb.psum(f"fOT{i}", [128, 512], F32) for i in range(2)]
    DEN = [kb.psum(f"fDEN{i}", [128, 512], F32) for i in range(2)]
    ones = cbs(cb, C.ONES)
    le = cbs(cb, C.LE)
    u = 0
    for h in range(8):
        k, v, A, B = KT[h % 2], VV[h % 2], AA[h % 2], BB[h % 2]
        kb.dma("sp", k[:], kdT.ap[h, :, :], k, reads=[kdT], writes=[k])
        kb.dma("sp", A[:], fA.ap[h, :, :], A, reads=[fA], writes=[A])
        kb.dma("sp", B[:], fB.ap[h, :, :], B, reads=[fB], writes=[B])
        vsrc = vd.ap[:, h * 128:(h + 1) * 128].rearrange("(b p) c -> p b c", p=128)
        for b0 in range(0, NB, 8):
            kb.dma("sp", v[:, b0:b0 + 8, :], vsrc[:, b0:b0 + 8, :], v, reads=[vd], writes=[v])
        for g in range(8):
            t0 = g * 512
            q = QT[u % 2]
            gg = gt[u % 2]
            ot = OT[u % 2]
            dn = DEN[u % 2]
            o = ob[u % 2]
            u += 1
            kb.dma("sp", q[:], qdT.ap[h, :, t0:t0 + 512], q, reads=[qdT], writes=[q])
            kb.dma("sp", gg[:], gT.ap[1024 + h * 128:1024 + (h + 1) * 128, t0:t0 + 512], gg,
                   reads=[gT], writes=[gg])
            nkb = 4 * (g + 1)

            def qk(kbi):
                c0 = max(0, kbi - 4 * g) * 128
                p = pS[kbi % 3]
                mm(kb, p[:, c0:512], k[:, kbi * 128:(kbi + 1) * 128], q[:, c0:512], True, False,
                   [k, q], [p])
                mm(kb, p[:, c0:512], A[:, kbi * 128:(kbi + 1) * 128], B[:, t0 + c0:t0 + 512],
                   False, True, [A, B], [p])

            def pv(kbi):
                c0 = max(0, kbi - 4 * g) * 128
                pt_ = PT[kbi % 4]
                mm(kb, ot[:, c0:512], v[:, kbi, :], pt_[:, c0:512], kbi == 0, kbi == nkb - 1,
                   [v, pt_], [ot])
                mm(kb, dn[:, c0:512], ones, pt_[:, c0:512], kbi == 0, kbi == nkb - 1,
                   [cb, pt_], [dn])

            qk(0)
            if nkb > 1:
                qk(1)
            for kbi in range(nkb):
                if kbi + 2 < nkb:
                    qk(kbi + 2)
                c0 = max(0, kbi - 4 * g) * 128
                p = pS[kbi % 3]
                pt_ = PT[kbi % 4]
                act(kb, pt_[:, c0:512], p[:, c0:512], AF.Exp, [p], [pt_])
                if kbi >= 4 * g:
                    tt(kb, "pool", pt_[:, c0:c0 + 128], pt_[:, c0:c0 + 128], le, ALU.mult,
                       [pt_, cb], [pt_])
                if kbi >= 1:
                    pv(kbi - 1)
            pv(nkb - 1)
            attn_epilogue(kb, ot, dn, gg, rden, o, ygT, 1024 + h * 128, t0)
    kb.end_phase()


def phase_outproj(kb, ygT, w_out, xres, xdst, norm_f):
    kb.begin_phase()
    W = kb.sbuf("oW", [128, 16, D], BF16)
    yg = [kb.sbuf(f"oyg{i}", [128, 16, 512], BF16) for i in range(2)]
    xr = [kb.sbuf(f"oxr{i}", [128, D], F32) for i in range(2)]
    xw = [kb.sbuf(f"oxw{i}", [128, D], F32) for i in range(2)]
    pp = [kb.psum(f"opp{i}", [128, 512], F32) for i in range(4)]
    if norm_f is not None:
        nf = kb.sbuf("onf", [128, D], F32)
        sq = kb.sbuf("osq", [128, D], BF16)
        stt = [kb.sbuf(f"ost{i}", [128, 4], F32) for i in range(2)]
        kb.dma("sp", nf[:], norm_f.ap.partition_broadcast(128), nf, reads=[norm_f], writes=[nf])
    wsrc = w_out.ap.rearrange("(k p) n -> p k n", p=128)
    for k0 in range(0, 16, 4):
        kb.dma("pool", W[:, k0:k0 + 4, :], wsrc[:, k0:k0 + 4, :], W, reads=[w_out], writes=[W])
    ysrc = ygT.ap.rearrange("(k p) t -> p k t", p=128)
    ti = 0
    for g in range(8):
        y = yg[g % 2]
        for k0 in range(0, 16, 8):
            kb.dma("sp", y[:, k0:k0 + 8, :], ysrc[:, k0:k0 + 8, g * 512:(g + 1) * 512], y,
                   reads=[ygT], writes=[y])
        for j in range(4):
            r0 = g * 512 + j * 128
            xi, xo = xr[ti % 2], xw[ti % 2]
            kb.dma("sp", xi[:], xres.ap[r0:r0 + 128, :], xi, reads=[xres], writes=[xi])
            for n in range(4):
                p = pp[n]
                for k in range(16):
                    mm(kb, p[:, :], y[:, k, j * 128:(j + 1) * 128], W[:, k, n * 512:(n + 1) * 512],
                       k == 0, k == 15, [y, W], [p])
                tt(kb, "dve", xo[:, n * 512:(n + 1) * 512], p[:, :], xi[:, n * 512:(n + 1) * 512],
                   ALU.add, [p, xi], [xo])
            if norm_f is None:
                kb.dma("sp", xdst.ap[r0:r0 + 128, :], xo[:], xo, reads=[xo], writes=[xdst])
            else:
                st_ = stt[ti % 2]
                act(kb, sq[:], xo[:], AF.Square, [xo], [sq, st_], accum_out=st_[:, 0:1])
                kb.op("dve", lambda e: e.tensor_scalar(out=st_[:, 1:2], in0=st_[:, 0:1],
                                                       scalar1=1.0 / D, scalar2=EPS, op0=ALU.mult,
                                                       op1=ALU.add), reads=[st_], writes=[st_])
                act(kb, st_[:, 2:3], st_[:, 1:2], AF.Ln, [st_], [st_])
                act(kb, st_[:, 3:4], st_[:, 2:3], AF.Exp, [st_], [st_], scale=-0.5)
                kb.op("dve", lambda e: e.scalar_tensor_tensor(
                    out=xi[:], in0=xo[:], scalar=st_[:, 3:4], in1=nf[:], op0=ALU.mult,
                    op1=ALU.mult), reads=[xo, st_, nf], writes=[xi])
                kb.dma("sp", xdst.ap[r0:r0 + 128, :], xi[:], xi, reads=[xi], writes=[xdst])
            ti += 1
    kb.end_phase()


def rope_segs(c0, n, half):
    segs = []
    for hs in range(0, n, 2 * half):
        segs.append((hs, c0 + hs + half, half))
        segs.append((hs + half, c0 + hs, half))
    return segs


def build(phases=None, debug=False):
    nc = bass.Bass("TRN2", target_bir_lowering=False)
    kb = KB(nc)
    P = (lambda p: True) if phases is None else (lambda p: p in phases)

    def ein(name, shape, dt=F32):
        return kb.dram(name, shape, dt, kind="ExternalInput")

    def scr(name, shape, dt):
        return kb.dram(name, shape, dt, kind="ExternalOutput")

    x = ein("x", [S, D])
    norm0 = ein("norm0", [D])
    w_in0 = ein("w_in0", [D, IN0])
    w_out0 = ein("w_out0", [D, D])
    norm1 = ein("norm1", [D])
    w_in1 = ein("w_in1", [D, IN1])
    b_f1 = ein("b_f1", [8])
    w_out1 = ein("w_out1", [D, D])
    norm_f = ein("norm_f", [D])
    cs128 = ein("cs128", [2, 128, S])
    cs64 = ein("cs64", [2, 128, S])
    cbd = ein("cb", [128, 7 * 128], BF16)
    out = kb.dram("out", [S, D], F32, kind="ExternalOutput")

    qaT = scr("qaT", [10, 128, S], BF16)
    kaT = scr("kaT", [128, S], BF16)
    va = scr("va", [S, 128], BF16)
    iqT = scr("iqT", [8, 128, S], BF16)
    ikT = scr("ikT", [128, S], BF16)
    iw = scr("iw", [S, 16], F32)
    qbT = scr("qbT", [6, 128, S], BF16)
    kbT = scr("kbT", [6, 128, S], BF16)
    vb = scr("vb", [S, 768], BF16)
    g0T = scr("g0T", [D, S], BF16)
    yg0T = scr("yg0T", [D, S], BF16)
    x1 = scr("x1", [S, D], F32)
    qcT = scr("qcT", [8, 128, S], BF16)
    kcT = scr("kcT", [8, 128, S], BF16)
    vc = scr("vc", [S, 1024], BF16)
    qdT = scr("qdT", [8, 128, S], BF16)
    kdT = scr("kdT", [8, 128, S], BF16)
    vd = scr("vd", [S, 1024], BF16)
    flT = scr("flT", [8, S], F32)
    fA = scr("fA", [8, 6, S], BF16)
    fB = scr("fB", [8, 6, S], BF16)
    g1T = scr("g1T", [D, S], BF16)
    yg1T = scr("yg1T", [D, S], BF16)

    cb = kb.sbuf("cb_sb", [128, 7 * 128], BF16, glob=True)
    kb.dma("sp", cb[:], cbd.ap[:, :], cb, reads=[cbd], writes=[cb])

    def fm(c0, n, kind, dest, scale=1.0, cs=None, dup=False):
        half = 64 if cs == 128 else 32
        if dup:
            segA = [(0, c0, n), (n, c0, n)]
            segB = [(d, s, m) for (d, s, m) in rope_segs(c0, n, half)] + \
                   [(d + n, s, m) for (d, s, m) in rope_segs(c0, n, half)]
            n = 2 * n
        else:
            segA = [(0, c0, n)]
            segB = rope_segs(c0, n, half) if kind == "rope" else None
        return dict(segA=segA, segB=segB, ncols=n, kind=kind, scale=scale, dest=dest, cs=cs)

    def dest3(buf, h):
        return lambda tok0: (buf, buf.ap[h, :, tok0:tok0 + 512])

    def dest2(buf, r0, n=128):
        return lambda tok0: (buf, buf.ap[r0:r0 + n, tok0:tok0 + 512])

    def tdest(buf, c0, n):
        return lambda r0: (buf, buf.ap[r0:r0 + 128, c0:c0 + n])

    if P("in0"):
        ch = []
        for h in range(10):
            ch.append(fm(128 * h, 128, "rope", dest3(qaT, h), SC128, 128))
        ch.append(fm(1280, 128, "rope", dest2(kaT, 0), 1.0, 128))
        for c in range(8):
            ch.append(fm(1536 + 128 * c, 128, "rope", dest3(iqT, c), 1.0, 64))
        ch.append(fm(2560, 64, "rope", dest2(ikT, 0), 1.0, 64, dup=True))
        for h in range(6):
            ch.append(fm(2640 + 128 * h, 128, "rope", dest3(qbT, h), SC128, 128))
        for h in range(6):
            ch.append(fm(3408 + 128 * h, 128, "rope", dest3(kbT, h), 1.0, 128))
        for c in range(16):
            ch.append(fm(4944 + 128 * c, 128, "silu", dest2(g0T, 128 * c)))
        tm = [
            dict(seg=[(0, 1408, 128), (128, 4176, 384)], ncols=512,
                 dests=[(0, 128, tdest(va, 0, 128), "bf"), (128, 384, tdest(vb, 0, 384), "bf")]),
            dict(seg=[(0, 4560, 384), (384, 2624, 16)], ncols=400,
                 dests=[(0, 384, tdest(vb, 384, 384), "bf"), (384, 16, tdest(iw, 0, 16), "f32")]),
        ]
        phase_inproj(kb, cb, x, norm0, w_in0, ch, tm, cs128, cs64)
    if P("dsa"):
        phase_dsa(kb, cb, qaT, kaT, va, iqT, ikT, iw, g0T, yg0T)
    if P("dil"):
        phase_dilated(kb, cb, qbT, kbT, vb, g0T, yg0T)
    if P("out0"):
        phase_outproj(kb, yg0T, w_out0, x, x1, None)
    if P("in1"):
        ch = []
        for h in range(8):
            ch.append(fm(128 * h, 128, "copy", dest3(qcT, h), 1.0))
        for h in range(8):
            ch.append(fm(1024 + 128 * h, 128, "copy", dest3(kcT, h), SC128))
        for h in range(8):
            ch.append(fm(3072 + 128 * h, 128, "copy", dest3(qdT, h), 1.0))
        for h in range(8):
            ch.append(fm(4096 + 128 * h, 128, "copy", dest3(kdT, h), SC128))
        ch.append(fm(6144, 8, "copy32", dest2(flT, 0, 8)))
        for c in range(16):
            ch.append(fm(6152 + 128 * c, 128, "silu", dest2(g1T, 128 * c)))
        tm = []
        for i in range(2):
            tm.append(dict(seg=[(0, 2048 + 512 * i, 512)], ncols=512,
                           dests=[(0, 512, tdest(vc, 512 * i, 512), "bf")]))
        for i in range(2):
            tm.append(dict(seg=[(0, 5120 + 512 * i, 512)], ncols=512,
                           dests=[(0, 512, tdest(vd, 512 * i, 512), "bf")]))
        phase_inproj(kb, cb, x1, norm1, w_in1, ch, tm, None, None)
    if P("sb"):
        phase_sb(kb, cb, qcT, kcT, vc, g1T, yg1T)
    if P("fox"):
        phase_fox_prep(kb, flT, b_f1, fA, fB)
        phase_fox(kb, cb, qdT, kdT, vd, fA, fB, g1T, yg1T)
    if P("out1"):
        phase_outproj(kb, yg1T, w_out1, x1, out, norm_f)
    kb.barrier()
    return nc, kb


def host_consts():
    pos = np.arange(S, dtype=np.float32)

    def tables(hd, rows):
        half = hd // 2
        inv = (np.float32(10000.0) ** (-np.arange(half, dtype=np.float32) / np.float32(half))).astype(np.float32)
        ang = pos[None, :] * inv[:, None]
        cos = np.cos(ang).astype(np.float32)
        sin = np.sin(ang).astype(np.float32)
        c = np.zeros((2, rows, S), np.float32)
        for p in range(rows):
            d = p % hd
            c[0, p] = cos[d % half]
            c[1, p] = -sin[d % half] if d < half else sin[d % half]
        return c

    cs128 = tables(128, 128)
    cs64 = tables(64, 128)
    i = np.arange(128)[:, None]
    j = np.arange(128)[None, :]
    blocks = [
        (i == j), (i < j), (i >= j), (i <= j), np.ones((128, 128), bool),
    ]
    cbv = [b.astype(np.float32) for b in blocks]
    cbv.append(-(i >= j).astype(np.float32))
    cbv.append(-np.ones((128, 128), np.float32))
    cb = np.concatenate(cbv, axis=1).astype(ml_dtypes.bfloat16)
    return cs128, cs64, cb


_CACHE = {}


def kernel(x, norm0, w_in0, w_out0, norm1, w_in1, b_f1, w_out1, norm_f):
    if "nc" not in _CACHE:
        _CACHE["nc"] = build()[0]
        _CACHE["consts"] = host_consts()
    nc = _CACHE["nc"]
    cs128, cs64, cb = _CACHE["consts"]
    f = lambda a: np.ascontiguousarray(np.asarray(a, dtype=np.float32))
    shared = dict(norm0=f(norm0), w_in0=f(w_in0), w_out0=f(w_out0), norm1=f(norm1),
                  w_in1=f(w_in1), b_f1=f(b_f1), w_out1=f(w_out1), norm_f=f(norm_f),
                  cs128=cs128, cs64=cs64, cb=cb)
    x = np.asarray(x, dtype=np.float32)
    in_maps = [dict(shared, x=np.ascontiguousarray(x[i])) for i in range(8)]
    res = run_bass_kernel_spmd(nc, in_maps, core_ids=list(range(8)))
    return np.stack([np.asarray(r["out"], dtype=np.float32) for r in res.results], axis=0)
```

```python
import contextlib
import math
import numpy as np
import ml_dtypes
import concourse.bass as bass
import concourse.mybir as mybir
from concourse.bass_utils import run_bass_kernel_spmd

F32 = mybir.dt.float32
BF16 = mybir.dt.bfloat16
AF = mybir.ActivationFunctionType
ALU = mybir.AluOpType

S = 4096
D = 2048
NB = 32
EPS = 1e-6
NEG = -3.0e38
SC128 = 128 ** -0.5
IN0 = 6992
IN1 = 8200

EPOCH = 20000


class DSem:
    def __init__(self, handle):
        self.h = handle
        self.cnt = 0


class Buf:
    def __init__(self, ap, name):
        self.ap = ap
        self.name = name
        self.w = {}
        self.r = {}
        self.ds = None
        self.is_psum = False

    def __getitem__(self, k):
        return self.ap[k]


class KB:
    def __init__(self, nc):
        self.nc = nc
        self.eng = {"pe": nc.tensor, "act": nc.scalar, "dve": nc.vector,
                    "pool": nc.gpsimd, "sp": nc.sync}
        self.n = {e: 0 for e in self.eng}
        self.esems = {}
        self.waited = {e: {} for e in self.eng}
        self.gstack = contextlib.ExitStack()
        self.pstack = None
        self.free_ds = []
        self.all_ds = []
        self.phase_bufs = []

    def begin_phase(self):
        self.pstack = contextlib.ExitStack()
        self.phase_bufs = []
        self.pid = getattr(self, "pid", 0) + 1

    def end_phase(self):
        self.barrier()
        for b in self.phase_bufs:
            if b.ds is not None:
                self.free_ds.append(b.ds)
                b.ds = None
        self.pstack.close()
        self.pstack = None

    def sbuf(self, name, shape, dtype, glob=False):
        st = self.gstack if glob else self.pstack
        name = name if glob else f"{name}_p{self.pid}"
        t = st.enter_context(self.nc.sbuf_tensor(name, list(shape), dtype))
        b = Buf(t, name)
        if not glob:
            self.phase_bufs.append(b)
        return b

    def psum(self, name, shape, dtype):
        t = self.pstack.enter_context(self.nc.psum_tensor(f"{name}_p{self.pid}", list(shape), dtype))
        b = Buf(t, name)
        b.is_psum = True
        self.phase_bufs.append(b)
        return b

    def dram(self, name, shape, dtype, kind="Internal"):
        t = self.nc.dram_tensor(name, list(shape), dtype, kind=kind).ap()
        return Buf(t, name)

    def _get_ds(self, buf):
        if buf.ds is None:
            if self.free_ds:
                buf.ds = self.free_ds.pop()
            else:
                h = self.gstack.enter_context(self.nc.semaphore(f"d{len(self.all_ds)}"))
                buf.ds = DSem(h)
                self.all_ds.append(buf.ds)
        return buf.ds

    def _esem(self, ename, epoch):
        key = (ename, epoch)
        if key not in self.esems:
            self.esems[key] = self.gstack.enter_context(
                self.nc.semaphore(f"e_{ename}_{epoch}"))
        return self.esems[key]

    def wait(self, ename, tok):
        if tok[0] == "e":
            _, src, idx = tok
            k = ("e", src)
            if self.waited[ename].get(k, 0) >= idx:
                return
            epoch = (idx - 1) // EPOCH
            self.eng[ename].wait_ge(self._esem(src, epoch), idx - epoch * EPOCH)
            self.waited[ename][k] = idx
        else:
            _, ds, val = tok
            k = ("d", id(ds))
            if self.waited[ename].get(k, 0) >= val:
                return
            self.eng[ename].wait_ge(ds.h, val)
            self.waited[ename][k] = val

    def _deps(self, ename, reads, writes, is_dma):
        for b in reads:
            for tok in b.w.values():
                if (not is_dma) and tok[0] == "e" and tok[1] == ename and ename == "pe":
                    continue
                self.wait(ename, tok)
            if b.is_psum:
                for tok in b.r.values():
                    if tok[0] == "e" and tok[1] != ename:
                        self.wait(ename, tok)
        for b in writes:
            for tok in list(b.w.values()) + list(b.r.values()):
                if (not is_dma) and tok[0] == "e" and tok[1] == ename:
                    continue
                self.wait(ename, tok)

    @staticmethod
    def _key(tok):
        return ("e", tok[1]) if tok[0] == "e" else ("d", id(tok[1]))

    def op(self, ename, fn, reads=(), writes=()):
        self._deps(ename, reads, writes, False)
        ins = fn(self.eng[ename])
        self.n[ename] += 1
        idx = self.n[ename]
        epoch = (idx - 1) // EPOCH
        ins.then_inc(self._esem(ename, epoch), 1)
        tok = ("e", ename, idx)
        k = self._key(tok)
        for b in writes:
            b.w[k] = tok
        for b in reads:
            b.r[k] = tok
        return tok

    def dma(self, qname, out_ap, in_ap, sb, reads=(), writes=(), **kw):
        self._deps(qname, reads, writes, True)
        ds = self._get_ds(sb)
        ins = self.eng[qname].dma_start(out=out_ap, in_=in_ap, **kw)
        ds.cnt += 16
        ins.then_inc(ds.h, 16)
        tok = ("d", ds, ds.cnt)
        k = self._key(tok)
        for b in writes:
            b.w[k] = tok
        for b in reads:
            b.r[k] = tok
        return tok

    def barrier(self):
        toks = [("e", e, self.n[e]) for e in self.eng if self.n[e] > 0]
        toks += [("d", ds, ds.cnt) for ds in self.all_ds if ds.cnt > 0]
        for e in self.eng:
            for t in toks:
                if t[0] == "e" and t[1] == e:
                    continue
                self.wait(e, t)


def mm(kb, out, lhsT, rhs, start, stop, reads, writes):
    return kb.op("pe", lambda e: e.matmul(out, lhsT=lhsT, rhs=rhs, start=start, stop=stop),
                 reads=reads, writes=writes)


def act(kb, out, in_, func, reads, writes, **kw):
    return kb.op("act", lambda e: e.activation(out=out, in_=in_, func=func, **kw),
                 reads=reads, writes=writes)


def tt(kb, eng, out, in0, in1, op, reads, writes):
    return kb.op(eng, lambda e: e.tensor_tensor(out=out, in0=in0, in1=in1, op=op),
                 reads=reads, writes=writes)


class C:
    IDENT, LT, GE, LE, ONES, NEGU, NEGONES = range(7)


def cbs(cb, i, n=1):
    return cb[:, i * 128:(i + n) * 128]


def phase_inproj(kb, cb, xsrc, norm, w_in, fm_chunks, tm_pieces, cs128, cs64, pm32=None):
    kb.begin_phase()
    TG = 2048
    NTT = TG // 128
    hT = kb.sbuf("hT", [128, 16, TG], BF16)
    gsb = kb.sbuf("gsb", [128, 16], F32)
    xt = [kb.sbuf(f"xt{i}", [128, D], F32) for i in range(2)]
    xn = [kb.sbuf(f"xn{i}", [128, D], BF16) for i in range(2)]
    sq = kb.sbuf("sq", [128, D], BF16)
    stt = [kb.sbuf(f"stt{i}", [128, 4], F32) for i in range(2)]
    tp = [kb.psum(f"tp{i}", [128, 8, 128], BF16) for i in range(2)]
    pA = [kb.psum(f"pA{i}", [128, 512], F32) for i in range(2)]
    pB = [kb.psum(f"pB{i}", [128, 512], F32) for i in range(2)]
    pT = [kb.psum(f"pT{i}", [128, 512], F32) for i in range(2)]
    wA = [kb.sbuf(f"wA{i}", [128, 16, 128], BF16) for i in range(2)]
    a32 = [kb.sbuf(f"a32_{i}", [128, 512], BF16) for i in range(2)]
    perm = kb.sbuf("perm", [128, 2, 128], BF16)
    if pm32 is not None:
        kb.dma("sp", perm[:], pm32.ap[:, :, :], perm, reads=[pm32], writes=[perm])
    wT = [kb.sbuf(f"wT{i}", [128, 16, 512], BF16) for i in range(2)]
    t1 = [kb.sbuf(f"t1_{i}", [128, 512], F32) for i in range(2)]
    t2 = [kb.sbuf(f"t2_{i}", [128, 512], F32) for i in range(2)]
    ob = [kb.sbuf(f"ob{i}", [128, 512], BF16) for i in range(3)]
    o32 = [kb.sbuf(f"o32_{i}", [128, 512], F32) for i in range(2)]
    otm = [kb.sbuf(f"otm{i}", [128, 512], BF16) for i in range(2)]
    o16 = [kb.sbuf(f"o16_{i}", [128, 16], F32) for i in range(2)]
    csA = kb.sbuf("csA", [128, 2, TG], F32)
    csB = kb.sbuf("csB", [128, 2, TG], F32) if cs64 is not None else None

    kb.dma("sp", gsb[:], norm.ap.rearrange("(k p) -> p k", p=128), gsb, reads=[norm],
           writes=[gsb], allow_slow_non_contiguous=True)
    ident = cbs(cb, C.IDENT)
    wsrc = w_in.ap.rearrange("(k p) n -> p k n", p=128)

    def load_w(dst, segs):
        for (do, so, n) in segs:
            for k0 in range(0, 16, 8):
                kb.dma("pool", dst[:, k0:k0 + 8, do:do + n], wsrc[:, k0:k0 + 8, so:so + n], dst,
                       reads=[w_in], writes=[dst])

    cnt = {"a": 0, "o": 0, "t": 0, "o32": 0, "tm": 0, "otm": 0}
    for G in range(S // TG):
        g0 = G * TG
        if cs128 is not None:
            kb.dma("sp", csA[:], cs128.ap[:, :, g0:g0 + TG].rearrange("c p t -> p c t"), csA,
                   reads=[cs128], writes=[csA])
        if cs64 is not None:
            kb.dma("sp", csB[:], cs64.ap[:, :, g0:g0 + TG].rearrange("c p t -> p c t"), csB,
                   reads=[cs64], writes=[csB])
        for i in range(NTT):
            b = i % 2
            r0 = g0 + i * 128
            kb.dma("sp", xt[b][:], xsrc.ap[r0:r0 + 128, :], xt[b], reads=[xsrc], writes=[xt[b]])
            act(kb, sq[:], xt[b][:], AF.Square, [xt[b]], [sq, stt[b]], accum_out=stt[b][:, 0:1])
            kb.op("dve", lambda e: e.tensor_scalar(out=stt[b][:, 1:2], in0=stt[b][:, 0:1],
                                                   scalar1=1.0 / D, scalar2=EPS, op0=ALU.mult,
                                                   op1=ALU.add), reads=[stt[b]], writes=[stt[b]])
            act(kb, stt[b][:, 2:3], stt[b][:, 1:2], AF.Ln, [stt[b]], [stt[b]])
            act(kb, stt[b][:, 3:4], stt[b][:, 2:3], AF.Exp, [stt[b]], [stt[b]], scale=-0.5)
            kb.op("dve", lambda e: e.tensor_scalar(out=xn[b][:], in0=xt[b][:],
                                                   scalar1=stt[b][:, 3:4], scalar2=None,
                                                   op0=ALU.mult), reads=[xt[b], stt[b]],
                  writes=[xn[b]])
            for half in range(2):
                for k in range(8):
                    kk = half * 8 + k
                    kb.op("pe", lambda e: e.transpose(out=tp[half][:, k, :],
                                                      in_=xn[b][:, kk * 128:(kk + 1) * 128],
                                                      identity=ident),
                          reads=[xn[b], cb], writes=[tp[half]])
                tt(kb, "dve", hT[:, half * 8:(half + 1) * 8, i * 128:(i + 1) * 128],
                   tp[half][:, :, :],
                   gsb[:, half * 8:(half + 1) * 8].unsqueeze(2).broadcast_to([128, 8, 128]),
                   ALU.mult, [tp[half], gsb], [hT])
        nfm = len(fm_chunks)
        if nfm:
            load_w(wA[0], fm_chunks[0]["segA"])
        for ci, ch in enumerate(fm_chunks):
            wa = wA[ci % 2]
            if ci + 1 < nfm:
                nx = fm_chunks[ci + 1]
                load_w(wA[(ci + 1) % 2], nx["segA"])
            nco = ch["ncols"]
            for sub in range(TG // 512):
                tok0 = g0 + sub * 512
                sl = slice(sub * 512, (sub + 1) * 512)
                pa = pA[cnt["a"] % 2]
                pb = pB[cnt["a"] % 2]
                cnt["a"] += 1
                for k in range(16):
                    mm(kb, pa[0:nco, :], wa[:, k, 0:nco], hT[:, k, sl], k == 0, k == 15,
                       [wa, hT], [pa])
                if ch["kind"] == "rope":
                    a_ = a32[cnt["a"] % 2]
                    act(kb, a_[0:nco, :], pa[0:nco, :], AF.Copy, [pa], [a_])
                    pidx = 0 if ch["cs"] == 128 else 1
                    mm(kb, pb[0:nco, :], perm[0:nco, pidx, 0:nco], a_[0:nco, :], True, True,
                       [perm, a_], [pb])
                kind = ch["kind"]
                if kind == "copy32":
                    o = o32[cnt["o32"] % 2]
                    cnt["o32"] += 1
                    act(kb, o[0:nco, :], pa[0:nco, :], AF.Copy, [pa], [o])
                else:
                    o = ob[cnt["o"] % 3]
                    cnt["o"] += 1
                    if kind == "rope":
                        cst = csA if ch["cs"] == 128 else csB
                        a1, a2 = t1[cnt["t"] % 2], t2[cnt["t"] % 2]
                        cnt["t"] += 1
                        sc = float(ch["scale"])
                        kb.op("dve", lambda e: e.scalar_tensor_tensor(
                            out=a1[0:nco, :], in0=pa[0:nco, :], scalar=sc, in1=cst[0:nco, 0, sl],
                            op0=ALU.mult, op1=ALU.mult), reads=[pa, cst], writes=[a1])
                        kb.op("dve", lambda e: e.scalar_tensor_tensor(
                            out=a2[0:nco, :], in0=pb[0:nco, :], scalar=sc, in1=cst[0:nco, 1, sl],
                            op0=ALU.mult, op1=ALU.mult), reads=[pb, cst], writes=[a2])
                        tt(kb, "pool", o[0:nco, :], a1[0:nco, :], a2[0:nco, :], ALU.add,
                           [a1, a2], [o])
                    elif kind == "copy":
                        act(kb, o[0:nco, :], pa[0:nco, :], AF.Copy, [pa], [o],
                            scale=float(ch["scale"]))
                    elif kind == "silu":
                        act(kb, o[0:nco, :], pa[0:nco, :], AF.Silu, [pa], [o])
                dbuf, dap = ch["dest"](tok0)
                kb.dma("sp", dap, o[0:nco, :], o, reads=[o], writes=[dbuf])
        ntm = len(tm_pieces)
        if ntm:
            load_w(wT[0], tm_pieces[0]["seg"])
        for pi, pc in enumerate(tm_pieces):
            wt = wT[pi % 2]
            if pi + 1 < ntm:
                load_w(wT[(pi + 1) % 2], tm_pieces[pi + 1]["seg"])
            nco = pc["ncols"]
            for i in range(NTT):
                r0 = g0 + i * 128
                pt_ = pT[cnt["tm"] % 2]
                cnt["tm"] += 1
                for k in range(16):
                    mm(kb, pt_[:, 0:nco], hT[:, k, i * 128:(i + 1) * 128], wt[:, k, 0:nco],
                       k == 0, k == 15, [hT, wt], [pt_])
                for (off, n, dfn, dt_) in pc["dests"]:
                    if dt_ == "f32":
                        o = o16[cnt["otm"] % 2]
                    else:
                        o = otm[cnt["otm"] % 2]
                    cnt["otm"] += 1
                    act(kb, o[:, 0:n], pt_[:, off:off + n], AF.Copy, [pt_], [o])
                    dbuf, dap = dfn(r0)
                    kb.dma("sp", dap, o[:, 0:n], o, reads=[o], writes=[dbuf])
    kb.end_phase()


def attn_epilogue(kb, OT, DEN, gt, rden, ob, ygT, row0, t0, ncol=512):
    if DEN is not None:
        act(kb, rden[:, 0:ncol], DEN[:, 0:ncol], AF.Ln, [DEN], [rden])
        act(kb, rden[:, 0:ncol], rden[:, 0:ncol], AF.Exp, [rden], [rden], scale=-1.0)
        tt(kb, "pool", rden[:, 0:ncol], rden[:, 0:ncol], gt[:, 0:ncol], ALU.mult,
           [rden, gt], [rden])
        tt(kb, "dve", ob[:, 0:ncol], OT[:, 0:ncol], rden[:, 0:ncol], ALU.mult, [OT, rden], [ob])
    else:
        tt(kb, "dve", ob[:, 0:ncol], OT[:, 0:ncol], gt[:, 0:ncol], ALU.mult, [OT, gt], [ob])
    kb.dma("sp", ygT.ap[row0:row0 + 128, t0:t0 + ncol], ob[:, 0:ncol], ob, reads=[ob],
           writes=[ygT])


NIT = 22


def phase_dsa(kb, cb, qaT, kaT, va, iqT, ikT, iw, gT, ygT):
    kb.begin_phase()
    KT = kb.sbuf("KT", [128, S], BF16)
    VA = kb.sbuf("VA", [128, NB, 128], BF16)
    IK = kb.sbuf("IK", [128, S], BF16)
    iqg = [kb.sbuf(f"iqg{i}", [128, 8, 512], BF16) for i in range(2)]
    iwg = [kb.sbuf(f"iwg{i}", [128, 4, 16], F32) for i in range(2)]
    score = [kb.sbuf(f"score{i}", [128, S], F32) for i in range(2)]
    junk = kb.sbuf("junk", [128, S], BF16)
    junk2 = kb.sbuf("junk2", [128, S], BF16)
    nmd = [kb.sbuf(f"nmd{i}", [128, NIT], F32) for i in range(2)]
    rl = [kb.sbuf(f"rl{i}", [128, 512], F32) for i in range(3)]
    bs = [kb.sbuf(f"bs{i}", [128, 4], F32) for i in range(2)]
    wt = [kb.sbuf(f"wt{i}", [128, NIT], F32) for i in range(2)]
    md = [kb.sbuf(f"md{i}", [128, NIT + 2], F32) for i in range(2)]
    cn = [kb.sbuf(f"cn{i}", [128, NIT], F32) for i in range(2)]
    tq = [kb.sbuf(f"tq{i}", [128, NIT], F32) for i in range(2)]
    pw = kb.sbuf("pw", [128, NIT], F32)
    mbf = [kb.sbuf(f"mbf{i}", [128, S], BF16) for i in range(2)]
    maskT = [kb.sbuf(f"maskT{i}", [128, NB, 512], BF16) for i in range(2)]
    QT = [kb.sbuf(f"QT{i}", [128, 512], BF16) for i in range(2)]
    PT = [kb.sbuf(f"PT{i}", [128, 512], BF16) for i in range(3)]
    PM = [kb.sbuf(f"PM{i}", [128, 512], BF16) for i in range(4)]
    gt = [kb.sbuf(f"gt{i}", [128, 512], BF16) for i in range(2)]
    rden = [kb.sbuf(f"rden{i}", [128, 512], F32) for i in range(2)]
    ob = [kb.sbuf(f"obd{i}", [128, 512], BF16) for i in range(2)]
    pd = [kb.psum(f"pd{i}", [128, 512], F32) for i in range(2)]
    ptp = kb.psum("ptp", [128, 8, 128], BF16)
    pS = [kb.psum(f"pS{i}", [128, 512], F32) for i in range(3)]
    OT = kb.psum("OT", [128, 512], F32)
    DEN = kb.psum("DEN", [128, 512], F32)
    ident = cbs(cb, C.IDENT)
    ones = cbs(cb, C.ONES)

    kb.dma("sp", KT[:], kaT.ap[:, :], KT, reads=[kaT], writes=[KT])
    kb.dma("sp", IK[:], ikT.ap[:, :], IK, reads=[ikT], writes=[IK])
    vsrc = va.ap.rearrange("(b p) c -> p b c", p=128)
    for b0 in range(0, NB, 8):
        kb.dma("sp", VA[:, b0:b0 + 8, :], vsrc[:, b0:b0 + 8, :], VA, reads=[va], writes=[VA])
    for i in range(NIT):
        kb.op("dve", lambda e: e.memset(pw[:, i:i + 1], (0.5 ** (i + 1)) * 1.000001), writes=[pw])

    cnt = {"pd": 0, "q": 0, "ob": 0, "jb": 0}

    def idx_tasks(g):
        tasks = []
        t0 = g * 512
        iq_, iw_ = iqg[g % 2], iwg[g % 2]
        mT = maskT[g % 2]

        def loads():
            kb.dma("sp", iq_[:], iqT.ap[:, :, t0:t0 + 512].rearrange("c p t -> p c t"), iq_,
                   reads=[iqT], writes=[iq_])
            kb.dma("sp", iw_[:], iw.ap[t0:t0 + 512, :].rearrange("(j p) h -> p j h", p=128), iw_,
                   reads=[iw], writes=[iw_])
        parts = []
        for j in range(4):
            tasks = []
            qb = 4 * g + j
            n = (qb + 1) * 128
            jb = (4 * g + j) % 2
            sc, mb = score[jb], mbf[jb]
            b_, w_, m_, c_, t_ = bs[jb], wt[jb], md[jb], cn[jb], tq[jb]
            nch = (n + 511) // 512
            for chn in range(nch):
                w = min(512, n - chn * 512)
                cs = slice(chn * 512, chn * 512 + w)
                for h in range(16):
                    def idx_unit(h=h, w=w, cs=cs, j=j, sc=sc):
                        c, half = h // 2, h % 2
                        ps_ = slice(half * 64, (half + 1) * 64)
                        p = pd[cnt["pd"] % 2]
                        r = rl[cnt["pd"] % 3]
                        cnt["pd"] += 1
                        mm(kb, p[:, 0:w], iq_[ps_, c, j * 128:(j + 1) * 128], IK[ps_, cs], True,
                           True, [iq_, IK], [p])
                        act(kb, r[:, 0:w], p[:, 0:w], AF.Relu, [p], [r])
                        if h == 0:
                            kb.op("dve", lambda e: e.tensor_scalar(
                                out=sc[:, cs], in0=r[:, 0:w], scalar1=iw_[:, j, 0:1], scalar2=None,
                                op0=ALU.mult), reads=[r, iw_], writes=[sc])
                        else:
                            kb.op("dve", lambda e: e.scalar_tensor_tensor(
                                out=sc[:, cs], in0=r[:, 0:w], scalar=iw_[:, j, h:h + 1],
                                in1=sc[:, cs], op0=ALU.mult, op1=ALU.add), reads=[r, iw_, sc],
                                writes=[sc])
                    tasks.append(idx_unit)

            def causal(qb=qb, sc=sc):
                dsl = slice(qb * 128, (qb + 1) * 128)
                kb.op("pool", lambda e: e.affine_select(
                    out=sc[:, dsl], in_=sc[:, dsl], pattern=[[-1, 128]], compare_op=ALU.is_ge,
                    fill=NEG, base=0, channel_multiplier=1), reads=[sc], writes=[sc])
            tasks.append(causal)
            idx_part = tasks
            tasks = []
            if qb >= 2:
                def bracket(n=n, sc=sc, b_=b_, w_=w_, m_=m_, c_=c_):
                    kb.op("dve", lambda e: e.tensor_reduce(out=b_[:, 0:1], in_=sc[:, 0:n],
                                                           axis=mybir.AxisListType.X, op=ALU.max),
                          reads=[sc], writes=[b_])
                    kb.op("dve", lambda e: e.tensor_reduce(out=b_[:, 1:2], in_=sc[:, 0:n - 128],
                                                           axis=mybir.AxisListType.X, op=ALU.min),
                          reads=[sc], writes=[b_])
                    tt(kb, "dve", b_[:, 2:3], b_[:, 0:1], b_[:, 1:2], ALU.subtract, [b_], [b_])
                    kb.op("dve", lambda e: e.tensor_scalar(out=w_[:, :], in0=pw[:, :],
                                                           scalar1=b_[:, 2:3], scalar2=None,
                                                           op0=ALU.mult), reads=[pw, b_], writes=[w_])
                    kb.op("dve", lambda e: e.memset(c_[:, :], 0.0), writes=[c_])
                    tt(kb, "dve", m_[:, 0:1], b_[:, 1:2], w_[:, 0:1], ALU.add, [b_, w_], [m_])
                tasks.append(bracket)
                for it in range(NIT):
                    def bis(it=it, n=n, sc=sc, w_=w_, m_=m_, c_=c_, t_=t_, nm_=nmd[jb]):
                        if (it % 5) in (0, 2):
                            kb.op("dve", lambda e: e.tensor_scalar(
                                out=nm_[:, it:it + 1], in0=m_[:, it:it + 1], scalar1=-1.0,
                                scalar2=None, op0=ALU.mult), reads=[m_], writes=[nm_])
                            act(kb, junk2[:, 0:n], sc[:, 0:n], AF.Sign, [sc, nm_], [junk2, c_],
                                bias=nm_[:, it:it + 1], accum_out=c_[:, it:it + 1])
                            thr_c = 511.0 - n
                        else:
                            kb.op("dve", lambda e: e.tensor_scalar(
                                out=junk[:, 0:n], in0=sc[:, 0:n], scalar1=m_[:, it:it + 1],
                                scalar2=0.0, op0=ALU.is_ge, op1=ALU.add,
                                accum_out=c_[:, it:it + 1]), reads=[sc, m_], writes=[junk, c_])
                            thr_c = 255.5
                        kb.op("dve", lambda e: e.tensor_scalar(
                            out=t_[:, it:it + 1], in0=c_[:, it:it + 1], scalar1=thr_c, scalar2=0.5,
                            op0=ALU.is_ge, op1=ALU.subtract), reads=[c_], writes=[t_])
                        kb.op("dve", lambda e: e.scalar_tensor_tensor(
                            out=m_[:, it + 1:it + 2], in0=t_[:, it:it + 1], scalar=w_[:, it:it + 1],
                            in1=m_[:, it:it + 1], op0=ALU.mult, op1=ALU.add), reads=[t_, w_, m_],
                            writes=[m_])
                    tasks.append(bis)

                def mk(n=n, sc=sc, mb=mb, w_=w_, m_=m_):
                    kb.op("dve", lambda e: e.scalar_tensor_tensor(
                        out=m_[:, NIT + 1:NIT + 2], in0=w_[:, NIT - 1:NIT], scalar=-0.5,
                        in1=m_[:, NIT:NIT + 1], op0=ALU.mult, op1=ALU.add), reads=[w_, m_],
                        writes=[m_])
                    kb.op("dve", lambda e: e.tensor_scalar(
                        out=mb[:, 0:n], in0=sc[:, 0:n], scalar1=m_[:, NIT + 1:NIT + 2],
                        scalar2=None, op0=ALU.is_ge), reads=[sc, m_], writes=[mb])
                tasks.append(mk)
            else:
                def mk(n=n, sc=sc, mb=mb):
                    kb.op("dve", lambda e: e.tensor_scalar(
                        out=mb[:, 0:n], in0=sc[:, 0:n], scalar1=-1.0e30, scalar2=None,
                        op0=ALU.is_ge), reads=[sc], writes=[mb])
                tasks.append(mk)
            for k0 in range(0, qb + 1, 8):
                def tr(k0=k0, qb=qb, mb=mb, j=j):
                    nk = min(8, qb + 1 - k0)
                    for kk in range(nk):
                        kb.op("pe", lambda e: e.transpose(
                            out=ptp[:, kk, :], in_=mb[:, (k0 + kk) * 128:(k0 + kk + 1) * 128],
                            identity=ident), reads=[mb, cb], writes=[ptp])
                    act(kb, mT[:, k0:k0 + nk, j * 128:(j + 1) * 128], ptp[:, 0:nk, :], AF.Copy,
                        [ptp], [mT])
                tasks.append(tr)
            parts.append((idx_part, tasks))

        def merge(a, b):
            out, ia, ib = [], 0, 0
            na, nb_ = len(a), len(b)
            while ia < na or ib < nb_:
                if ib >= nb_ or (ia < na and ia * nb_ <= ib * na):
                    out.append(a[ia])
                    ia += 1
                else:
                    out.append(b[ib])
                    ib += 1
            return out

        res = [loads] + parts[0][0]
        for j in range(3):
            res += merge(parts[j][1], parts[j + 1][0])
        res += parts[3][1]
        return res

    for t in idx_tasks(0):
        t()
    for g in range(8):
        t0 = g * 512
        mT = maskT[g % 2]
        nxt = idx_tasks(g + 1) if g + 1 < 8 else []
        nkb = 4 * (g + 1)
        total_units = 10 * nkb
        done_tasks = 0
        unit_i = 0
        for h in range(10):
            q = QT[cnt["q"] % 2]
            gg = gt[cnt["q"] % 2]
            rd = rden[cnt["q"] % 2]
            cnt["q"] += 1
            kb.dma("sp", q[:], qaT.ap[h, :, t0:t0 + 512], q, reads=[qaT], writes=[q])
            kb.dma("sp", gg[:], gT.ap[h * 128:(h + 1) * 128, t0:t0 + 512], gg, reads=[gT],
                   writes=[gg])

            def qk(kbi):
                c0 = max(0, kbi - 4 * g) * 128
                p = pS[kbi % 3]
                mm(kb, p[:, c0:512], KT[:, kbi * 128:(kbi + 1) * 128], q[:, c0:512], True, True,
                   [KT, q], [p])

            def pv(kbi):
                c0 = max(0, kbi - 4 * g) * 128
                pm = PM[kbi % 4]
                mm(kb, OT[:, c0:512], VA[:, kbi, :], pm[:, c0:512], kbi == 0, kbi == nkb - 1,
                   [VA, pm], [OT])
                mm(kb, DEN[:, c0:512], ones, pm[:, c0:512], kbi == 0, kbi == nkb - 1,
                   [cb, pm], [DEN])

            qk(0)
            if nkb > 1:
                qk(1)
            for kbi in range(nkb):
                if kbi + 2 < nkb:
                    qk(kbi + 2)
                c0 = max(0, kbi - 4 * g) * 128
                p = pS[kbi % 3]
                pt_ = PT[kbi % 3]
                pm = PM[kbi % 4]
                act(kb, pt_[:, c0:512], p[:, c0:512], AF.Exp, [p], [pt_])
                tt(kb, "pool", pm[:, c0:512], pt_[:, c0:512], mT[:, kbi, c0:512], ALU.mult,
                   [pt_, mT], [pm])
                if kbi >= 1:
                    pv(kbi - 1)
                unit_i += 1
                target = (len(nxt) * unit_i) // total_units
                while done_tasks < target:
                    nxt[done_tasks]()
                    done_tasks += 1
            pv(nkb - 1)
            o = ob[cnt["ob"] % 2]
            cnt["ob"] += 1
            attn_epilogue(kb, OT, DEN, gg, rd, o, ygT, h * 128, t0)
        while done_tasks < len(nxt):
            nxt[done_tasks]()
            done_tasks += 1
    kb.end_phase()


def phase_dilated(kb, cb, qbT, kbT, vb, gT, ygT):
    kb.begin_phase()
    QT = [kb.sbuf(f"dQ{i}", [128, S], BF16) for i in range(2)]
    KT = [kb.sbuf(f"dK{i}", [128, S], BF16) for i in range(2)]
    VC = [kb.sbuf(f"dV{i}", [128, NB, 128], BF16) for i in range(2)]
    acc = kb.sbuf("acc", [128, S], F32)
    dacc = kb.sbuf("dacc", [128, S], F32)
    gt = kb.sbuf("dgt", [128, S], BF16)
    ob = kb.sbuf("dob", [128, S], BF16)
    PT = [kb.sbuf(f"dPT{i}", [128, 2, 128], BF16) for i in range(3)]
    PM = [kb.sbuf(f"dPM{i}", [128, 2, 128], BF16) for i in range(3)]
    pS = [kb.psum(f"dpS{i}", [128, 512], F32) for i in range(2)]
    pO = [kb.psum(f"dpO{i}", [128, 512], F32) for i in range(2)]
    pD = [kb.psum(f"dpD{i}", [128, 512], F32) for i in range(2)]
    ones = cbs(cb, C.ONES)
    band = cb[:, C.GE * 128:(C.GE + 2) * 128].rearrange("p (a c) -> p a c", a=2)
    u = 0
    vi = 0
    for h in range(6):
        q, k = QT[h % 2], KT[h % 2]
        kb.dma("sp", q[:], qbT.ap[h, :, :], q, reads=[qbT], writes=[q])
        kb.dma("sp", k[:], kbT.ap[h, :, :], k, reads=[kbT], writes=[k])
        kb.dma("sp", gt[:], gT.ap[1280 + h * 128:1280 + (h + 1) * 128, :], gt, reads=[gT],
               writes=[gt])
        for pi, dil in enumerate((1, 4, 16)):
            nb = NB // dil
            v = VC[vi % 2]
            vi += 1
            vsrc = vb.ap[:, h * 128:(h + 1) * 128].rearrange("(n i d) c -> i d n c", d=dil, i=128)
            vdst = v[:].rearrange("p (d n) c -> p d n c", d=dil)
            for r in range(dil):
                for n0 in range(0, nb, 8):
                    n1 = min(nb, n0 + 8)
                    kb.dma("sp", vdst[:, r, n0:n1, :], vsrc[:, r, n0:n1, :], v, reads=[vb],
                           writes=[v])
            units = [(r, n) for r in range(dil) for n in range(nb)]

            def cols(r, m):
                s0_ = m * 128 * dil + r
                return slice(s0_, s0_ + 127 * dil + 1, dil)

            def qk(ui, r, n):
                st = pS[ui % 2]
                s0 = 0 if n > 0 else 1
                stv = st[:, 0:256].rearrange("p (a c) -> p a c", a=2)
                for sl_ in range(s0, 2):
                    m = n - 1 + sl_
                    mm(kb, stv[:, sl_, :], k[:, cols(r, m)], q[:, cols(r, n)], True, True, [k, q],
                       [st])

            qk(u, *units[0])
            for li, (r, n) in enumerate(units):
                if li + 1 < len(units):
                    qk(u + 1, *units[li + 1])
                st = pS[u % 2]
                po = pO[u % 2]
                pdn = pD[u % 2]
                pt_ = PT[u % 3]
                pm = PM[u % 3]
                u += 1
                s0 = 0 if n > 0 else 1
                stv = st[:, 0:256].rearrange("p (a c) -> p a c", a=2)
                act(kb, pt_[:, s0:2, :], stv[:, s0:2, :], AF.Exp, [st], [pt_])
                tt(kb, "pool", pm[:, s0:2, :], pt_[:, s0:2, :], band[:, s0:2, :], ALU.mult,
                   [pt_, cb], [pm])
                for sl_ in range(s0, 2):
                    m = n - 1 + sl_
                    mm(kb, po[:, 0:128], v[:, r * nb + m, :], pm[:, sl_, :], sl_ == s0, sl_ == 1,
                       [v, pm], [po])
                for sl_ in range(s0, 2):
                    mm(kb, pdn[:, 0:128], ones, pm[:, sl_, :], sl_ == s0, sl_ == 1, [cb, pm],
                       [pdn])
                cs = cols(r, n)
                if pi == 0:
                    kb.op("dve", lambda e: e.tensor_copy(out=acc[:, cs], in_=po[:, 0:128]),
                          reads=[po], writes=[acc])
                    kb.op("dve", lambda e: e.tensor_copy(out=dacc[:, cs], in_=pdn[:, 0:128]),
                          reads=[pdn], writes=[dacc])
                else:
                    tt(kb, "dve", acc[:, cs], po[:, 0:128], acc[:, cs], ALU.add, [po, acc], [acc])
                    tt(kb, "dve", dacc[:, cs], pdn[:, 0:128], dacc[:, cs], ALU.add,
                       [pdn, dacc], [dacc])
        act(kb, dacc[:], dacc[:], AF.Ln, [dacc], [dacc])
        act(kb, dacc[:], dacc[:], AF.Exp, [dacc], [dacc], scale=-1.0)
        tt(kb, "dve", acc[:], acc[:], dacc[:], ALU.mult, [acc, dacc], [acc])
        tt(kb, "dve", ob[:], acc[:], gt[:], ALU.mult, [acc, gt], [ob])
        kb.dma("sp", ygT.ap[1280 + h * 128:1280 + (h + 1) * 128, :], ob[:], ob, reads=[ob],
               writes=[ygT])
    kb.end_phase()


def phase_sb(kb, cb, qcT, kcT, vc, gT, ygT):
    kb.begin_phase()
    KT = [kb.sbuf(f"sK{i}", [128, S], BF16) for i in range(2)]
    VV = [kb.sbuf(f"sV{i}", [128, NB, 128], BF16) for i in range(2)]
    QT = [kb.sbuf(f"sQ{i}", [128, 512], BF16) for i in range(2)]
    gt = [kb.sbuf(f"sg{i}", [128, 512], BF16) for i in range(2)]
    E = [kb.sbuf(f"sE{i}", [128, 512], F32) for i in range(2)]
    LP = [kb.sbuf(f"sL{i}", [128, 512], BF16) for i in range(3)]
    RS = kb.sbuf("sRS", [128, 512], BF16)
    PT = [kb.sbuf(f"sP{i}", [128, 512], BF16) for i in range(4)]
    ob = [kb.sbuf(f"sob{i}", [128, 512], BF16) for i in range(2)]
    pZ = [kb.psum(f"pZ{i}", [128, 512], F32) for i in range(3)]
    pW = [kb.psum(f"pW{i}", [128, 512], F32) for i in range(2)]
    OT = [kb.psum(f"sOT{i}", [128, 512], F32) for i in range(2)]
    negU = cbs(cb, C.NEGU)
    negones = cbs(cb, C.NEGONES)
    lt = cbs(cb, C.LT)
    u = 0
    for h in range(8):
        k, v = KT[h % 2], VV[h % 2]
        kb.dma("sp", k[:], kcT.ap[h, :, :], k, reads=[kcT], writes=[k])
        vsrc = vc.ap[:, h * 128:(h + 1) * 128].rearrange("(b p) c -> p b c", p=128)
        for b0 in range(0, NB, 8):
            kb.dma("sp", v[:, b0:b0 + 8, :], vsrc[:, b0:b0 + 8, :], v, reads=[vc], writes=[v])
        for g in range(8):
            t0 = g * 512
            q = QT[u % 2]
            gg = gt[u % 2]
            ot = OT[u % 2]
            o = ob[u % 2]
            u += 1
            kb.dma("sp", q[:], qcT.ap[h, :, t0:t0 + 512], q, reads=[qcT], writes=[q])
            kb.dma("sp", gg[:], gT.ap[h * 128:(h + 1) * 128, t0:t0 + 512], gg, reads=[gT],
                   writes=[gg])
            nkb = 4 * (g + 1)
            order = list(range(nkb - 1, -1, -1))

            def zmm(kbi, idx):
                c0 = max(0, kbi - 4 * g) * 128
                mm(kb, pZ[idx % 3][:, c0:512], k[:, kbi * 128:(kbi + 1) * 128], q[:, c0:512],
                   True, True, [k, q], [pZ[idx % 3]])

            def c0_of(kbi):
                return max(0, kbi - 4 * g) * 128

            def st_el(idx):
                kbi = order[idx]
                c0 = c0_of(kbi)
                z, e_, lp = pZ[idx % 3], E[idx % 2], LP[idx % 3]
                act(kb, e_[:, c0:512], z[:, c0:512], AF.Exp, [z], [e_])
                act(kb, lp[:, c0:512], e_[:, c0:512], AF.Ln, [e_], [lp], bias=1.0)
                if kbi >= 4 * g:
                    tt(kb, "pool", lp[:, c0:c0 + 128], lp[:, c0:c0 + 128], lt, ALU.mult,
                       [lp, cb], [lp])

            def st_w(idx):
                kbi = order[idx]
                c0 = c0_of(kbi)
                w, lp = pW[idx % 2], LP[idx % 3]
                mm(kb, w[:, c0:512], k[:, kbi * 128:(kbi + 1) * 128], q[:, c0:512], True, False,
                   [k, q], [w])
                mm(kb, w[:, c0:512], negU, lp[:, c0:512], False, idx == 0, [cb, lp], [w])
                if idx > 0:
                    mm(kb, w[:, c0:512], negones, RS[:, c0:512], False, True, [cb, RS], [w])
                if idx == 0:
                    kb.op("dve", lambda e: e.memset(RS[:], 0.0), writes=[RS])
                if idx + 1 < nkb:
                    tt(kb, "dve", RS[:, c0:512], RS[:, c0:512], lp[:, c0:512], ALU.add, [RS, lp],
                       [RS])

            def st_p(idx):
                kbi = order[idx]
                c0 = c0_of(kbi)
                w, pt_ = pW[idx % 2], PT[idx % 4]
                act(kb, pt_[:, c0:512], w[:, c0:512], AF.Exp, [w], [pt_])
                if kbi >= 4 * g:
                    tt(kb, "pool", pt_[:, c0:c0 + 128], pt_[:, c0:c0 + 128], lt, ALU.mult,
                       [pt_, cb], [pt_])

            def st_pv(idx):
                kbi = order[idx]
                c0 = c0_of(kbi)
                pt_ = PT[idx % 4]
                mm(kb, ot[:, c0:512], v[:, kbi, :], pt_[:, c0:512], idx == 0, idx == nkb - 1,
                   [v, pt_], [ot])

            zmm(order[0], 0)
            if nkb > 1:
                zmm(order[1], 1)
            for i in range(nkb + 3):
                if i < nkb:
                    st_el(i)
                if i + 2 < nkb:
                    zmm(order[i + 2], i + 2)
                if 0 <= i - 1 < nkb:
                    st_w(i - 1)
                if 0 <= i - 2 < nkb:
                    st_p(i - 2)
                if 0 <= i - 3 < nkb:
                    st_pv(i - 3)
            attn_epilogue(kb, ot, None, gg, None, o, ygT, h * 128, t0)
    kb.end_phase()


def phase_fox_prep(kb, flT, b_f, fA, fB):
    kb.begin_phase()
    fl = kb.sbuf("fl", [8, S], F32)
    bf = kb.sbuf("bf", [8, 2], F32)
    e_ = kb.sbuf("fe", [8, S], F32)
    ones = kb.sbuf("fones", [8, S], F32)
    cc = kb.sbuf("fcc", [8, S], F32)
    rr = kb.sbuf("frr", [8, S], F32)
    A = kb.sbuf("fAs", [8, 6, S], BF16)
    B = kb.sbuf("fBs", [8, 6, S], BF16)
    kb.dma("sp", fl[:], flT.ap[:, :], fl, reads=[flT], writes=[fl])
    kb.dma("sp", bf[:, 0:1], b_f.ap.rearrange("(h o) -> h o", o=1), bf, reads=[b_f], writes=[bf])
    kb.op("dve", lambda e: e.tensor_scalar(out=bf[:, 1:2], in0=bf[:, 0:1], scalar1=-1.0,
                                           scalar2=None, op0=ALU.mult), reads=[bf], writes=[bf])
    act(kb, e_[:], fl[:], AF.Exp, [fl, bf], [e_], scale=-1.0, bias=bf[:, 1:2])
    act(kb, e_[:], e_[:], AF.Ln, [e_], [e_], bias=1.0)
    kb.op("dve", lambda e: e.memset(ones[:], 1.0), writes=[ones])
    kb.op("dve", lambda e: e.tensor_tensor_scan(out=cc[:], data0=ones[:], data1=e_[:], initial=0.0,
                                                op0=ALU.mult, op1=ALU.add), reads=[ones, e_],
          writes=[cc])
    cur = cc
    for i in range(3):
        kb.op("dve", lambda e: e.tensor_copy(out=A[:, i, :], in_=cur[:]), reads=[cur], writes=[A])
        kb.op("dve", lambda e: e.tensor_scalar(out=B[:, 3 + i, :], in0=A[:, i, :], scalar1=-1.0,
                                               scalar2=None, op0=ALU.mult), reads=[A], writes=[B])
        if i < 2:
            nxt = rr if cur is cc else cc
            tt(kb, "dve", nxt[:], cur[:], A[:, i, :], ALU.subtract, [cur, A], [nxt])
            cur = nxt
    kb.op("dve", lambda e: e.memset(A[:, 3:6, :], 1.0), writes=[A])
    kb.op("dve", lambda e: e.memset(B[:, 0:3, :], 1.0), writes=[B])
    kb.dma("sp", fA.ap[:, :, :], A[:], A, reads=[A], writes=[fA])
    kb.dma("sp", fB.ap[:, :, :], B[:], B, reads=[B], writes=[fB])
    kb.end_phase()


def phase_fox(kb, cb, qdT, kdT, vd, fA, fB, gT, ygT):
    kb.begin_phase()
    KT = [kb.sbuf(f"fK{i}", [128, S], BF16) for i in range(2)]
    VV = [kb.sbuf(f"fV{i}", [128, NB, 128], BF16) for i in range(2)]
    AA = [kb.sbuf(f"fA{i}", [6, S], BF16) for i in range(2)]
    BB = [kb.sbuf(f"fB{i}", [6, S], BF16) for i in range(2)]
    QT = [kb.sbuf(f"fQ{i}", [128, 512], BF16) for i in range(2)]
    gt = [kb.sbuf(f"fg{i}", [128, 512], BF16) for i in range(2)]
    PT = [kb.sbuf(f"fP{i}", [128, 512], BF16) for i in range(4)]
    rden = kb.sbuf("frden", [128, 512], F32)
    ob = [kb.sbuf(f"fob{i}", [128, 512], BF16) for i in range(2)]
    pS = [kb.psum(f"fpS{i}", [128, 512], F32) for i in range(3)]
    OT = [kb.psum(f"fOT{i}", [128, 512], F32) for i in range(2)]
    DEN = [kb.psum(f"fDEN{i}", [128, 512], F32) for i in range(2)]
    ones = cbs(cb, C.ONES)
    le = cbs(cb, C.LE)
    u = 0
    for h in range(8):
        k, v, A, B = KT[h % 2], VV[h % 2], AA[h % 2], BB[h % 2]
        kb.dma("sp", k[:], kdT.ap[h, :, :], k, reads=[kdT], writes=[k])
        kb.dma("sp", A[:], fA.ap[h, :, :], A, reads=[fA], writes=[A])
        kb.dma("sp", B[:], fB.ap[h, :, :], B, reads=[fB], writes=[B])
        vsrc = vd.ap[:, h * 128:(h + 1) * 128].rearrange("(b p) c -> p b c", p=128)
        for b0 in range(0, NB, 8):
            kb.dma("sp", v[:, b0:b0 + 8, :], vsrc[:, b0:b0 + 8, :], v, reads=[vd], writes=[v])
        for g in range(8):
            t0 = g * 512
            q = QT[u % 2]
            gg = gt[u % 2]
            ot = OT[u % 2]
            dn = DEN[u % 2]
            o = ob[u % 2]
            u += 1
            kb.dma("sp", q[:], qdT.ap[h, :, t0:t0 + 512], q, reads=[qdT], writes=[q])
            kb.dma("sp", gg[:], gT.ap[1024 + h * 128:1024 + (h + 1) * 128, t0:t0 + 512], gg,
                   reads=[gT], writes=[gg])
            nkb = 4 * (g + 1)

            def qk(kbi):
                c0 = max(0, kbi - 4 * g) * 128
                p = pS[kbi % 3]
                mm(kb, p[:, c0:512], k[:, kbi * 128:(kbi + 1) * 128], q[:, c0:512], True, False,
                   [k, q], [p])
                mm(kb, p[:, c0:512], A[:, kbi * 128:(kbi + 1) * 128], B[:, t0 + c0:t0 + 512],
                   False, True, [A, B], [p])

            def pv(kbi):
                c0 = max(0, kbi - 4 * g) * 128
                pt_ = PT[kbi % 4]
                mm(kb, ot[:, c0:512], v[:, kbi, :], pt_[:, c0:512], kbi == 0, kbi == nkb - 1,
                   [v, pt_], [ot])
                mm(kb, dn[:, c0:512], ones, pt_[:, c0:512], kbi == 0, kbi == nkb - 1,
                   [cb, pt_], [dn])

            qk(0)
            if nkb > 1:
                qk(1)
            for kbi in range(nkb):
                if kbi + 2 < nkb:
                    qk(kbi + 2)
                c0 = max(0, kbi - 4 * g) * 128
                p = pS[kbi % 3]
                pt_ = PT[kbi % 4]
                act(kb, pt_[:, c0:512], p[:, c0:512], AF.Exp, [p], [pt_])
                if kbi >= 4 * g:
                    tt(kb, "pool", pt_[:, c0:c0 + 128], pt_[:, c0:c0 + 128], le, ALU.mult,
                       [pt_, cb], [pt_])
                if kbi >= 1:
                    pv(kbi - 1)
            pv(nkb - 1)
            attn_epilogue(kb, ot, dn, gg, rden, o, ygT, 1024 + h * 128, t0)
    kb.end_phase()


def phase_outproj(kb, ygT, w_out, xres, xdst, norm_f):
    kb.begin_phase()
    W = kb.sbuf("oW", [128, 16, D], BF16)
    yg = [kb.sbuf(f"oyg{i}", [128, 16, 512], BF16) for i in range(2)]
    xr = [kb.sbuf(f"oxr{i}", [128, D], F32) for i in range(2)]
    xw = [kb.sbuf(f"oxw{i}", [128, D], F32) for i in range(2)]
    pp = [kb.psum(f"opp{i}", [128, 512], F32) for i in range(4)]
    if norm_f is not None:
        nf = kb.sbuf("onf", [128, D], F32)
        sq = kb.sbuf("osq", [128, D], BF16)
        stt = [kb.sbuf(f"ost{i}", [128, 4], F32) for i in range(2)]
        kb.dma("sp", nf[:], norm_f.ap.partition_broadcast(128), nf, reads=[norm_f], writes=[nf])
    wsrc = w_out.ap.rearrange("(k p) n -> p k n", p=128)
    for k0 in range(0, 16, 4):
        kb.dma("pool", W[:, k0:k0 + 4, :], wsrc[:, k0:k0 + 4, :], W, reads=[w_out], writes=[W])
    ysrc = ygT.ap.rearrange("(k p) t -> p k t", p=128)
    ti = 0
    for g in range(8):
        y = yg[g % 2]
        for k0 in range(0, 16, 8):
            kb.dma("sp", y[:, k0:k0 + 8, :], ysrc[:, k0:k0 + 8, g * 512:(g + 1) * 512], y,
                   reads=[ygT], writes=[y])
        for j in range(4):
            r0 = g * 512 + j * 128
            xi, xo = xr[ti % 2], xw[ti % 2]
            kb.dma("sp", xi[:], xres.ap[r0:r0 + 128, :], xi, reads=[xres], writes=[xi])
            for n in range(4):
                p = pp[n]
                for k in range(16):
                    mm(kb, p[:, :], y[:, k, j * 128:(j + 1) * 128], W[:, k, n * 512:(n + 1) * 512],
                       k == 0, k == 15, [y, W], [p])
                tt(kb, "dve", xo[:, n * 512:(n + 1) * 512], p[:, :], xi[:, n * 512:(n + 1) * 512],
                   ALU.add, [p, xi], [xo])
            if norm_f is None:
                kb.dma("sp", xdst.ap[r0:r0 + 128, :], xo[:], xo, reads=[xo], writes=[xdst])
            else:
                st_ = stt[ti % 2]
                act(kb, sq[:], xo[:], AF.Square, [xo], [sq, st_], accum_out=st_[:, 0:1])
                kb.op("dve", lambda e: e.tensor_scalar(out=st_[:, 1:2], in0=st_[:, 0:1],
                                                       scalar1=1.0 / D, scalar2=EPS, op0=ALU.mult,
                                                       op1=ALU.add), reads=[st_], writes=[st_])
                act(kb, st_[:, 2:3], st_[:, 1:2], AF.Ln, [st_], [st_])
                act(kb, st_[:, 3:4], st_[:, 2:3], AF.Exp, [st_], [st_], scale=-0.5)
                kb.op("dve", lambda e: e.scalar_tensor_tensor(
                    out=xi[:], in0=xo[:], scalar=st_[:, 3:4], in1=nf[:], op0=ALU.mult,
                    op1=ALU.mult), reads=[xo, st_, nf], writes=[xi])
                kb.dma("sp", xdst.ap[r0:r0 + 128, :], xi[:], xi, reads=[xi], writes=[xdst])
            ti += 1
    kb.end_phase()


def rope_segs(c0, n, half):
    segs = []
    for hs in range(0, n, 2 * half):
        segs.append((hs, c0 + hs + half, half))
        segs.append((hs + half, c0 + hs, half))
    return segs


def build(phases=None, debug=False):
    nc = bass.Bass("TRN2", target_bir_lowering=False)
    kb = KB(nc)
    P = (lambda p: True) if phases is None else (lambda p: p in phases)

    def ein(name, shape, dt=F32):
        return kb.dram(name, shape, dt, kind="ExternalInput")

    def scr(name, shape, dt):
        return kb.dram(name, shape, dt, kind="ExternalOutput")

    x = ein("x", [S, D])
    norm0 = ein("norm0", [D])
    w_in0 = ein("w_in0", [D, IN0])
    w_out0 = ein("w_out0", [D, D])
    norm1 = ein("norm1", [D])
    w_in1 = ein("w_in1", [D, IN1])
    b_f1 = ein("b_f1", [8])
    w_out1 = ein("w_out1", [D, D])
    norm_f = ein("norm_f", [D])
    cs128 = ein("cs128", [2, 128, S])
    cs64 = ein("cs64", [2, 128, S])
    cbd = ein("cb", [128, 7 * 128], BF16)
    pm32 = ein("pm32", [128, 2, 128], BF16)
    out = kb.dram("out", [S, D], F32, kind="ExternalOutput")

    qaT = scr("qaT", [10, 128, S], BF16)
    kaT = scr("kaT", [128, S], BF16)
    va = scr("va", [S, 128], BF16)
    iqT = scr("iqT", [8, 128, S], BF16)
    ikT = scr("ikT", [128, S], BF16)
    iw = scr("iw", [S, 16], F32)
    qbT = scr("qbT", [6, 128, S], BF16)
    kbT = scr("kbT", [6, 128, S], BF16)
    vb = scr("vb", [S, 768], BF16)
    g0T = scr("g0T", [D, S], BF16)
    yg0T = scr("yg0T", [D, S], BF16)
    x1 = scr("x1", [S, D], F32)
    qcT = scr("qcT", [8, 128, S], BF16)
    kcT = scr("kcT", [8, 128, S], BF16)
    vc = scr("vc", [S, 1024], BF16)
    qdT = scr("qdT", [8, 128, S], BF16)
    kdT = scr("kdT", [8, 128, S], BF16)
    vd = scr("vd", [S, 1024], BF16)
    flT = scr("flT", [8, S], F32)
    fA = scr("fA", [8, 6, S], BF16)
    fB = scr("fB", [8, 6, S], BF16)
    g1T = scr("g1T", [D, S], BF16)
    yg1T = scr("yg1T", [D, S], BF16)

    cb = kb.sbuf("cb_sb", [128, 7 * 128], BF16, glob=True)
    kb.dma("sp", cb[:], cbd.ap[:, :], cb, reads=[cbd], writes=[cb])

    def fm(c0, n, kind, dest, scale=1.0, cs=None, dup=False):
        half = 64 if cs == 128 else 32
        if dup:
            segA = [(0, c0, n), (n, c0, n)]
            segB = [(d, s, m) for (d, s, m) in rope_segs(c0, n, half)] + \
                   [(d + n, s, m) for (d, s, m) in rope_segs(c0, n, half)]
            n = 2 * n
        else:
            segA = [(0, c0, n)]
            segB = rope_segs(c0, n, half) if kind == "rope" else None
        return dict(segA=segA, segB=segB, ncols=n, kind=kind, scale=scale, dest=dest, cs=cs)

    def dest3(buf, h):
        return lambda tok0: (buf, buf.ap[h, :, tok0:tok0 + 512])

    def dest2(buf, r0, n=128):
        return lambda tok0: (buf, buf.ap[r0:r0 + n, tok0:tok0 + 512])

    def tdest(buf, c0, n):
        return lambda r0: (buf, buf.ap[r0:r0 + 128, c0:c0 + n])

    if P("in0"):
        ch = []
        for h in range(10):
            ch.append(fm(128 * h, 128, "rope", dest3(qaT, h), SC128, 128))
        ch.append(fm(1280, 128, "rope", dest2(kaT, 0), 1.0, 128))
        for c in range(8):
            ch.append(fm(1536 + 128 * c, 128, "rope", dest3(iqT, c), 1.0, 64))
        ch.append(fm(2560, 64, "rope", dest2(ikT, 0), 1.0, 64, dup=True))
        for h in range(6):
            ch.append(fm(2640 + 128 * h, 128, "rope", dest3(qbT, h), SC128, 128))
        for h in range(6):
            ch.append(fm(3408 + 128 * h, 128, "rope", dest3(kbT, h), 1.0, 128))
        for c in range(16):
            ch.append(fm(4944 + 128 * c, 128, "silu", dest2(g0T, 128 * c)))
        tm = [
            dict(seg=[(0, 1408, 128), (128, 4176, 384)], ncols=512,
                 dests=[(0, 128, tdest(va, 0, 128), "bf"), (128, 384, tdest(vb, 0, 384), "bf")]),
            dict(seg=[(0, 4560, 384), (384, 2624, 16)], ncols=400,
                 dests=[(0, 384, tdest(vb, 384, 384), "bf"), (384, 16, tdest(iw, 0, 16), "f32")]),
        ]
        phase_inproj(kb, cb, x, norm0, w_in0, ch, tm, cs128, cs64, pm32)
    if P("dsa"):
        phase_dsa(kb, cb, qaT, kaT, va, iqT, ikT, iw, g0T, yg0T)
    if P("dil"):
        phase_dilated(kb, cb, qbT, kbT, vb, g0T, yg0T)
    if P("out0"):
        phase_outproj(kb, yg0T, w_out0, x, x1, None)
    if P("in1"):
        ch = []
        for h in range(8):
            ch.append(fm(128 * h, 128, "copy", dest3(qcT, h), 1.0))
        for h in range(8):
            ch.append(fm(1024 + 128 * h, 128, "copy", dest3(kcT, h), SC128))
        for h in range(8):
            ch.append(fm(3072 + 128 * h, 128, "copy", dest3(qdT, h), 1.0))
        for h in range(8):
            ch.append(fm(4096 + 128 * h, 128, "copy", dest3(kdT, h), SC128))
        ch.append(fm(6144, 8, "copy32", dest2(flT, 0, 8)))
        for c in range(16):
            ch.append(fm(6152 + 128 * c, 128, "silu", dest2(g1T, 128 * c)))
        tm = []
        for i in range(2):
            tm.append(dict(seg=[(0, 2048 + 512 * i, 512)], ncols=512,
                           dests=[(0, 512, tdest(vc, 512 * i, 512), "bf")]))
        for i in range(2):
            tm.append(dict(seg=[(0, 5120 + 512 * i, 512)], ncols=512,
                           dests=[(0, 512, tdest(vd, 512 * i, 512), "bf")]))
        phase_inproj(kb, cb, x1, norm1, w_in1, ch, tm, None, None)
    if P("sb"):
        phase_sb(kb, cb, qcT, kcT, vc, g1T, yg1T)
    if P("fox"):
        phase_fox_prep(kb, flT, b_f1, fA, fB)
        phase_fox(kb, cb, qdT, kdT, vd, fA, fB, g1T, yg1T)
    if P("out1"):
        phase_outproj(kb, yg1T, w_out1, x1, out, norm_f)
    kb.barrier()
    return nc, kb


def host_consts():
    pos = np.arange(S, dtype=np.float32)

    def tables(hd, rows):
        half = hd // 2
        inv = (np.float32(10000.0) ** (-np.arange(half, dtype=np.float32) / np.float32(half))).astype(np.float32)
        ang = pos[None, :] * inv[:, None]
        cos = np.cos(ang).astype(np.float32)
        sin = np.sin(ang).astype(np.float32)
        c = np.zeros((2, rows, S), np.float32)
        for p in range(rows):
            d = p % hd
            c[0, p] = cos[d % half]
            c[1, p] = -sin[d % half] if d < half else sin[d % half]
        return c

    cs128 = tables(128, 128)
    cs64 = tables(64, 128)
    i = np.arange(128)[:, None]
    j = np.arange(128)[None, :]
    blocks = [
        (i == j), (i < j), (i >= j), (i <= j), np.ones((128, 128), bool),
    ]
    cbv = [b.astype(np.float32) for b in blocks]
    cbv.append(-(i >= j).astype(np.float32))
    cbv.append(-np.ones((128, 128), np.float32))
    cb = np.concatenate(cbv, axis=1).astype(ml_dtypes.bfloat16)
    pm = np.zeros((128, 2, 128), np.float32)
    pm_dt = ml_dtypes.bfloat16
    for m in range(128):
        pm[(m + 64) % 128, 0, m] = 1.0
        pm[(m // 64) * 64 + ((m % 64) + 32) % 64, 1, m] = 1.0
    return cs128, cs64, cb, pm.astype(pm_dt)


_CACHE = {}


def kernel(x, norm0, w_in0, w_out0, norm1, w_in1, b_f1, w_out1, norm_f):
    if "nc" not in _CACHE:
        _CACHE["nc"] = build()[0]
        _CACHE["consts"] = host_consts()
    nc = _CACHE["nc"]
    cs128, cs64, cb, pm = _CACHE["consts"]
    f = lambda a: np.ascontiguousarray(np.asarray(a, dtype=np.float32))
    shared = dict(norm0=f(norm0), w_in0=f(w_in0), w_out0=f(w_out0), norm1=f(norm1),
                  w_in1=f(w_in1), b_f1=f(b_f1), w_out1=f(w_out1), norm_f=f(norm_f),
                  cs128=cs128, cs64=cs64, cb=cb, pm32=pm)
    x = np.asarray(x, dtype=np.float32)
    in_maps = [dict(shared, x=np.ascontiguousarray(x[i])) for i in range(8)]
    res = run_bass_kernel_spmd(nc, in_maps, core_ids=list(range(8)))
    return np.stack([np.asarray(r["out"], dtype=np.float32) for r in res.results], axis=0)
```

```python
import contextlib
import math
import numpy as np
import ml_dtypes
import concourse.bass as bass
import concourse.mybir as mybir
from concourse.bass_utils import run_bass_kernel_spmd

F32 = mybir.dt.float32
BF16 = mybir.dt.bfloat16
AF = mybir.ActivationFunctionType
ALU = mybir.AluOpType

S = 4096
D = 2048
NB = 32
EPS = 1e-6
NEG = -3.0e38
SC128 = 128 ** -0.5
IN0 = 6992
IN1 = 8200

EPOCH = 20000


class DSem:
    def __init__(self, handle):
        self.h = handle
        self.cnt = 0


class Buf:
    def __init__(self, ap, name):
        self.ap = ap
        self.name = name
        self.w = {}
        self.r = {}
        self.ds = None
        self.is_psum = False

    def __getitem__(self, k):
        return self.ap[k]


class KB:
    def __init__(self, nc):
        self.nc = nc
        self.eng = {"pe": nc.tensor, "act": nc.scalar, "dve": nc.vector,
                    "pool": nc.gpsimd, "sp": nc.sync}
        self.n = {e: 0 for e in self.eng}
        self.esems = {}
        self.waited = {e: {} for e in self.eng}
        self.gstack = contextlib.ExitStack()
        self.pstack = None
        self.free_ds = []
        self.all_ds = []
        self.phase_bufs = []

    def begin_phase(self):
        self.pstack = contextlib.ExitStack()
        self.phase_bufs = []
        self.pid = getattr(self, "pid", 0) + 1

    def end_phase(self):
        self.barrier()
        for b in self.phase_bufs:
            if b.ds is not None:
                self.free_ds.append(b.ds)
                b.ds = None
        self.pstack.close()
        self.pstack = None

    def sbuf(self, name, shape, dtype, glob=False):
        st = self.gstack if glob else self.pstack
        name = name if glob else f"{name}_p{self.pid}"
        t = st.enter_context(self.nc.sbuf_tensor(name, list(shape), dtype))
        b = Buf(t, name)
        if not glob:
            self.phase_bufs.append(b)
        return b

    def psum(self, name, shape, dtype):
        t = self.pstack.enter_context(self.nc.psum_tensor(f"{name}_p{self.pid}", list(shape), dtype))
        b = Buf(t, name)
        b.is_psum = True
        self.phase_bufs.append(b)
        return b

    def dram(self, name, shape, dtype, kind="Internal"):
        t = self.nc.dram_tensor(name, list(shape), dtype, kind=kind).ap()
        return Buf(t, name)

    def _get_ds(self, buf):
        if buf.ds is None:
            if self.free_ds:
                buf.ds = self.free_ds.pop()
            else:
                h = self.gstack.enter_context(self.nc.semaphore(f"d{len(self.all_ds)}"))
                buf.ds = DSem(h)
                self.all_ds.append(buf.ds)
        return buf.ds

    def _esem(self, ename, epoch):
        key = (ename, epoch)
        if key not in self.esems:
            self.esems[key] = self.gstack.enter_context(
                self.nc.semaphore(f"e_{ename}_{epoch}"))
        return self.esems[key]

    def wait(self, ename, tok):
        if tok[0] == "e":
            _, src, idx = tok
            k = ("e", src)
            if self.waited[ename].get(k, 0) >= idx:
                return
            epoch = (idx - 1) // EPOCH
            self.eng[ename].wait_ge(self._esem(src, epoch), idx - epoch * EPOCH)
            self.waited[ename][k] = idx
        else:
            _, ds, val = tok
            k = ("d", id(ds))
            if self.waited[ename].get(k, 0) >= val:
                return
            self.eng[ename].wait_ge(ds.h, val)
            self.waited[ename][k] = val

    def _deps(self, ename, reads, writes, is_dma):
        for b in reads:
            for tok in b.w.values():
                if (not is_dma) and tok[0] == "e" and tok[1] == ename and ename == "pe":
                    continue
                self.wait(ename, tok)
            if b.is_psum:
                for tok in b.r.values():
                    if tok[0] == "e" and tok[1] != ename:
                        self.wait(ename, tok)
        for b in writes:
            for tok in list(b.w.values()) + list(b.r.values()):
                if (not is_dma) and tok[0] == "e" and tok[1] == ename:
                    continue
                self.wait(ename, tok)

    @staticmethod
    def _key(tok):
        return ("e", tok[1]) if tok[0] == "e" else ("d", id(tok[1]))

    def op(self, ename, fn, reads=(), writes=()):
        self._deps(ename, reads, writes, False)
        ins = fn(self.eng[ename])
        self.n[ename] += 1
        idx = self.n[ename]
        epoch = (idx - 1) // EPOCH
        ins.then_inc(self._esem(ename, epoch), 1)
        tok = ("e", ename, idx)
        k = self._key(tok)
        for b in writes:
            b.w[k] = tok
        for b in reads:
            b.r[k] = tok
        return tok

    def dma(self, qname, out_ap, in_ap, sb, reads=(), writes=(), **kw):
        self._deps(qname, reads, writes, True)
        ds = self._get_ds(sb)
        ins = self.eng[qname].dma_start(out=out_ap, in_=in_ap, **kw)
        ds.cnt += 16
        ins.then_inc(ds.h, 16)
        tok = ("d", ds, ds.cnt)
        k = self._key(tok)
        for b in writes:
            b.w[k] = tok
        for b in reads:
            b.r[k] = tok
        return tok

    def barrier(self):
        toks = [("e", e, self.n[e]) for e in self.eng if self.n[e] > 0]
        toks += [("d", ds, ds.cnt) for ds in self.all_ds if ds.cnt > 0]
        for e in self.eng:
            for t in toks:
                if t[0] == "e" and t[1] == e:
                    continue
                self.wait(e, t)


def mm(kb, out, lhsT, rhs, start, stop, reads, writes):
    return kb.op("pe", lambda e: e.matmul(out, lhsT=lhsT, rhs=rhs, start=start, stop=stop),
                 reads=reads, writes=writes)


def act(kb, out, in_, func, reads, writes, **kw):
    return kb.op("act", lambda e: e.activation(out=out, in_=in_, func=func, **kw),
                 reads=reads, writes=writes)


def tt(kb, eng, out, in0, in1, op, reads, writes):
    return kb.op(eng, lambda e: e.tensor_tensor(out=out, in0=in0, in1=in1, op=op),
                 reads=reads, writes=writes)


class C:
    IDENT, LT, GE, LE, ONES, NEGU, NEGONES = range(7)


def cbs(cb, i, n=1):
    return cb[:, i * 128:(i + n) * 128]


def phase_inproj(kb, cb, xsrc, norm, w_in, fm_chunks, tm_pieces, cs128, cs64, pm32=None):
    kb.begin_phase()
    TG = 2048
    NTT = TG // 128
    hT = kb.sbuf("hT", [128, 16, TG], BF16)
    gsb = kb.sbuf("gsb", [128, 16], F32)
    xt = [kb.sbuf(f"xt{i}", [128, D], F32) for i in range(2)]
    xn = [kb.sbuf(f"xn{i}", [128, D], BF16) for i in range(2)]
    sq = kb.sbuf("sq", [128, D], BF16)
    stt = [kb.sbuf(f"stt{i}", [128, 4], F32) for i in range(2)]
    tp = [kb.psum(f"tp{i}", [128, 8, 128], BF16) for i in range(2)]
    pA = [kb.psum(f"pA{i}", [128, 512], F32) for i in range(2)]
    pB = [kb.psum(f"pB{i}", [128, 512], F32) for i in range(2)]
    pT = [kb.psum(f"pT{i}", [128, 512], F32) for i in range(2)]
    wA = [kb.sbuf(f"wA{i}", [128, 16, 128], BF16) for i in range(2)]
    a32 = [kb.sbuf(f"a32_{i}", [128, 512], BF16) for i in range(2)]
    perm = kb.sbuf("perm", [128, 2, 128], BF16)
    if pm32 is not None:
        kb.dma("sp", perm[:], pm32.ap[:, :, :], perm, reads=[pm32], writes=[perm])
    wT = [kb.sbuf(f"wT{i}", [128, 16, 512], BF16) for i in range(2)]
    t1 = [kb.sbuf(f"t1_{i}", [128, 512], F32) for i in range(2)]
    t2 = [kb.sbuf(f"t2_{i}", [128, 512], F32) for i in range(2)]
    ob = [kb.sbuf(f"ob{i}", [128, 512], BF16) for i in range(3)]
    o32 = [kb.sbuf(f"o32_{i}", [128, 512], F32) for i in range(2)]
    otm = [kb.sbuf(f"otm{i}", [128, 512], BF16) for i in range(2)]
    o16 = [kb.sbuf(f"o16_{i}", [128, 16], F32) for i in range(2)]
    csA = kb.sbuf("csA", [128, 2, TG], F32)
    csB = kb.sbuf("csB", [128, 2, TG], F32) if cs64 is not None else None

    kb.dma("sp", gsb[:], norm.ap.rearrange("(k p) -> p k", p=128), gsb, reads=[norm],
           writes=[gsb], allow_slow_non_contiguous=True)
    ident = cbs(cb, C.IDENT)
    wsrc = w_in.ap.rearrange("(k p) n -> p k n", p=128)

    def load_w(dst, segs):
        for (do, so, n) in segs:
            for k0 in range(0, 16, 8):
                kb.dma("pool", dst[:, k0:k0 + 8, do:do + n], wsrc[:, k0:k0 + 8, so:so + n], dst,
                       reads=[w_in], writes=[dst])

    cnt = {"a": 0, "o": 0, "t": 0, "o32": 0, "tm": 0, "otm": 0}
    for G in range(S // TG):
        g0 = G * TG
        if cs128 is not None:
            kb.dma("sp", csA[:], cs128.ap[:, :, g0:g0 + TG].rearrange("c p t -> p c t"), csA,
                   reads=[cs128], writes=[csA])
        if cs64 is not None:
            kb.dma("sp", csB[:], cs64.ap[:, :, g0:g0 + TG].rearrange("c p t -> p c t"), csB,
                   reads=[cs64], writes=[csB])
        for i in range(NTT):
            b = i % 2
            r0 = g0 + i * 128
            kb.dma("sp", xt[b][:], xsrc.ap[r0:r0 + 128, :], xt[b], reads=[xsrc], writes=[xt[b]])
            act(kb, sq[:], xt[b][:], AF.Square, [xt[b]], [sq, stt[b]], accum_out=stt[b][:, 0:1])
            kb.op("dve", lambda e: e.tensor_scalar(out=stt[b][:, 1:2], in0=stt[b][:, 0:1],
                                                   scalar1=1.0 / D, scalar2=EPS, op0=ALU.mult,
                                                   op1=ALU.add), reads=[stt[b]], writes=[stt[b]])
            act(kb, stt[b][:, 2:3], stt[b][:, 1:2], AF.Ln, [stt[b]], [stt[b]])
            act(kb, stt[b][:, 3:4], stt[b][:, 2:3], AF.Exp, [stt[b]], [stt[b]], scale=-0.5)
            kb.op("dve", lambda e: e.tensor_scalar(out=xn[b][:], in0=xt[b][:],
                                                   scalar1=stt[b][:, 3:4], scalar2=None,
                                                   op0=ALU.mult), reads=[xt[b], stt[b]],
                  writes=[xn[b]])
            for half in range(2):
                for k in range(8):
                    kk = half * 8 + k
                    kb.op("pe", lambda e: e.transpose(out=tp[half][:, k, :],
                                                      in_=xn[b][:, kk * 128:(kk + 1) * 128],
                                                      identity=ident),
                          reads=[xn[b], cb], writes=[tp[half]])
                tt(kb, "dve", hT[:, half * 8:(half + 1) * 8, i * 128:(i + 1) * 128],
                   tp[half][:, :, :],
                   gsb[:, half * 8:(half + 1) * 8].unsqueeze(2).broadcast_to([128, 8, 128]),
                   ALU.mult, [tp[half], gsb], [hT])
        nfm = len(fm_chunks)
        if nfm:
            load_w(wA[0], fm_chunks[0]["segA"])
        for ci, ch in enumerate(fm_chunks):
            wa = wA[ci % 2]
            if ci + 1 < nfm:
                nx = fm_chunks[ci + 1]
                load_w(wA[(ci + 1) % 2], nx["segA"])
            nco = ch["ncols"]
            for sub in range(TG // 512):
                tok0 = g0 + sub * 512
                sl = slice(sub * 512, (sub + 1) * 512)
                pa = pA[cnt["a"] % 2]
                pb = pB[cnt["a"] % 2]
                cnt["a"] += 1
                for k in range(16):
                    mm(kb, pa[0:nco, :], wa[:, k, 0:nco], hT[:, k, sl], k == 0, k == 15,
                       [wa, hT], [pa])
                if ch["kind"] == "rope":
                    a_ = a32[cnt["a"] % 2]
                    act(kb, a_[0:nco, :], pa[0:nco, :], AF.Copy, [pa], [a_])
                    pidx = 0 if ch["cs"] == 128 else 1
                    mm(kb, pb[0:nco, :], perm[0:nco, pidx, 0:nco], a_[0:nco, :], True, True,
                       [perm, a_], [pb])
                kind = ch["kind"]
                if kind == "copy32":
                    o = o32[cnt["o32"] % 2]
                    cnt["o32"] += 1
                    act(kb, o[0:nco, :], pa[0:nco, :], AF.Copy, [pa], [o])
                else:
                    o = ob[cnt["o"] % 3]
                    cnt["o"] += 1
                    if kind == "rope":
                        cst = csA if ch["cs"] == 128 else csB
                        a1, a2 = t1[cnt["t"] % 2], t2[cnt["t"] % 2]
                        cnt["t"] += 1
                        sc = float(ch["scale"])
                        kb.op("dve", lambda e: e.scalar_tensor_tensor(
                            out=a1[0:nco, :], in0=pa[0:nco, :], scalar=sc, in1=cst[0:nco, 0, sl],
                            op0=ALU.mult, op1=ALU.mult), reads=[pa, cst], writes=[a1])
                        kb.op("dve", lambda e: e.scalar_tensor_tensor(
                            out=a2[0:nco, :], in0=pb[0:nco, :], scalar=sc, in1=cst[0:nco, 1, sl],
                            op0=ALU.mult, op1=ALU.mult), reads=[pb, cst], writes=[a2])
                        tt(kb, "pool", o[0:nco, :], a1[0:nco, :], a2[0:nco, :], ALU.add,
                           [a1, a2], [o])
                    elif kind == "copy":
                        act(kb, o[0:nco, :], pa[0:nco, :], AF.Copy, [pa], [o],
                            scale=float(ch["scale"]))
                    elif kind == "silu":
                        act(kb, o[0:nco, :], pa[0:nco, :], AF.Silu, [pa], [o])
                dbuf, dap = ch["dest"](tok0)
                kb.dma("sp", dap, o[0:nco, :], o, reads=[o], writes=[dbuf])
        ntm = len(tm_pieces)
        if ntm:
            load_w(wT[0], tm_pieces[0]["seg"])
        for pi, pc in enumerate(tm_pieces):
            wt = wT[pi % 2]
            if pi + 1 < ntm:
                load_w(wT[(pi + 1) % 2], tm_pieces[pi + 1]["seg"])
            nco = pc["ncols"]
            for i in range(NTT):
                r0 = g0 + i * 128
                pt_ = pT[cnt["tm"] % 2]
                cnt["tm"] += 1
                for k in range(16):
                    mm(kb, pt_[:, 0:nco], hT[:, k, i * 128:(i + 1) * 128], wt[:, k, 0:nco],
                       k == 0, k == 15, [hT, wt], [pt_])
                for (off, n, dfn, dt_) in pc["dests"]:
                    if dt_ == "f32":
                        o = o16[cnt["otm"] % 2]
                    else:
                        o = otm[cnt["otm"] % 2]
                    cnt["otm"] += 1
                    act(kb, o[:, 0:n], pt_[:, off:off + n], AF.Copy, [pt_], [o])
                    dbuf, dap = dfn(r0)
                    kb.dma("sp", dap, o[:, 0:n], o, reads=[o], writes=[dbuf])
    kb.end_phase()


def attn_epilogue(kb, OT, DEN, gt, rden, ob, ygT, row0, t0, ncol=512):
    if DEN is not None:
        act(kb, rden[:, 0:ncol], DEN[:, 0:ncol], AF.Ln, [DEN], [rden])
        act(kb, rden[:, 0:ncol], rden[:, 0:ncol], AF.Exp, [rden], [rden], scale=-1.0)
        tt(kb, "pool", rden[:, 0:ncol], rden[:, 0:ncol], gt[:, 0:ncol], ALU.mult,
           [rden, gt], [rden])
        tt(kb, "dve", ob[:, 0:ncol], OT[:, 0:ncol], rden[:, 0:ncol], ALU.mult, [OT, rden], [ob])
    else:
        tt(kb, "dve", ob[:, 0:ncol], OT[:, 0:ncol], gt[:, 0:ncol], ALU.mult, [OT, gt], [ob])
    kb.dma("sp", ygT.ap[row0:row0 + 128, t0:t0 + ncol], ob[:, 0:ncol], ob, reads=[ob],
           writes=[ygT])


NIT = 22


def phase_dsa(kb, cb, qaT, kaT, va, iqT, ikT, iw, gT, ygT):
    kb.begin_phase()
    KT = kb.sbuf("KT", [128, S], BF16)
    VA = kb.sbuf("VA", [128, NB, 128], BF16)
    IK = kb.sbuf("IK", [128, S], BF16)
    iqg = [kb.sbuf(f"iqg{i}", [128, 8, 512], BF16) for i in range(2)]
    iwg = [kb.sbuf(f"iwg{i}", [128, 4, 16], F32) for i in range(2)]
    score = [kb.sbuf(f"score{i}", [128, S], F32) for i in range(2)]
    junk = kb.sbuf("junk", [128, S], BF16)
    junk2 = kb.sbuf("junk2", [128, S], BF16)
    nmd = [kb.sbuf(f"nmd{i}", [128, NIT], F32) for i in range(2)]
    rl = [kb.sbuf(f"rl{i}", [128, 512], F32) for i in range(6)]
    bs = [kb.sbuf(f"bs{i}", [128, 4], F32) for i in range(2)]
    wt = [kb.sbuf(f"wt{i}", [128, NIT], F32) for i in range(2)]
    md = [kb.sbuf(f"md{i}", [128, NIT + 2], F32) for i in range(2)]
    cn = [kb.sbuf(f"cn{i}", [128, NIT], F32) for i in range(2)]
    tq = [kb.sbuf(f"tq{i}", [128, NIT], F32) for i in range(2)]
    pw = kb.sbuf("pw", [128, NIT], F32)
    mbf = [kb.sbuf(f"mbf{i}", [128, S], BF16) for i in range(2)]
    maskT = [kb.sbuf(f"maskT{i}", [128, NB, 512], BF16) for i in range(2)]
    QT = [kb.sbuf(f"QT{i}", [128, 512], BF16) for i in range(2)]
    PT = [kb.sbuf(f"PT{i}", [128, 512], BF16) for i in range(3)]
    PM = [kb.sbuf(f"PM{i}", [128, 512], BF16) for i in range(4)]
    gt = [kb.sbuf(f"gt{i}", [128, 512], BF16) for i in range(2)]
    rden = [kb.sbuf(f"rden{i}", [128, 512], F32) for i in range(2)]
    ob = [kb.sbuf(f"obd{i}", [128, 512], BF16) for i in range(2)]
    pd = [kb.psum(f"pd{i}", [128, 512], F32) for i in range(2)]
    ptp = kb.psum("ptp", [128, 8, 128], BF16)
    pS = [kb.psum(f"pS{i}", [128, 512], F32) for i in range(3)]
    OT = kb.psum("OT", [128, 512], F32)
    DEN = kb.psum("DEN", [128, 512], F32)
    ident = cbs(cb, C.IDENT)
    ones = cbs(cb, C.ONES)

    kb.dma("sp", KT[:], kaT.ap[:, :], KT, reads=[kaT], writes=[KT])
    kb.dma("sp", IK[:], ikT.ap[:, :], IK, reads=[ikT], writes=[IK])
    vsrc = va.ap.rearrange("(b p) c -> p b c", p=128)
    for b0 in range(0, NB, 8):
        kb.dma("sp", VA[:, b0:b0 + 8, :], vsrc[:, b0:b0 + 8, :], VA, reads=[va], writes=[VA])
    for i in range(NIT):
        kb.op("dve", lambda e: e.memset(pw[:, i:i + 1], (0.5 ** (i + 1)) * 1.000001), writes=[pw])

    cnt = {"pd": 0, "q": 0, "ob": 0, "jb": 0}

    def idx_tasks(g):
        tasks = []
        t0 = g * 512
        iq_, iw_ = iqg[g % 2], iwg[g % 2]
        mT = maskT[g % 2]

        def loads():
            kb.dma("sp", iq_[:], iqT.ap[:, :, t0:t0 + 512].rearrange("c p t -> p c t"), iq_,
                   reads=[iqT], writes=[iq_])
            kb.dma("sp", iw_[:], iw.ap[t0:t0 + 512, :].rearrange("(j p) h -> p j h", p=128), iw_,
                   reads=[iw], writes=[iw_])
        parts = []
        for j in range(4):
            tasks = []
            qb = 4 * g + j
            n = (qb + 1) * 128
            jb = (4 * g + j) % 2
            sc, mb = score[jb], mbf[jb]
            b_, w_, m_, c_, t_ = bs[jb], wt[jb], md[jb], cn[jb], tq[jb]
            nch = (n + 511) // 512
            for chn in range(nch):
                w = min(512, n - chn * 512)
                cs = slice(chn * 512, chn * 512 + w)
                for h in range(16):
                    def idx_unit(h=h, w=w, cs=cs, j=j, sc=sc):
                        c, half = h // 2, h % 2
                        ps_ = slice(half * 64, (half + 1) * 64)
                        p = pd[cnt["pd"] % 2]
                        r = rl[cnt["pd"] % 6]
                        cnt["pd"] += 1
                        mm(kb, p[:, 0:w], iq_[ps_, c, j * 128:(j + 1) * 128], IK[ps_, cs], True,
                           True, [iq_, IK], [p])
                        act(kb, r[:, 0:w], p[:, 0:w], AF.Relu, [p], [r])
                        if h == 0:
                            kb.op("dve", lambda e: e.tensor_scalar(
                                out=sc[:, cs], in0=r[:, 0:w], scalar1=iw_[:, j, 0:1], scalar2=None,
                                op0=ALU.mult), reads=[r, iw_], writes=[sc])
                        else:
                            kb.op("dve", lambda e: e.scalar_tensor_tensor(
                                out=sc[:, cs], in0=r[:, 0:w], scalar=iw_[:, j, h:h + 1],
                                in1=sc[:, cs], op0=ALU.mult, op1=ALU.add), reads=[r, iw_, sc],
                                writes=[sc])
                    tasks.append(idx_unit)

            def causal(qb=qb, sc=sc):
                dsl = slice(qb * 128, (qb + 1) * 128)
                kb.op("pool", lambda e: e.affine_select(
                    out=sc[:, dsl], in_=sc[:, dsl], pattern=[[-1, 128]], compare_op=ALU.is_ge,
                    fill=NEG, base=0, channel_multiplier=1), reads=[sc], writes=[sc])
            tasks.append(causal)
            idx_part = tasks
            tasks = []
            if qb >= 2:
                def bracket(n=n, sc=sc, b_=b_, w_=w_, m_=m_, c_=c_):
                    kb.op("dve", lambda e: e.tensor_reduce(out=b_[:, 0:1], in_=sc[:, 0:n],
                                                           axis=mybir.AxisListType.X, op=ALU.max),
                          reads=[sc], writes=[b_])
                    kb.op("dve", lambda e: e.tensor_reduce(out=b_[:, 1:2], in_=sc[:, 0:n - 128],
                                                           axis=mybir.AxisListType.X, op=ALU.min),
                          reads=[sc], writes=[b_])
                    tt(kb, "dve", b_[:, 2:3], b_[:, 0:1], b_[:, 1:2], ALU.subtract, [b_], [b_])
                    kb.op("dve", lambda e: e.tensor_scalar(out=w_[:, :], in0=pw[:, :],
                                                           scalar1=b_[:, 2:3], scalar2=None,
                                                           op0=ALU.mult), reads=[pw, b_], writes=[w_])
                    kb.op("dve", lambda e: e.memset(c_[:, :], 0.0), writes=[c_])
                    tt(kb, "dve", m_[:, 0:1], b_[:, 1:2], w_[:, 0:1], ALU.add, [b_, w_], [m_])
                tasks.append(bracket)
                for it in range(NIT):
                    def bis(it=it, n=n, sc=sc, w_=w_, m_=m_, c_=c_, t_=t_, nm_=nmd[jb]):
                        if (it % 5) in (0, 2, 4):
                            kb.op("dve", lambda e: e.tensor_scalar(
                                out=nm_[:, it:it + 1], in0=m_[:, it:it + 1], scalar1=-1.0,
                                scalar2=None, op0=ALU.mult), reads=[m_], writes=[nm_])
                            act(kb, junk2[:, 0:n], sc[:, 0:n], AF.Sign, [sc, nm_], [junk2, c_],
                                bias=nm_[:, it:it + 1], accum_out=c_[:, it:it + 1])
                            thr_c = 511.0 - n
                        else:
                            kb.op("dve", lambda e: e.tensor_scalar(
                                out=junk[:, 0:n], in0=sc[:, 0:n], scalar1=m_[:, it:it + 1],
                                scalar2=0.0, op0=ALU.is_ge, op1=ALU.add,
                                accum_out=c_[:, it:it + 1]), reads=[sc, m_], writes=[junk, c_])
                            thr_c = 255.5
                        kb.op("dve", lambda e: e.tensor_scalar(
                            out=t_[:, it:it + 1], in0=c_[:, it:it + 1], scalar1=thr_c, scalar2=0.5,
                            op0=ALU.is_ge, op1=ALU.subtract), reads=[c_], writes=[t_])
                        kb.op("dve", lambda e: e.scalar_tensor_tensor(
                            out=m_[:, it + 1:it + 2], in0=t_[:, it:it + 1], scalar=w_[:, it:it + 1],
                            in1=m_[:, it:it + 1], op0=ALU.mult, op1=ALU.add), reads=[t_, w_, m_],
                            writes=[m_])
                    tasks.append(bis)

                def mk(n=n, sc=sc, mb=mb, w_=w_, m_=m_):
                    kb.op("dve", lambda e: e.scalar_tensor_tensor(
                        out=m_[:, NIT + 1:NIT + 2], in0=w_[:, NIT - 1:NIT], scalar=-0.5,
                        in1=m_[:, NIT:NIT + 1], op0=ALU.mult, op1=ALU.add), reads=[w_, m_],
                        writes=[m_])
                    kb.op("dve", lambda e: e.tensor_scalar(
                        out=mb[:, 0:n], in0=sc[:, 0:n], scalar1=m_[:, NIT + 1:NIT + 2],
                        scalar2=None, op0=ALU.is_ge), reads=[sc, m_], writes=[mb])
                tasks.append(mk)
            else:
                def mk(n=n, sc=sc, mb=mb):
                    kb.op("dve", lambda e: e.tensor_scalar(
                        out=mb[:, 0:n], in0=sc[:, 0:n], scalar1=-1.0e30, scalar2=None,
                        op0=ALU.is_ge), reads=[sc], writes=[mb])
                tasks.append(mk)
            for k0 in range(0, qb + 1, 8):
                def tr(k0=k0, qb=qb, mb=mb, j=j):
                    nk = min(8, qb + 1 - k0)
                    for kk in range(nk):
                        kb.op("pe", lambda e: e.transpose(
                            out=ptp[:, kk, :], in_=mb[:, (k0 + kk) * 128:(k0 + kk + 1) * 128],
                            identity=ident), reads=[mb, cb], writes=[ptp])
                    act(kb, mT[:, k0:k0 + nk, j * 128:(j + 1) * 128], ptp[:, 0:nk, :], AF.Copy,
                        [ptp], [mT])
                tasks.append(tr)
            parts.append((idx_part, tasks))

        def merge(a, b):
            out, ia, ib = [], 0, 0
            na, nb_ = len(a), len(b)
            while ia < na or ib < nb_:
                if ib >= nb_ or (ia < na and ia * nb_ <= ib * na):
                    out.append(a[ia])
                    ia += 1
                else:
                    out.append(b[ib])
                    ib += 1
            return out

        res = [loads] + parts[0][0]
        for j in range(3):
            res += merge(parts[j][1], parts[j + 1][0])
        res += parts[3][1]
        return res

    for t in idx_tasks(0):
        t()
    for g in range(8):
        t0 = g * 512
        mT = maskT[g % 2]
        nxt = idx_tasks(g + 1) if g + 1 < 8 else []
        nkb = 4 * (g + 1)
        total_units = 10 * nkb
        done_tasks = 0
        unit_i = 0
        for h in range(10):
            q = QT[cnt["q"] % 2]
            gg = gt[cnt["q"] % 2]
            rd = rden[cnt["q"] % 2]
            cnt["q"] += 1
            kb.dma("sp", q[:], qaT.ap[h, :, t0:t0 + 512], q, reads=[qaT], writes=[q])
            kb.dma("sp", gg[:], gT.ap[h * 128:(h + 1) * 128, t0:t0 + 512], gg, reads=[gT],
                   writes=[gg])

            def qk(kbi):
                c0 = max(0, kbi - 4 * g) * 128
                p = pS[kbi % 3]
                mm(kb, p[:, c0:512], KT[:, kbi * 128:(kbi + 1) * 128], q[:, c0:512], True, True,
                   [KT, q], [p])

            def pv(kbi):
                c0 = max(0, kbi - 4 * g) * 128
                pm = PM[kbi % 4]
                mm(kb, OT[:, c0:512], VA[:, kbi, :], pm[:, c0:512], kbi == 0, kbi == nkb - 1,
                   [VA, pm], [OT])
                mm(kb, DEN[:, c0:512], ones, pm[:, c0:512], kbi == 0, kbi == nkb - 1,
                   [cb, pm], [DEN])

            qk(0)
            if nkb > 1:
                qk(1)
            for kbi in range(nkb):
                if kbi + 2 < nkb:
                    qk(kbi + 2)
                c0 = max(0, kbi - 4 * g) * 128
                p = pS[kbi % 3]
                pt_ = PT[kbi % 3]
                pm = PM[kbi % 4]
                act(kb, pt_[:, c0:512], p[:, c0:512], AF.Exp, [p], [pt_])
                tt(kb, "pool", pm[:, c0:512], pt_[:, c0:512], mT[:, kbi, c0:512], ALU.mult,
                   [pt_, mT], [pm])
                if kbi >= 1:
                    pv(kbi - 1)
                unit_i += 1
                target = (len(nxt) * unit_i) // total_units
                while done_tasks < target:
                    nxt[done_tasks]()
                    done_tasks += 1
            pv(nkb - 1)
            o = ob[cnt["ob"] % 2]
            cnt["ob"] += 1
            attn_epilogue(kb, OT, DEN, gg, rd, o, ygT, h * 128, t0)
        while done_tasks < len(nxt):
            nxt[done_tasks]()
            done_tasks += 1
    kb.end_phase()


def phase_dilated(kb, cb, qbT, kbT, vb, gT, ygT):
    kb.begin_phase()
    QT = [kb.sbuf(f"dQ{i}", [128, S], BF16) for i in range(2)]
    KT = [kb.sbuf(f"dK{i}", [128, S], BF16) for i in range(2)]
    VC = [kb.sbuf(f"dV{i}", [128, NB, 128], BF16) for i in range(2)]
    acc = kb.sbuf("acc", [128, S], F32)
    dacc = kb.sbuf("dacc", [128, S], F32)
    gt = kb.sbuf("dgt", [128, S], BF16)
    ob = kb.sbuf("dob", [128, S], BF16)
    PT = [kb.sbuf(f"dPT{i}", [128, 2, 128], BF16) for i in range(3)]
    PM = [kb.sbuf(f"dPM{i}", [128, 2, 128], BF16) for i in range(3)]
    pS = [kb.psum(f"dpS{i}", [128, 512], F32) for i in range(2)]
    pO = [kb.psum(f"dpO{i}", [128, 512], F32) for i in range(2)]
    pD = [kb.psum(f"dpD{i}", [128, 512], F32) for i in range(2)]
    ones = cbs(cb, C.ONES)
    band = cb[:, C.GE * 128:(C.GE + 2) * 128].rearrange("p (a c) -> p a c", a=2)
    u = 0
    vi = 0
    for h in range(6):
        q, k = QT[h % 2], KT[h % 2]
        kb.dma("sp", q[:], qbT.ap[h, :, :], q, reads=[qbT], writes=[q])
        kb.dma("sp", k[:], kbT.ap[h, :, :], k, reads=[kbT], writes=[k])
        kb.dma("sp", gt[:], gT.ap[1280 + h * 128:1280 + (h + 1) * 128, :], gt, reads=[gT],
               writes=[gt])
        for pi, dil in enumerate((1, 4, 16)):
            nb = NB // dil
            v = VC[vi % 2]
            vi += 1
            vsrc = vb.ap[:, h * 128:(h + 1) * 128].rearrange("(n i d) c -> i d n c", d=dil, i=128)
            vdst = v[:].rearrange("p (d n) c -> p d n c", d=dil)
            for r in range(dil):
                for n0 in range(0, nb, 8):
                    n1 = min(nb, n0 + 8)
                    kb.dma("sp", vdst[:, r, n0:n1, :], vsrc[:, r, n0:n1, :], v, reads=[vb],
                           writes=[v])
            units = [(r, n) for r in range(dil) for n in range(nb)]

            def cols(r, m):
                s0_ = m * 128 * dil + r
                return slice(s0_, s0_ + 127 * dil + 1, dil)

            def qk(ui, r, n):
                st = pS[ui % 2]
                s0 = 0 if n > 0 else 1
                stv = st[:, 0:256].rearrange("p (a c) -> p a c", a=2)
                for sl_ in range(s0, 2):
                    m = n - 1 + sl_
                    mm(kb, stv[:, sl_, :], k[:, cols(r, m)], q[:, cols(r, n)], True, True, [k, q],
                       [st])

            qk(u, *units[0])
            for li, (r, n) in enumerate(units):
                if li + 1 < len(units):
                    qk(u + 1, *units[li + 1])
                st = pS[u % 2]
                po = pO[u % 2]
                pdn = pD[u % 2]
                pt_ = PT[u % 3]
                pm = PM[u % 3]
                u += 1
                s0 = 0 if n > 0 else 1
                stv = st[:, 0:256].rearrange("p (a c) -> p a c", a=2)
                act(kb, pt_[:, s0:2, :], stv[:, s0:2, :], AF.Exp, [st], [pt_])
                tt(kb, "pool", pm[:, s0:2, :], pt_[:, s0:2, :], band[:, s0:2, :], ALU.mult,
                   [pt_, cb], [pm])
                for sl_ in range(s0, 2):
                    m = n - 1 + sl_
                    mm(kb, po[:, 0:128], v[:, r * nb + m, :], pm[:, sl_, :], sl_ == s0, sl_ == 1,
                       [v, pm], [po])
                for sl_ in range(s0, 2):
                    mm(kb, pdn[:, 0:128], ones, pm[:, sl_, :], sl_ == s0, sl_ == 1, [cb, pm],
                       [pdn])
                cs = cols(r, n)
                if pi == 0:
                    kb.op("dve", lambda e: e.tensor_copy(out=acc[:, cs], in_=po[:, 0:128]),
                          reads=[po], writes=[acc])
                    kb.op("dve", lambda e: e.tensor_copy(out=dacc[:, cs], in_=pdn[:, 0:128]),
                          reads=[pdn], writes=[dacc])
                else:
                    tt(kb, "dve", acc[:, cs], po[:, 0:128], acc[:, cs], ALU.add, [po, acc], [acc])
                    tt(kb, "dve", dacc[:, cs], pdn[:, 0:128], dacc[:, cs], ALU.add,
                       [pdn, dacc], [dacc])
        act(kb, dacc[:], dacc[:], AF.Ln, [dacc], [dacc])
        act(kb, dacc[:], dacc[:], AF.Exp, [dacc], [dacc], scale=-1.0)
        tt(kb, "dve", acc[:], acc[:], dacc[:], ALU.mult, [acc, dacc], [acc])
        tt(kb, "dve", ob[:], acc[:], gt[:], ALU.mult, [acc, gt], [ob])
        kb.dma("sp", ygT.ap[1280 + h * 128:1280 + (h + 1) * 128, :], ob[:], ob, reads=[ob],
               writes=[ygT])
    kb.end_phase()


def phase_sb(kb, cb, qcT, kcT, vc, gT, ygT):
    kb.begin_phase()
    KT = [kb.sbuf(f"sK{i}", [128, S], BF16) for i in range(2)]
    VV = [kb.sbuf(f"sV{i}", [128, NB, 128], BF16) for i in range(2)]
    QT = [kb.sbuf(f"sQ{i}", [128, 512], BF16) for i in range(2)]
    gt = [kb.sbuf(f"sg{i}", [128, 512], BF16) for i in range(2)]
    E = [kb.sbuf(f"sE{i}", [128, 512], F32) for i in range(2)]
    LP = [kb.sbuf(f"sL{i}", [128, 512], BF16) for i in range(3)]
    RS = kb.sbuf("sRS", [128, 512], BF16)
    PT = [kb.sbuf(f"sP{i}", [128, 512], BF16) for i in range(4)]
    ob = [kb.sbuf(f"sob{i}", [128, 512], BF16) for i in range(2)]
    pZ = [kb.psum(f"pZ{i}", [128, 512], F32) for i in range(3)]
    pW = [kb.psum(f"pW{i}", [128, 512], F32) for i in range(2)]
    OT = [kb.psum(f"sOT{i}", [128, 512], F32) for i in range(2)]
    negU = cbs(cb, C.NEGU)
    negones = cbs(cb, C.NEGONES)
    lt = cbs(cb, C.LT)
    u = 0
    for h in range(8):
        k, v = KT[h % 2], VV[h % 2]
        kb.dma("sp", k[:], kcT.ap[h, :, :], k, reads=[kcT], writes=[k])
        vsrc = vc.ap[:, h * 128:(h + 1) * 128].rearrange("(b p) c -> p b c", p=128)
        for b0 in range(0, NB, 8):
            kb.dma("sp", v[:, b0:b0 + 8, :], vsrc[:, b0:b0 + 8, :], v, reads=[vc], writes=[v])
        for g in range(8):
            t0 = g * 512
            q = QT[u % 2]
            gg = gt[u % 2]
            ot = OT[u % 2]
            o = ob[u % 2]
            u += 1
            kb.dma("sp", q[:], qcT.ap[h, :, t0:t0 + 512], q, reads=[qcT], writes=[q])
            kb.dma("sp", gg[:], gT.ap[h * 128:(h + 1) * 128, t0:t0 + 512], gg, reads=[gT],
                   writes=[gg])
            nkb = 4 * (g + 1)
            order = list(range(nkb - 1, -1, -1))

            def zmm(kbi, idx):
                c0 = max(0, kbi - 4 * g) * 128
                mm(kb, pZ[idx % 3][:, c0:512], k[:, kbi * 128:(kbi + 1) * 128], q[:, c0:512],
                   True, True, [k, q], [pZ[idx % 3]])

            def c0_of(kbi):
                return max(0, kbi - 4 * g) * 128

            def st_el(idx):
                kbi = order[idx]
                c0 = c0_of(kbi)
                z, e_, lp = pZ[idx % 3], E[idx % 2], LP[idx % 3]
                act(kb, e_[:, c0:512], z[:, c0:512], AF.Exp, [z], [e_])
                act(kb, lp[:, c0:512], e_[:, c0:512], AF.Ln, [e_], [lp], bias=1.0)
                if kbi >= 4 * g:
                    tt(kb, "pool", lp[:, c0:c0 + 128], lp[:, c0:c0 + 128], lt, ALU.mult,
                       [lp, cb], [lp])

            def st_w(idx):
                kbi = order[idx]
                c0 = c0_of(kbi)
                w, lp = pW[idx % 2], LP[idx % 3]
                mm(kb, w[:, c0:512], k[:, kbi * 128:(kbi + 1) * 128], q[:, c0:512], True, False,
                   [k, q], [w])
                mm(kb, w[:, c0:512], negU, lp[:, c0:512], False, idx == 0, [cb, lp], [w])
                if idx > 0:
                    mm(kb, w[:, c0:512], negones, RS[:, c0:512], False, True, [cb, RS], [w])
                if idx == 0:
                    kb.op("dve", lambda e: e.memset(RS[:], 0.0), writes=[RS])
                if idx + 1 < nkb:
                    tt(kb, "dve", RS[:, c0:512], RS[:, c0:512], lp[:, c0:512], ALU.add, [RS, lp],
                       [RS])

            def st_p(idx):
                kbi = order[idx]
                c0 = c0_of(kbi)
                w, pt_ = pW[idx % 2], PT[idx % 4]
                act(kb, pt_[:, c0:512], w[:, c0:512], AF.Exp, [w], [pt_])
                if kbi >= 4 * g:
                    tt(kb, "pool", pt_[:, c0:c0 + 128], pt_[:, c0:c0 + 128], lt, ALU.mult,
                       [pt_, cb], [pt_])

            def st_pv(idx):
                kbi = order[idx]
                c0 = c0_of(kbi)
                pt_ = PT[idx % 4]
                mm(kb, ot[:, c0:512], v[:, kbi, :], pt_[:, c0:512], idx == 0, idx == nkb - 1,
                   [v, pt_], [ot])

            zmm(order[0], 0)
            if nkb > 1:
                zmm(order[1], 1)
            for i in range(nkb + 3):
                if i < nkb:
                    st_el(i)
                if i + 2 < nkb:
                    zmm(order[i + 2], i + 2)
                if 0 <= i - 1 < nkb:
                    st_w(i - 1)
                if 0 <= i - 2 < nkb:
                    st_p(i - 2)
                if 0 <= i - 3 < nkb:
                    st_pv(i - 3)
            attn_epilogue(kb, ot, None, gg, None, o, ygT, h * 128, t0)
    kb.end_phase()


def phase_fox_prep(kb, flT, b_f, fA, fB):
    kb.begin_phase()
    fl = kb.sbuf("fl", [8, S], F32)
    bf = kb.sbuf("bf", [8, 2], F32)
    e_ = kb.sbuf("fe", [8, S], F32)
    ones = kb.sbuf("fones", [8, S], F32)
    cc = kb.sbuf("fcc", [8, S], F32)
    rr = kb.sbuf("frr", [8, S], F32)
    A = kb.sbuf("fAs", [8, 6, S], BF16)
    B = kb.sbuf("fBs", [8, 6, S], BF16)
    kb.dma("sp", fl[:], flT.ap[:, :], fl, reads=[flT], writes=[fl])
    kb.dma("sp", bf[:, 0:1], b_f.ap.rearrange("(h o) -> h o", o=1), bf, reads=[b_f], writes=[bf])
    kb.op("dve", lambda e: e.tensor_scalar(out=bf[:, 1:2], in0=bf[:, 0:1], scalar1=-1.0,
                                           scalar2=None, op0=ALU.mult), reads=[bf], writes=[bf])
    act(kb, e_[:], fl[:], AF.Exp, [fl, bf], [e_], scale=-1.0, bias=bf[:, 1:2])
    act(kb, e_[:], e_[:], AF.Ln, [e_], [e_], bias=1.0)
    kb.op("dve", lambda e: e.memset(ones[:], 1.0), writes=[ones])
    kb.op("dve", lambda e: e.tensor_tensor_scan(out=cc[:], data0=ones[:], data1=e_[:], initial=0.0,
                                                op0=ALU.mult, op1=ALU.add), reads=[ones, e_],
          writes=[cc])
    cur = cc
    for i in range(3):
        kb.op("dve", lambda e: e.tensor_copy(out=A[:, i, :], in_=cur[:]), reads=[cur], writes=[A])
        kb.op("dve", lambda e: e.tensor_scalar(out=B[:, 3 + i, :], in0=A[:, i, :], scalar1=-1.0,
                                               scalar2=None, op0=ALU.mult), reads=[A], writes=[B])
        if i < 2:
            nxt = rr if cur is cc else cc
            tt(kb, "dve", nxt[:], cur[:], A[:, i, :], ALU.subtract, [cur, A], [nxt])
            cur = nxt
    kb.op("dve", lambda e: e.memset(A[:, 3:6, :], 1.0), writes=[A])
    kb.op("dve", lambda e: e.memset(B[:, 0:3, :], 1.0), writes=[B])
    kb.dma("sp", fA.ap[:, :, :], A[:], A, reads=[A], writes=[fA])
    kb.dma("sp", fB.ap[:, :, :], B[:], B, reads=[B], writes=[fB])
    kb.end_phase()


def phase_fox(kb, cb, qdT, kdT, vd, fA, fB, gT, ygT):
    kb.begin_phase()
    KT = [kb.sbuf(f"fK{i}", [128, S], BF16) for i in range(2)]
    VV = [kb.sbuf(f"fV{i}", [128, NB, 128], BF16) for i in range(2)]
    AA = [kb.sbuf(f"fA{i}", [6, S], BF16) for i in range(2)]
    BB = [kb.sbuf(f"fB{i}", [6, S], BF16) for i in range(2)]
    QT = [kb.sbuf(f"fQ{i}", [128, 512], BF16) for i in range(2)]
    gt = [kb.sbuf(f"fg{i}", [128, 512], BF16) for i in range(2)]
    PT = [kb.sbuf(f"fP{i}", [128, 512], BF16) for i in range(4)]
    PS = [kb.sbuf(f"fPS{i}", [128, 512], F32) for i in range(2)]
    PH = [kb.sbuf(f"fPH{i}", [128, 512], BF16) for i in range(2)]
    PL = [kb.sbuf(f"fPL{i}", [128, 512], BF16) for i in range(2)]
    rden = [kb.sbuf(f"frden{i}", [128, 512], F32) for i in range(2)]
    ob = [kb.sbuf(f"fob{i}", [128, 512], BF16) for i in range(2)]
    pS = [kb.psum(f"fpS{i}", [128, 512], F32) for i in range(3)]
    OT = [kb.psum(f"fOT{i}", [128, 512], F32) for i in range(2)]
    DEN = [kb.psum(f"fDEN{i}", [128, 512], F32) for i in range(2)]
    ones = cbs(cb, C.ONES)
    le = cbs(cb, C.LE)
    u = 0
    pending = []
    for h in range(8):
        k, v, A, B = KT[h % 2], VV[h % 2], AA[h % 2], BB[h % 2]
        kb.dma("sp", k[:], kdT.ap[h, :, :], k, reads=[kdT], writes=[k])
        kb.dma("sp", A[:], fA.ap[h, :, :], A, reads=[fA], writes=[A])
        kb.dma("sp", B[:], fB.ap[h, :, :], B, reads=[fB], writes=[B])
        vsrc = vd.ap[:, h * 128:(h + 1) * 128].rearrange("(b p) c -> p b c", p=128)
        for b0 in range(0, NB, 8):
            kb.dma("sp", v[:, b0:b0 + 8, :], vsrc[:, b0:b0 + 8, :], v, reads=[vd], writes=[v])
        for g in range(8):
            t0 = g * 512
            q = QT[u % 2]
            gg = gt[u % 2]
            ot = OT[u % 2]
            dn = DEN[u % 2]
            o = ob[u % 2]
            ps, ph, pl, rd = PS[u % 2], PH[u % 2], PL[u % 2], rden[u % 2]
            u += 1
            kb.dma("sp", q[:], qdT.ap[h, :, t0:t0 + 512], q, reads=[qdT], writes=[q])
            kb.dma("sp", gg[:], gT.ap[1024 + h * 128:1024 + (h + 1) * 128, t0:t0 + 512], gg,
                   reads=[gT], writes=[gg])
            nkb = 4 * (g + 1)

            def qk(kbi):
                c0 = max(0, kbi - 4 * g) * 128
                p = pS[kbi % 3]
                mm(kb, p[:, c0:512], k[:, kbi * 128:(kbi + 1) * 128], q[:, c0:512], True, False,
                   [k, q], [p])
                mm(kb, p[:, c0:512], A[:, kbi * 128:(kbi + 1) * 128], B[:, t0 + c0:t0 + 512],
                   False, True, [A, B], [p])

            def pv(kbi):
                c0 = max(0, kbi - 4 * g) * 128
                pt_ = PT[kbi % 4]
                mm(kb, ot[:, c0:512], v[:, kbi, :], pt_[:, c0:512], kbi == 0, kbi == nkb - 1,
                   [v, pt_], [ot])

            qk(0)
            if nkb > 1:
                qk(1)
            while pending:
                pending.pop(0)()
            for kbi in range(nkb):
                if kbi + 2 < nkb:
                    qk(kbi + 2)
                c0 = max(0, kbi - 4 * g) * 128
                p = pS[kbi % 3]
                pt_ = PT[kbi % 4]
                act(kb, pt_[:, c0:512], p[:, c0:512], AF.Exp, [p], [pt_])
                if kbi >= 4 * g:
                    tt(kb, "pool", pt_[:, c0:c0 + 128], pt_[:, c0:c0 + 128], le, ALU.mult,
                       [pt_, cb], [pt_])
                if kbi == 0:
                    kb.op("dve", lambda e: e.tensor_copy(out=ps[:, :], in_=pt_[:, :]), reads=[pt_],
                          writes=[ps])
                else:
                    tt(kb, "dve", ps[:, c0:512], ps[:, c0:512], pt_[:, c0:512], ALU.add, [ps, pt_],
                       [ps])
                if kbi >= 1:
                    pv(kbi - 1)
            pv(nkb - 1)
            kb.op("dve", lambda e: e.tensor_copy(out=ph[:, :], in_=ps[:, :]), reads=[ps], writes=[ph])
            tt(kb, "dve", pl[:, :], ps[:, :], ph[:, :], ALU.subtract, [ps, ph], [pl])

            def epi(ot=ot, dn=dn, gg=gg, rd=rd, o=o, ph=ph, pl=pl, row0=1024 + h * 128, t0=t0):
                mm(kb, dn[:, :], ones, ph[:, :], True, False, [cb, ph], [dn])
                mm(kb, dn[:, :], ones, pl[:, :], False, True, [cb, pl], [dn])
                attn_epilogue(kb, ot, dn, gg, rd, o, ygT, row0, t0)
            pending.append(epi)
    while pending:
        pending.pop(0)()
    kb.end_phase()


def phase_outproj(kb, ygT, w_out, xres, xdst, norm_f):
    kb.begin_phase()
    W = kb.sbuf("oW", [128, 16, D], BF16)
    yg = [kb.sbuf(f"oyg{i}", [128, 16, 512], BF16) for i in range(2)]
    xr = [kb.sbuf(f"oxr{i}", [128, D], F32) for i in range(2)]
    xw = [kb.sbuf(f"oxw{i}", [128, D], F32) for i in range(2)]
    pp = [kb.psum(f"opp{i}", [128, 512], F32) for i in range(4)]
    if norm_f is not None:
        nf = kb.sbuf("onf", [128, D], F32)
        sq = kb.sbuf("osq", [128, D], BF16)
        stt = [kb.sbuf(f"ost{i}", [128, 4], F32) for i in range(2)]
        kb.dma("sp", nf[:], norm_f.ap.partition_broadcast(128), nf, reads=[norm_f], writes=[nf])
    wsrc = w_out.ap.rearrange("(k p) n -> p k n", p=128)
    for k0 in range(0, 16, 4):
        kb.dma("pool", W[:, k0:k0 + 4, :], wsrc[:, k0:k0 + 4, :], W, reads=[w_out], writes=[W])
    ysrc = ygT.ap.rearrange("(k p) t -> p k t", p=128)
    ti = 0
    for g in range(8):
        y = yg[g % 2]
        for k0 in range(0, 16, 8):
            kb.dma("sp", y[:, k0:k0 + 8, :], ysrc[:, k0:k0 + 8, g * 512:(g + 1) * 512], y,
                   reads=[ygT], writes=[y])
        for j in range(4):
            r0 = g * 512 + j * 128
            xi, xo = xr[ti % 2], xw[ti % 2]
            kb.dma("sp", xi[:], xres.ap[r0:r0 + 128, :], xi, reads=[xres], writes=[xi])
            for n in range(4):
                p = pp[n]
                for k in range(16):
                    mm(kb, p[:, :], y[:, k, j * 128:(j + 1) * 128], W[:, k, n * 512:(n + 1) * 512],
                       k == 0, k == 15, [y, W], [p])
                tt(kb, "dve", xo[:, n * 512:(n + 1) * 512], p[:, :], xi[:, n * 512:(n + 1) * 512],
                   ALU.add, [p, xi], [xo])
            if norm_f is None:
                kb.dma("sp", xdst.ap[r0:r0 + 128, :], xo[:], xo, reads=[xo], writes=[xdst])
            else:
                st_ = stt[ti % 2]
                act(kb, sq[:], xo[:], AF.Square, [xo], [sq, st_], accum_out=st_[:, 0:1])
                kb.op("dve", lambda e: e.tensor_scalar(out=st_[:, 1:2], in0=st_[:, 0:1],
                                                       scalar1=1.0 / D, scalar2=EPS, op0=ALU.mult,
                                                       op1=ALU.add), reads=[st_], writes=[st_])
                act(kb, st_[:, 2:3], st_[:, 1:2], AF.Ln, [st_], [st_])
                act(kb, st_[:, 3:4], st_[:, 2:3], AF.Exp, [st_], [st_], scale=-0.5)
                kb.op("dve", lambda e: e.scalar_tensor_tensor(
                    out=xi[:], in0=xo[:], scalar=st_[:, 3:4], in1=nf[:], op0=ALU.mult,
                    op1=ALU.mult), reads=[xo, st_, nf], writes=[xi])
                kb.dma("sp", xdst.ap[r0:r0 + 128, :], xi[:], xi, reads=[xi], writes=[xdst])
            ti += 1
    kb.end_phase()


def rope_segs(c0, n, half):
    segs = []
    for hs in range(0, n, 2 * half):
        segs.append((hs, c0 + hs + half, half))
        segs.append((hs + half, c0 + hs, half))
    return segs


def build(phases=None, debug=False):
    nc = bass.Bass("TRN2", target_bir_lowering=False)
    kb = KB(nc)
    P = (lambda p: True) if phases is None else (lambda p: p in phases)

    def ein(name, shape, dt=F32):
        return kb.dram(name, shape, dt, kind="ExternalInput")

    def scr(name, shape, dt):
        return kb.dram(name, shape, dt, kind="ExternalOutput")

    x = ein("x", [S, D])
    norm0 = ein("norm0", [D])
    w_in0 = ein("w_in0", [D, IN0])
    w_out0 = ein("w_out0", [D, D])
    norm1 = ein("norm1", [D])
    w_in1 = ein("w_in1", [D, IN1])
    b_f1 = ein("b_f1", [8])
    w_out1 = ein("w_out1", [D, D])
    norm_f = ein("norm_f", [D])
    cs128 = ein("cs128", [2, 128, S])
    cs64 = ein("cs64", [2, 128, S])
    cbd = ein("cb", [128, 7 * 128], BF16)
    pm32 = ein("pm32", [128, 2, 128], BF16)
    out = kb.dram("out", [S, D], F32, kind="ExternalOutput")

    qaT = scr("qaT", [10, 128, S], BF16)
    kaT = scr("kaT", [128, S], BF16)
    va = scr("va", [S, 128], BF16)
    iqT = scr("iqT", [8, 128, S], BF16)
    ikT = scr("ikT", [128, S], BF16)
    iw = scr("iw", [S, 16], F32)
    qbT = scr("qbT", [6, 128, S], BF16)
    kbT = scr("kbT", [6, 128, S], BF16)
    vb = scr("vb", [S, 768], BF16)
    g0T = scr("g0T", [D, S], BF16)
    yg0T = scr("yg0T", [D, S], BF16)
    x1 = scr("x1", [S, D], F32)
    qcT = scr("qcT", [8, 128, S], BF16)
    kcT = scr("kcT", [8, 128, S], BF16)
    vc = scr("vc", [S, 1024], BF16)
    qdT = scr("qdT", [8, 128, S], BF16)
    kdT = scr("kdT", [8, 128, S], BF16)
    vd = scr("vd", [S, 1024], BF16)
    flT = scr("flT", [8, S], F32)
    fA = scr("fA", [8, 6, S], BF16)
    fB = scr("fB", [8, 6, S], BF16)
    g1T = scr("g1T", [D, S], BF16)
    yg1T = scr("yg1T", [D, S], BF16)

    cb = kb.sbuf("cb_sb", [128, 7 * 128], BF16, glob=True)
    kb.dma("sp", cb[:], cbd.ap[:, :], cb, reads=[cbd], writes=[cb])

    def fm(c0, n, kind, dest, scale=1.0, cs=None, dup=False):
        half = 64 if cs == 128 else 32
        if dup:
            segA = [(0, c0, n), (n, c0, n)]
            segB = [(d, s, m) for (d, s, m) in rope_segs(c0, n, half)] + \
                   [(d + n, s, m) for (d, s, m) in rope_segs(c0, n, half)]
            n = 2 * n
        else:
            segA = [(0, c0, n)]
            segB = rope_segs(c0, n, half) if kind == "rope" else None
        return dict(segA=segA, segB=segB, ncols=n, kind=kind, scale=scale, dest=dest, cs=cs)

    def dest3(buf, h):
        return lambda tok0: (buf, buf.ap[h, :, tok0:tok0 + 512])

    def dest2(buf, r0, n=128):
        return lambda tok0: (buf, buf.ap[r0:r0 + n, tok0:tok0 + 512])

    def tdest(buf, c0, n):
        return lambda r0: (buf, buf.ap[r0:r0 + 128, c0:c0 + n])

    if P("in0"):
        ch = []
        for h in range(10):
            ch.append(fm(128 * h, 128, "rope", dest3(qaT, h), SC128, 128))
        ch.append(fm(1280, 128, "rope", dest2(kaT, 0), 1.0, 128))
        for c in range(8):
            ch.append(fm(1536 + 128 * c, 128, "rope", dest3(iqT, c), 1.0, 64))
        ch.append(fm(2560, 64, "rope", dest2(ikT, 0), 1.0, 64, dup=True))
        for h in range(6):
            ch.append(fm(2640 + 128 * h, 128, "rope", dest3(qbT, h), SC128, 128))
        for h in range(6):
            ch.append(fm(3408 + 128 * h, 128, "rope", dest3(kbT, h), 1.0, 128))
        for c in range(16):
            ch.append(fm(4944 + 128 * c, 128, "silu", dest2(g0T, 128 * c)))
        tm = [
            dict(seg=[(0, 1408, 128), (128, 4176, 384)], ncols=512,
                 dests=[(0, 128, tdest(va, 0, 128), "bf"), (128, 384, tdest(vb, 0, 384), "bf")]),
            dict(seg=[(0, 4560, 384), (384, 2624, 16)], ncols=400,
                 dests=[(0, 384, tdest(vb, 384, 384), "bf"), (384, 16, tdest(iw, 0, 16), "f32")]),
        ]
        phase_inproj(kb, cb, x, norm0, w_in0, ch, tm, cs128, cs64, pm32)
    if P("dsa"):
        phase_dsa(kb, cb, qaT, kaT, va, iqT, ikT, iw, g0T, yg0T)
    if P("dil"):
        phase_dilated(kb, cb, qbT, kbT, vb, g0T, yg0T)
    if P("out0"):
        phase_outproj(kb, yg0T, w_out0, x, x1, None)
    if P("in1"):
        ch = []
        for h in range(8):
            ch.append(fm(128 * h, 128, "copy", dest3(qcT, h), 1.0))
        for h in range(8):
            ch.append(fm(1024 + 128 * h, 128, "copy", dest3(kcT, h), SC128))
        for h in range(8):
            ch.append(fm(3072 + 128 * h, 128, "copy", dest3(qdT, h), 1.0))
        for h in range(8):
            ch.append(fm(4096 + 128 * h, 128, "copy", dest3(kdT, h), SC128))
        ch.append(fm(6144, 8, "copy32", dest2(flT, 0, 8)))
        for c in range(16):
            ch.append(fm(6152 + 128 * c, 128, "silu", dest2(g1T, 128 * c)))
        tm = []
        for i in range(2):
            tm.append(dict(seg=[(0, 2048 + 512 * i, 512)], ncols=512,
                           dests=[(0, 512, tdest(vc, 512 * i, 512), "bf")]))
        for i in range(2):
            tm.append(dict(seg=[(0, 5120 + 512 * i, 512)], ncols=512,
                           dests=[(0, 512, tdest(vd, 512 * i, 512), "bf")]))
        phase_inproj(kb, cb, x1, norm1, w_in1, ch, tm, None, None)
    if P("sb"):
        phase_sb(kb, cb, qcT, kcT, vc, g1T, yg1T)
    if P("fox"):
        phase_fox_prep(kb, flT, b_f1, fA, fB)
        phase_fox(kb, cb, qdT, kdT, vd, fA, fB, g1T, yg1T)
    if P("out1"):
        phase_outproj(kb, yg1T, w_out1, x1, out, norm_f)
    kb.barrier()
    return nc, kb


def host_consts():
    pos = np.arange(S, dtype=np.float32)

    def tables(hd, rows):
        half = hd // 2
        inv = (np.float32(10000.0) ** (-np.arange(half, dtype=np.float32) / np.float32(half))).astype(np.float32)
        ang = pos[None, :] * inv[:, None]
        cos = np.cos(ang).astype(np.float32)
        sin = np.sin(ang).astype(np.float32)
        c = np.zeros((2, rows, S), np.float32)
        for p in range(rows):
            d = p % hd
            c[0, p] = cos[d % half]
            c[1, p] = -sin[d % half] if d < half else sin[d % half]
        return c

    cs128 = tables(128, 128)
    cs64 = tables(64, 128)
    i = np.arange(128)[:, None]
    j = np.arange(128)[None, :]
    blocks = [
        (i == j), (i < j), (i >= j), (i <= j), np.ones((128, 128), bool),
    ]
    cbv = [b.astype(np.float32) for b in blocks]
    cbv.append(-(i >= j).astype(np.float32))
    cbv.append(-np.ones((128, 128), np.float32))
    cb = np.concatenate(cbv, axis=1).astype(ml_dtypes.bfloat16)
    pm = np.zeros((128, 2, 128), np.float32)
    pm_dt = ml_dtypes.bfloat16
    for m in range(128):
        pm[(m + 64) % 128, 0, m] = 1.0
        pm[(m // 64) * 64 + ((m % 64) + 32) % 64, 1, m] = 1.0
    return cs128, cs64, cb, pm.astype(pm_dt)


_CACHE = {}


def kernel(x, norm0, w_in0, w_out0, norm1, w_in1, b_f1, w_out1, norm_f):
    if "nc" not in _CACHE:
        _CACHE["nc"] = build()[0]
        _CACHE["consts"] = host_consts()
    nc = _CACHE["nc"]
    cs128, cs64, cb, pm = _CACHE["consts"]
    f = lambda a: np.ascontiguousarray(np.asarray(a, dtype=np.float32))
    shared = dict(norm0=f(norm0), w_in0=f(w_in0), w_out0=f(w_out0), norm1=f(norm1),
                  w_in1=f(w_in1), b_f1=f(b_f1), w_out1=f(w_out1), norm_f=f(norm_f),
                  cs128=cs128, cs64=cs64, cb=cb, pm32=pm)
    x = np.asarray(x, dtype=np.float32)
    in_maps = [dict(shared, x=np.ascontiguousarray(x[i])) for i in range(8)]
    res = run_bass_kernel_spmd(nc, in_maps, core_ids=list(range(8)))
    return np.stack([np.asarray(r["out"], dtype=np.float32) for r in res.results], axis=0)
```

```python
import contextlib
import math
import numpy as np
import ml_dtypes
import concourse.bass as bass
import concourse.mybir as mybir
from concourse.bass_utils import run_bass_kernel_spmd

F32 = mybir.dt.float32
BF16 = mybir.dt.bfloat16
AF = mybir.ActivationFunctionType
ALU = mybir.AluOpType

S = 4096
D = 2048
NB = 32
EPS = 1e-6
NEG = -3.0e38
SC128 = 128 ** -0.5
IN0 = 6992
IN1 = 8200

EPOCH = 20000


class DSem:
    def __init__(self, handle, q):
        self.h = handle
        self.cnt = 0
        self.q = q


class Buf:
    def __init__(self, ap, name):
        self.ap = ap
        self.name = name
        self.w = {}
        self.r = {}
        self.ds = None
        self.is_psum = False

    def __getitem__(self, k):
        return self.ap[k]


class KB:
    def __init__(self, nc):
        self.nc = nc
        self.eng = {"pe": nc.tensor, "act": nc.scalar, "dve": nc.vector,
                    "pool": nc.gpsimd, "sp": nc.sync}
        self.n = {e: 0 for e in self.eng}
        self.esems = {}
        self.waited = {e: {} for e in self.eng}
        self.gstack = contextlib.ExitStack()
        self.pstack = None
        self.free_ds = {"sp": [], "pool": []}
        self.all_ds = []
        self.phase_bufs = []

    def begin_phase(self):
        self.pstack = contextlib.ExitStack()
        self.phase_bufs = []
        self.pid = getattr(self, "pid", 0) + 1

    def end_phase(self):
        self.barrier()
        for b in self.phase_bufs:
            if b.ds is not None:
                self.free_ds[b.ds.q].append(b.ds)
                b.ds = None
        self.pstack.close()
        self.pstack = None

    def sbuf(self, name, shape, dtype, glob=False):
        st = self.gstack if glob else self.pstack
        name = name if glob else f"{name}_p{self.pid}"
        t = st.enter_context(self.nc.sbuf_tensor(name, list(shape), dtype))
        b = Buf(t, name)
        if not glob:
            self.phase_bufs.append(b)
        return b

    def psum(self, name, shape, dtype):
        t = self.pstack.enter_context(self.nc.psum_tensor(f"{name}_p{self.pid}", list(shape), dtype))
        b = Buf(t, name)
        b.is_psum = True
        self.phase_bufs.append(b)
        return b

    def dram(self, name, shape, dtype, kind="Internal"):
        t = self.nc.dram_tensor(name, list(shape), dtype, kind=kind).ap()
        return Buf(t, name)

    def _get_ds(self, buf, qname):
        if buf.ds is None:
            if self.free_ds[qname]:
                buf.ds = self.free_ds[qname].pop()
            else:
                h = self.gstack.enter_context(self.nc.semaphore(f"d{len(self.all_ds)}"))
                buf.ds = DSem(h, qname)
                self.all_ds.append(buf.ds)
        assert buf.ds.q == qname, (buf.name, buf.ds.q, qname)
        return buf.ds

    def _esem(self, ename, epoch):
        key = (ename, epoch)
        if key not in self.esems:
            self.esems[key] = self.gstack.enter_context(
                self.nc.semaphore(f"e_{ename}_{epoch}"))
        return self.esems[key]

    def wait(self, ename, tok):
        if tok[0] == "e":
            _, src, idx = tok
            k = ("e", src)
            if self.waited[ename].get(k, 0) >= idx:
                return
            epoch = (idx - 1) // EPOCH
            self.eng[ename].wait_ge(self._esem(src, epoch), idx - epoch * EPOCH)
            self.waited[ename][k] = idx
        else:
            _, ds, val = tok
            k = ("d", id(ds))
            if self.waited[ename].get(k, 0) >= val:
                return
            self.eng[ename].wait_ge(ds.h, val)
            self.waited[ename][k] = val

    def _deps(self, ename, reads, writes, is_dma):
        for b in reads:
            for tok in b.w.values():
                if (not is_dma) and tok[0] == "e" and tok[1] == ename and ename == "pe":
                    continue
                self.wait(ename, tok)
            if b.is_psum:
                for tok in b.r.values():
                    if tok[0] == "e" and tok[1] != ename:
                        self.wait(ename, tok)
        for b in writes:
            for tok in list(b.w.values()) + list(b.r.values()):
                if (not is_dma) and tok[0] == "e" and tok[1] == ename:
                    continue
                self.wait(ename, tok)

    @staticmethod
    def _key(tok):
        return ("e", tok[1]) if tok[0] == "e" else ("d", id(tok[1]))

    def op(self, ename, fn, reads=(), writes=()):
        self._deps(ename, reads, writes, False)
        ins = fn(self.eng[ename])
        self.n[ename] += 1
        idx = self.n[ename]
        epoch = (idx - 1) // EPOCH
        ins.then_inc(self._esem(ename, epoch), 1)
        tok = ("e", ename, idx)
        k = self._key(tok)
        for b in writes:
            b.w[k] = tok
        for b in reads:
            b.r[k] = tok
        return tok

    def dma(self, qname, out_ap, in_ap, sb, reads=(), writes=(), **kw):
        self._deps(qname, reads, writes, True)
        ds = self._get_ds(sb, qname)
        ins = self.eng[qname].dma_start(out=out_ap, in_=in_ap, **kw)
        ds.cnt += 16
        ins.then_inc(ds.h, 16)
        tok = ("d", ds, ds.cnt)
        k = self._key(tok)
        for b in writes:
            b.w[k] = tok
        for b in reads:
            b.r[k] = tok
        return tok

    def barrier(self):
        toks = [("e", e, self.n[e]) for e in self.eng if self.n[e] > 0]
        toks += [("d", ds, ds.cnt) for ds in self.all_ds if ds.cnt > 0]
        for e in self.eng:
            for t in toks:
                if t[0] == "e" and t[1] == e:
                    continue
                self.wait(e, t)


def mm(kb, out, lhsT, rhs, start, stop, reads, writes):
    return kb.op("pe", lambda e: e.matmul(out, lhsT=lhsT, rhs=rhs, start=start, stop=stop),
                 reads=reads, writes=writes)


def act(kb, out, in_, func, reads, writes, **kw):
    return kb.op("act", lambda e: e.activation(out=out, in_=in_, func=func, **kw),
                 reads=reads, writes=writes)


def tt(kb, eng, out, in0, in1, op, reads, writes):
    return kb.op(eng, lambda e: e.tensor_tensor(out=out, in0=in0, in1=in1, op=op),
                 reads=reads, writes=writes)


class C:
    IDENT, LT, GE, LE, ONES, NEGU, NEGONES = range(7)


def cbs(cb, i, n=1):
    return cb[:, i * 128:(i + n) * 128]


def phase_inproj(kb, cb, xsrc, norm, w_in, fm_chunks, tm_pieces, cs128, cs64, pm32=None):
    kb.begin_phase()
    TG = 2048
    NTT = TG // 128
    hT = kb.sbuf("hT", [128, 16, TG], BF16)
    gsb = kb.sbuf("gsb", [128, 16], F32)
    xt = [kb.sbuf(f"xt{i}", [128, D], F32) for i in range(2)]
    xn = [kb.sbuf(f"xn{i}", [128, D], BF16) for i in range(2)]
    sq = kb.sbuf("sq", [128, D], BF16)
    stt = [kb.sbuf(f"stt{i}", [128, 4], F32) for i in range(2)]
    tp = [kb.psum(f"tp{i}", [128, 8, 128], BF16) for i in range(2)]
    pA = [kb.psum(f"pA{i}", [128, 512], F32) for i in range(2)]
    pB = [kb.psum(f"pB{i}", [128, 512], F32) for i in range(2)]
    pT = [kb.psum(f"pT{i}", [128, 512], F32) for i in range(2)]
    wA = [kb.sbuf(f"wA{i}", [128, 16, 128], BF16) for i in range(2)]
    a32 = [kb.sbuf(f"a32_{i}", [128, 512], BF16) for i in range(2)]
    perm = kb.sbuf("perm", [128, 2, 128], BF16)
    if pm32 is not None:
        kb.dma("sp", perm[:], pm32.ap[:, :, :], perm, reads=[pm32], writes=[perm])
    wT = [kb.sbuf(f"wT{i}", [128, 16, 512], BF16) for i in range(2)]
    t1 = [kb.sbuf(f"t1_{i}", [128, 512], F32) for i in range(2)]
    t2 = [kb.sbuf(f"t2_{i}", [128, 512], F32) for i in range(2)]
    ob = [kb.sbuf(f"ob{i}", [128, 512], BF16) for i in range(3)]
    o32 = [kb.sbuf(f"o32_{i}", [128, 512], F32) for i in range(2)]
    otm = [kb.sbuf(f"otm{i}", [128, 512], BF16) for i in range(2)]
    o16 = [kb.sbuf(f"o16_{i}", [128, 16], F32) for i in range(2)]
    csA = kb.sbuf("csA", [128, 2, TG], F32)
    csB = kb.sbuf("csB", [128, 2, TG], F32) if cs64 is not None else None

    kb.dma("sp", gsb[:], norm.ap.rearrange("(k p) -> p k", p=128), gsb, reads=[norm],
           writes=[gsb], allow_slow_non_contiguous=True)
    ident = cbs(cb, C.IDENT)
    wsrc = w_in.ap.rearrange("(k p) n -> p k n", p=128)

    def load_w(dst, segs):
        for (do, so, n) in segs:
            for k0 in range(0, 16, 8):
                kb.dma("pool", dst[:, k0:k0 + 8, do:do + n], wsrc[:, k0:k0 + 8, so:so + n], dst,
                       reads=[w_in], writes=[dst])

    cnt = {"a": 0, "o": 0, "t": 0, "o32": 0, "tm": 0, "otm": 0}
    for G in range(S // TG):
        g0 = G * TG
        if cs128 is not None:
            kb.dma("sp", csA[:], cs128.ap[:, :, g0:g0 + TG].rearrange("c p t -> p c t"), csA,
                   reads=[cs128], writes=[csA])
        if cs64 is not None:
            kb.dma("sp", csB[:], cs64.ap[:, :, g0:g0 + TG].rearrange("c p t -> p c t"), csB,
                   reads=[cs64], writes=[csB])
        for i in range(NTT):
            b = i % 2
            r0 = g0 + i * 128
            kb.dma("sp", xt[b][:], xsrc.ap[r0:r0 + 128, :], xt[b], reads=[xsrc], writes=[xt[b]])
            act(kb, sq[:], xt[b][:], AF.Square, [xt[b]], [sq, stt[b]], accum_out=stt[b][:, 0:1])
            kb.op("dve", lambda e: e.tensor_scalar(out=stt[b][:, 1:2], in0=stt[b][:, 0:1],
                                                   scalar1=1.0 / D, scalar2=EPS, op0=ALU.mult,
                                                   op1=ALU.add), reads=[stt[b]], writes=[stt[b]])
            act(kb, stt[b][:, 2:3], stt[b][:, 1:2], AF.Ln, [stt[b]], [stt[b]])
            act(kb, stt[b][:, 3:4], stt[b][:, 2:3], AF.Exp, [stt[b]], [stt[b]], scale=-0.5)
            kb.op("dve", lambda e: e.tensor_scalar(out=xn[b][:], in0=xt[b][:],
                                                   scalar1=stt[b][:, 3:4], scalar2=None,
                                                   op0=ALU.mult), reads=[xt[b], stt[b]],
                  writes=[xn[b]])
            for half in range(2):
                for k in range(8):
                    kk = half * 8 + k
                    kb.op("pe", lambda e: e.transpose(out=tp[half][:, k, :],
                                                      in_=xn[b][:, kk * 128:(kk + 1) * 128],
                                                      identity=ident),
                          reads=[xn[b], cb], writes=[tp[half]])
                tt(kb, "dve", hT[:, half * 8:(half + 1) * 8, i * 128:(i + 1) * 128],
                   tp[half][:, :, :],
                   gsb[:, half * 8:(half + 1) * 8].unsqueeze(2).broadcast_to([128, 8, 128]),
                   ALU.mult, [tp[half], gsb], [hT])
        nfm = len(fm_chunks)
        if nfm:
            load_w(wA[0], fm_chunks[0]["segA"])
        for ci, ch in enumerate(fm_chunks):
            wa = wA[ci % 2]
            if ci + 1 < nfm:
                nx = fm_chunks[ci + 1]
                load_w(wA[(ci + 1) % 2], nx["segA"])
            nco = ch["ncols"]
            for sub in range(TG // 512):
                tok0 = g0 + sub * 512
                sl = slice(sub * 512, (sub + 1) * 512)
                pa = pA[cnt["a"] % 2]
                pb = pB[cnt["a"] % 2]
                cnt["a"] += 1
                for k in range(16):
                    mm(kb, pa[0:nco, :], wa[:, k, 0:nco], hT[:, k, sl], k == 0, k == 15,
                       [wa, hT], [pa])
                if ch["kind"] == "rope":
                    a_ = a32[cnt["a"] % 2]
                    act(kb, a_[0:nco, :], pa[0:nco, :], AF.Copy, [pa], [a_])
                    pidx = 0 if ch["cs"] == 128 else 1
                    mm(kb, pb[0:nco, :], perm[0:nco, pidx, 0:nco], a_[0:nco, :], True, True,
                       [perm, a_], [pb])
                kind = ch["kind"]
                if kind == "copy32":
                    o = o32[cnt["o32"] % 2]
                    cnt["o32"] += 1
                    act(kb, o[0:nco, :], pa[0:nco, :], AF.Copy, [pa], [o])
                else:
                    o = ob[cnt["o"] % 3]
                    cnt["o"] += 1
                    if kind == "rope":
                        cst = csA if ch["cs"] == 128 else csB
                        a1, a2 = t1[cnt["t"] % 2], t2[cnt["t"] % 2]
                        cnt["t"] += 1
                        sc = float(ch["scale"])
                        kb.op("dve", lambda e: e.scalar_tensor_tensor(
                            out=a1[0:nco, :], in0=pa[0:nco, :], scalar=sc, in1=cst[0:nco, 0, sl],
                            op0=ALU.mult, op1=ALU.mult), reads=[pa, cst], writes=[a1])
                        kb.op("dve", lambda e: e.scalar_tensor_tensor(
                            out=a2[0:nco, :], in0=pb[0:nco, :], scalar=sc, in1=cst[0:nco, 1, sl],
                            op0=ALU.mult, op1=ALU.mult), reads=[pb, cst], writes=[a2])
                        tt(kb, "pool", o[0:nco, :], a1[0:nco, :], a2[0:nco, :], ALU.add,
                           [a1, a2], [o])
                    elif kind == "copy":
                        act(kb, o[0:nco, :], pa[0:nco, :], AF.Copy, [pa], [o],
                            scale=float(ch["scale"]))
                    elif kind == "silu":
                        act(kb, o[0:nco, :], pa[0:nco, :], AF.Silu, [pa], [o])
                dbuf, dap = ch["dest"](tok0)
                kb.dma("sp", dap, o[0:nco, :], o, reads=[o], writes=[dbuf])
        ntm = len(tm_pieces)
        if ntm:
            load_w(wT[0], tm_pieces[0]["seg"])
        for pi, pc in enumerate(tm_pieces):
            wt = wT[pi % 2]
            if pi + 1 < ntm:
                load_w(wT[(pi + 1) % 2], tm_pieces[pi + 1]["seg"])
            nco = pc["ncols"]
            for i in range(NTT):
                r0 = g0 + i * 128
                pt_ = pT[cnt["tm"] % 2]
                cnt["tm"] += 1
                for k in range(16):
                    mm(kb, pt_[:, 0:nco], hT[:, k, i * 128:(i + 1) * 128], wt[:, k, 0:nco],
                       k == 0, k == 15, [hT, wt], [pt_])
                for (off, n, dfn, dt_) in pc["dests"]:
                    if dt_ == "f32":
                        o = o16[cnt["otm"] % 2]
                    else:
                        o = otm[cnt["otm"] % 2]
                    cnt["otm"] += 1
                    act(kb, o[:, 0:n], pt_[:, off:off + n], AF.Copy, [pt_], [o])
                    dbuf, dap = dfn(r0)
                    kb.dma("sp", dap, o[:, 0:n], o, reads=[o], writes=[dbuf])
    kb.end_phase()


def attn_epilogue(kb, OT, DEN, gt, rden, ob, ygT, row0, t0, ncol=512):
    if DEN is not None:
        act(kb, rden[:, 0:ncol], DEN[:, 0:ncol], AF.Ln, [DEN], [rden])
        act(kb, rden[:, 0:ncol], rden[:, 0:ncol], AF.Exp, [rden], [rden], scale=-1.0)
        tt(kb, "pool", rden[:, 0:ncol], rden[:, 0:ncol], gt[:, 0:ncol], ALU.mult,
           [rden, gt], [rden])
        tt(kb, "dve", ob[:, 0:ncol], OT[:, 0:ncol], rden[:, 0:ncol], ALU.mult, [OT, rden], [ob])
    else:
        tt(kb, "dve", ob[:, 0:ncol], OT[:, 0:ncol], gt[:, 0:ncol], ALU.mult, [OT, gt], [ob])
    kb.dma("sp", ygT.ap[row0:row0 + 128, t0:t0 + ncol], ob[:, 0:ncol], ob, reads=[ob],
           writes=[ygT])


NIT = 22


def phase_dsa(kb, cb, qaT, kaT, va, iqT, ikT, iw, gT, ygT):
    kb.begin_phase()
    KT = kb.sbuf("KT", [128, S], BF16)
    VA = kb.sbuf("VA", [128, NB, 128], BF16)
    IK = kb.sbuf("IK", [128, S], BF16)
    iqg = [kb.sbuf(f"iqg{i}", [128, 8, 512], BF16) for i in range(2)]
    iwg = [kb.sbuf(f"iwg{i}", [128, 4, 16], F32) for i in range(2)]
    score = [kb.sbuf(f"score{i}", [128, S], F32) for i in range(2)]
    junk = kb.sbuf("junk", [128, S], BF16)
    junk2 = kb.sbuf("junk2", [128, S], BF16)
    nmd = [kb.sbuf(f"nmd{i}", [128, NIT], F32) for i in range(2)]
    rl = [kb.sbuf(f"rl{i}", [128, 512], F32) for i in range(6)]
    bs = [kb.sbuf(f"bs{i}", [128, 4], F32) for i in range(2)]
    wt = [kb.sbuf(f"wt{i}", [128, NIT], F32) for i in range(2)]
    md = [kb.sbuf(f"md{i}", [128, NIT + 2], F32) for i in range(2)]
    cn = [kb.sbuf(f"cn{i}", [128, NIT], F32) for i in range(2)]
    tq = [kb.sbuf(f"tq{i}", [128, NIT], F32) for i in range(2)]
    pw = kb.sbuf("pw", [128, NIT], F32)
    mbf = [kb.sbuf(f"mbf{i}", [128, S], BF16) for i in range(2)]
    maskT = [kb.sbuf(f"maskT{i}", [128, NB, 512], BF16) for i in range(2)]
    QT = [kb.sbuf(f"QT{i}", [128, 512], BF16) for i in range(2)]
    PT = [kb.sbuf(f"PT{i}", [128, 512], BF16) for i in range(3)]
    PM = [kb.sbuf(f"PM{i}", [128, 512], BF16) for i in range(4)]
    gt = [kb.sbuf(f"gt{i}", [128, 512], BF16) for i in range(2)]
    rden = [kb.sbuf(f"rden{i}", [128, 512], F32) for i in range(2)]
    ob = [kb.sbuf(f"obd{i}", [128, 512], BF16) for i in range(2)]
    pd = [kb.psum(f"pd{i}", [128, 512], F32) for i in range(2)]
    ptp = kb.psum("ptp", [128, 8, 128], BF16)
    pS = [kb.psum(f"pS{i}", [128, 512], F32) for i in range(3)]
    OT = kb.psum("OT", [128, 512], F32)
    DEN = kb.psum("DEN", [128, 512], F32)
    ident = cbs(cb, C.IDENT)
    ones = cbs(cb, C.ONES)

    kb.dma("sp", KT[:], kaT.ap[:, :], KT, reads=[kaT], writes=[KT])
    kb.dma("sp", IK[:], ikT.ap[:, :], IK, reads=[ikT], writes=[IK])
    vsrc = va.ap.rearrange("(b p) c -> p b c", p=128)
    for b0 in range(0, NB, 8):
        kb.dma("sp", VA[:, b0:b0 + 8, :], vsrc[:, b0:b0 + 8, :], VA, reads=[va], writes=[VA])
    for i in range(NIT):
        kb.op("dve", lambda e: e.memset(pw[:, i:i + 1], (0.5 ** (i + 1)) * 1.000001), writes=[pw])

    cnt = {"pd": 0, "q": 0, "ob": 0, "jb": 0}

    def idx_tasks(g):
        tasks = []
        t0 = g * 512
        iq_, iw_ = iqg[g % 2], iwg[g % 2]
        mT = maskT[g % 2]

        def loads():
            kb.dma("sp", iq_[:], iqT.ap[:, :, t0:t0 + 512].rearrange("c p t -> p c t"), iq_,
                   reads=[iqT], writes=[iq_])
            kb.dma("sp", iw_[:], iw.ap[t0:t0 + 512, :].rearrange("(j p) h -> p j h", p=128), iw_,
                   reads=[iw], writes=[iw_])
        parts = []
        for j in range(4):
            tasks = []
            qb = 4 * g + j
            n = (qb + 1) * 128
            jb = (4 * g + j) % 2
            sc, mb = score[jb], mbf[jb]
            b_, w_, m_, c_, t_ = bs[jb], wt[jb], md[jb], cn[jb], tq[jb]
            nch = (n + 511) // 512
            for chn in range(nch):
                w = min(512, n - chn * 512)
                cs = slice(chn * 512, chn * 512 + w)
                for h in range(16):
                    def idx_unit(h=h, w=w, cs=cs, j=j, sc=sc):
                        c, half = h // 2, h % 2
                        ps_ = slice(half * 64, (half + 1) * 64)
                        p = pd[cnt["pd"] % 2]
                        r = rl[cnt["pd"] % 6]
                        cnt["pd"] += 1
                        mm(kb, p[:, 0:w], iq_[ps_, c, j * 128:(j + 1) * 128], IK[ps_, cs], True,
                           True, [iq_, IK], [p])
                        act(kb, r[:, 0:w], p[:, 0:w], AF.Relu, [p], [r])
                        if h == 0:
                            kb.op("dve", lambda e: e.tensor_scalar(
                                out=sc[:, cs], in0=r[:, 0:w], scalar1=iw_[:, j, 0:1], scalar2=None,
                                op0=ALU.mult), reads=[r, iw_], writes=[sc])
                        else:
                            kb.op("dve", lambda e: e.scalar_tensor_tensor(
                                out=sc[:, cs], in0=r[:, 0:w], scalar=iw_[:, j, h:h + 1],
                                in1=sc[:, cs], op0=ALU.mult, op1=ALU.add), reads=[r, iw_, sc],
                                writes=[sc])
                    tasks.append(idx_unit)

            def causal(qb=qb, sc=sc):
                dsl = slice(qb * 128, (qb + 1) * 128)
                kb.op("pool", lambda e: e.affine_select(
                    out=sc[:, dsl], in_=sc[:, dsl], pattern=[[-1, 128]], compare_op=ALU.is_ge,
                    fill=NEG, base=0, channel_multiplier=1), reads=[sc], writes=[sc])
            tasks.append(causal)
            idx_part = tasks
            tasks = []
            if qb >= 2:
                def bracket(n=n, sc=sc, b_=b_, w_=w_, m_=m_, c_=c_):
                    kb.op("dve", lambda e: e.tensor_reduce(out=b_[:, 0:1], in_=sc[:, 0:n],
                                                           axis=mybir.AxisListType.X, op=ALU.max),
                          reads=[sc], writes=[b_])
                    kb.op("dve", lambda e: e.tensor_reduce(out=b_[:, 1:2], in_=sc[:, 0:n - 128],
                                                           axis=mybir.AxisListType.X, op=ALU.min),
                          reads=[sc], writes=[b_])
                    tt(kb, "dve", b_[:, 2:3], b_[:, 0:1], b_[:, 1:2], ALU.subtract, [b_], [b_])
                    kb.op("dve", lambda e: e.tensor_scalar(out=w_[:, :], in0=pw[:, :],
                                                           scalar1=b_[:, 2:3], scalar2=None,
                                                           op0=ALU.mult), reads=[pw, b_], writes=[w_])
                    kb.op("dve", lambda e: e.memset(c_[:, :], 0.0), writes=[c_])
                    tt(kb, "dve", m_[:, 0:1], b_[:, 1:2], w_[:, 0:1], ALU.add, [b_, w_], [m_])
                tasks.append(bracket)
                for it in range(NIT):
                    def bis(it=it, n=n, sc=sc, w_=w_, m_=m_, c_=c_, t_=t_, nm_=nmd[jb]):
                        if (it % 5) in (0, 2, 4):
                            kb.op("dve", lambda e: e.tensor_scalar(
                                out=nm_[:, it:it + 1], in0=m_[:, it:it + 1], scalar1=-1.0,
                                scalar2=None, op0=ALU.mult), reads=[m_], writes=[nm_])
                            act(kb, junk2[:, 0:n], sc[:, 0:n], AF.Sign, [sc, nm_], [junk2, c_],
                                bias=nm_[:, it:it + 1], accum_out=c_[:, it:it + 1])
                            thr_c = 511.0 - n
                        else:
                            kb.op("dve", lambda e: e.tensor_scalar(
                                out=junk[:, 0:n], in0=sc[:, 0:n], scalar1=m_[:, it:it + 1],
                                scalar2=0.0, op0=ALU.is_ge, op1=ALU.add,
                                accum_out=c_[:, it:it + 1]), reads=[sc, m_], writes=[junk, c_])
                            thr_c = 255.5
                        kb.op("dve", lambda e: e.tensor_scalar(
                            out=t_[:, it:it + 1], in0=c_[:, it:it + 1], scalar1=thr_c, scalar2=0.5,
                            op0=ALU.is_ge, op1=ALU.subtract), reads=[c_], writes=[t_])
                        kb.op("dve", lambda e: e.scalar_tensor_tensor(
                            out=m_[:, it + 1:it + 2], in0=t_[:, it:it + 1], scalar=w_[:, it:it + 1],
                            in1=m_[:, it:it + 1], op0=ALU.mult, op1=ALU.add), reads=[t_, w_, m_],
                            writes=[m_])
                    tasks.append(bis)

                def mk(n=n, sc=sc, mb=mb, w_=w_, m_=m_):
                    kb.op("dve", lambda e: e.scalar_tensor_tensor(
                        out=m_[:, NIT + 1:NIT + 2], in0=w_[:, NIT - 1:NIT], scalar=-0.5,
                        in1=m_[:, NIT:NIT + 1], op0=ALU.mult, op1=ALU.add), reads=[w_, m_],
                        writes=[m_])
                    kb.op("dve", lambda e: e.tensor_scalar(
                        out=mb[:, 0:n], in0=sc[:, 0:n], scalar1=m_[:, NIT + 1:NIT + 2],
                        scalar2=None, op0=ALU.is_ge), reads=[sc, m_], writes=[mb])
                tasks.append(mk)
            else:
                def mk(n=n, sc=sc, mb=mb):
                    kb.op("dve", lambda e: e.tensor_scalar(
                        out=mb[:, 0:n], in0=sc[:, 0:n], scalar1=-1.0e30, scalar2=None,
                        op0=ALU.is_ge), reads=[sc], writes=[mb])
                tasks.append(mk)
            for k0 in range(0, qb + 1, 8):
                def tr(k0=k0, qb=qb, mb=mb, j=j):
                    nk = min(8, qb + 1 - k0)
                    for kk in range(nk):
                        kb.op("pe", lambda e: e.transpose(
                            out=ptp[:, kk, :], in_=mb[:, (k0 + kk) * 128:(k0 + kk + 1) * 128],
                            identity=ident), reads=[mb, cb], writes=[ptp])
                    act(kb, mT[:, k0:k0 + nk, j * 128:(j + 1) * 128], ptp[:, 0:nk, :], AF.Copy,
                        [ptp], [mT])
                tasks.append(tr)
            parts.append((idx_part, tasks))

        def merge(a, b):
            out, ia, ib = [], 0, 0
            na, nb_ = len(a), len(b)
            while ia < na or ib < nb_:
                if ib >= nb_ or (ia < na and ia * nb_ <= ib * na):
                    out.append(a[ia])
                    ia += 1
                else:
                    out.append(b[ib])
                    ib += 1
            return out

        res = [loads] + parts[0][0]
        for j in range(3):
            res += merge(parts[j][1], parts[j + 1][0])
        res += parts[3][1]
        return res

    def ld_qg(g_, h_, uu):
        q_, gg_ = QT[uu % 2], gt[uu % 2]
        kb.dma("sp", q_[:], qaT.ap[h_, :, g_ * 512:(g_ + 1) * 512], q_, reads=[qaT], writes=[q_])
        kb.dma("sp", gg_[:], gT.ap[h_ * 128:(h_ + 1) * 128, g_ * 512:(g_ + 1) * 512], gg_,
               reads=[gT], writes=[gg_])

    ld_qg(0, 0, 0)
    for t in idx_tasks(0):
        t()
    for g in range(8):
        t0 = g * 512
        mT = maskT[g % 2]
        nxt = idx_tasks(g + 1) if g + 1 < 8 else []
        nkb = 4 * (g + 1)
        total_units = 10 * nkb
        done_tasks = 0
        unit_i = 0
        for h in range(10):
            q = QT[cnt["q"] % 2]
            gg = gt[cnt["q"] % 2]
            rd = rden[cnt["q"] % 2]
            cnt["q"] += 1

            def qk(kbi):
                c0 = max(0, kbi - 4 * g) * 128
                p = pS[kbi % 3]
                mm(kb, p[:, c0:512], KT[:, kbi * 128:(kbi + 1) * 128], q[:, c0:512], True, True,
                   [KT, q], [p])

            def pv(kbi):
                c0 = max(0, kbi - 4 * g) * 128
                pm = PM[kbi % 4]
                mm(kb, OT[:, c0:512], VA[:, kbi, :], pm[:, c0:512], kbi == 0, kbi == nkb - 1,
                   [VA, pm], [OT])
                mm(kb, DEN[:, c0:512], ones, pm[:, c0:512], kbi == 0, kbi == nkb - 1,
                   [cb, pm], [DEN])

            qk(0)
            if nkb > 1:
                qk(1)
            for kbi in range(nkb):
                if kbi + 2 < nkb:
                    qk(kbi + 2)
                c0 = max(0, kbi - 4 * g) * 128
                p = pS[kbi % 3]
                pt_ = PT[kbi % 3]
                pm = PM[kbi % 4]
                act(kb, pt_[:, c0:512], p[:, c0:512], AF.Exp, [p], [pt_])
                tt(kb, "pool", pm[:, c0:512], pt_[:, c0:512], mT[:, kbi, c0:512], ALU.mult,
                   [pt_, mT], [pm])
                if kbi >= 1:
                    pv(kbi - 1)
                unit_i += 1
                target = (len(nxt) * unit_i) // total_units
                while done_tasks < target:
                    nxt[done_tasks]()
                    done_tasks += 1
            pv(nkb - 1)
            if h + 1 < 10:
                ld_qg(g, h + 1, cnt["q"])
            elif g + 1 < 8:
                ld_qg(g + 1, 0, cnt["q"])
            o = ob[cnt["ob"] % 2]
            cnt["ob"] += 1
            attn_epilogue(kb, OT, DEN, gg, rd, o, ygT, h * 128, t0)
        while done_tasks < len(nxt):
            nxt[done_tasks]()
            done_tasks += 1
    kb.end_phase()


def phase_dilated(kb, cb, qbT, kbT, vb, gT, ygT):
    kb.begin_phase()
    QT = [kb.sbuf(f"dQ{i}", [128, S], BF16) for i in range(2)]
    KT = [kb.sbuf(f"dK{i}", [128, S], BF16) for i in range(2)]
    VC = [kb.sbuf(f"dV{i}", [128, NB, 128], BF16) for i in range(2)]
    acc = kb.sbuf("acc", [128, S], F32)
    dacc = kb.sbuf("dacc", [128, S], F32)
    gt = kb.sbuf("dgt", [128, S], BF16)
    ob = kb.sbuf("dob", [128, S], BF16)
    PT = [kb.sbuf(f"dPT{i}", [128, 2, 128], BF16) for i in range(3)]
    PM = [kb.sbuf(f"dPM{i}", [128, 2, 128], BF16) for i in range(3)]
    pS = [kb.psum(f"dpS{i}", [128, 512], F32) for i in range(2)]
    pO = [kb.psum(f"dpO{i}", [128, 512], F32) for i in range(2)]
    pD = [kb.psum(f"dpD{i}", [128, 512], F32) for i in range(2)]
    ones = cbs(cb, C.ONES)
    band = cb[:, C.GE * 128:(C.GE + 2) * 128].rearrange("p (a c) -> p a c", a=2)
    u = 0
    vi = 0
    for h in range(6):
        q, k = QT[h % 2], KT[h % 2]
        kb.dma("sp", q[:], qbT.ap[h, :, :], q, reads=[qbT], writes=[q])
        kb.dma("sp", k[:], kbT.ap[h, :, :], k, reads=[kbT], writes=[k])
        kb.dma("sp", gt[:], gT.ap[1280 + h * 128:1280 + (h + 1) * 128, :], gt, reads=[gT],
               writes=[gt])
        for pi, dil in enumerate((1, 4, 16)):
            nb = NB // dil
            v = VC[vi % 2]
            vi += 1
            vsrc = vb.ap[:, h * 128:(h + 1) * 128].rearrange("(n i d) c -> i d n c", d=dil, i=128)
            vdst = v[:].rearrange("p (d n) c -> p d n c", d=dil)
            for r in range(dil):
                for n0 in range(0, nb, 8):
                    n1 = min(nb, n0 + 8)
                    kb.dma("sp", vdst[:, r, n0:n1, :], vsrc[:, r, n0:n1, :], v, reads=[vb],
                           writes=[v])
            units = [(r, n) for r in range(dil) for n in range(nb)]

            def cols(r, m):
                s0_ = m * 128 * dil + r
                return slice(s0_, s0_ + 127 * dil + 1, dil)

            def qk(ui, r, n):
                st = pS[ui % 2]
                s0 = 0 if n > 0 else 1
                stv = st[:, 0:256].rearrange("p (a c) -> p a c", a=2)
                for sl_ in range(s0, 2):
                    m = n - 1 + sl_
                    mm(kb, stv[:, sl_, :], k[:, cols(r, m)], q[:, cols(r, n)], True, True, [k, q],
                       [st])

            qk(u, *units[0])
            for li, (r, n) in enumerate(units):
                if li + 1 < len(units):
                    qk(u + 1, *units[li + 1])
                st = pS[u % 2]
                po = pO[u % 2]
                pdn = pD[u % 2]
                pt_ = PT[u % 3]
                pm = PM[u % 3]
                u += 1
                s0 = 0 if n > 0 else 1
                stv = st[:, 0:256].rearrange("p (a c) -> p a c", a=2)
                act(kb, pt_[:, s0:2, :], stv[:, s0:2, :], AF.Exp, [st], [pt_])
                tt(kb, "pool", pm[:, s0:2, :], pt_[:, s0:2, :], band[:, s0:2, :], ALU.mult,
                   [pt_, cb], [pm])
                for sl_ in range(s0, 2):
                    m = n - 1 + sl_
                    mm(kb, po[:, 0:128], v[:, r * nb + m, :], pm[:, sl_, :], sl_ == s0, sl_ == 1,
                       [v, pm], [po])
                for sl_ in range(s0, 2):
                    mm(kb, pdn[:, 0:128], ones, pm[:, sl_, :], sl_ == s0, sl_ == 1, [cb, pm],
                       [pdn])
                cs = cols(r, n)
                if pi == 0:
                    kb.op("dve", lambda e: e.tensor_copy(out=acc[:, cs], in_=po[:, 0:128]),
                          reads=[po], writes=[acc])
                    kb.op("dve", lambda e: e.tensor_copy(out=dacc[:, cs], in_=pdn[:, 0:128]),
                          reads=[pdn], writes=[dacc])
                else:
                    tt(kb, "dve", acc[:, cs], po[:, 0:128], acc[:, cs], ALU.add, [po, acc], [acc])
                    tt(kb, "dve", dacc[:, cs], pdn[:, 0:128], dacc[:, cs], ALU.add,
                       [pdn, dacc], [dacc])
        act(kb, dacc[:], dacc[:], AF.Ln, [dacc], [dacc])
        act(kb, dacc[:], dacc[:], AF.Exp, [dacc], [dacc], scale=-1.0)
        tt(kb, "dve", acc[:], acc[:], dacc[:], ALU.mult, [acc, dacc], [acc])
        tt(kb, "dve", ob[:], acc[:], gt[:], ALU.mult, [acc, gt], [ob])
        kb.dma("sp", ygT.ap[1280 + h * 128:1280 + (h + 1) * 128, :], ob[:], ob, reads=[ob],
               writes=[ygT])
    kb.end_phase()


def phase_sb(kb, cb, qcT, kcT, vc, gT, ygT):
    kb.begin_phase()
    KT = [kb.sbuf(f"sK{i}", [128, S], BF16) for i in range(2)]
    VV = [kb.sbuf(f"sV{i}", [128, NB, 128], BF16) for i in range(2)]
    QT = [kb.sbuf(f"sQ{i}", [128, 512], BF16) for i in range(2)]
    gt = [kb.sbuf(f"sg{i}", [128, 512], BF16) for i in range(2)]
    E = [kb.sbuf(f"sE{i}", [128, 512], F32) for i in range(2)]
    LP = [kb.sbuf(f"sL{i}", [128, 512], BF16) for i in range(3)]
    RS = kb.sbuf("sRS", [128, 512], BF16)
    PT = [kb.sbuf(f"sP{i}", [128, 512], BF16) for i in range(4)]
    ob = [kb.sbuf(f"sob{i}", [128, 512], BF16) for i in range(2)]
    pZ = [kb.psum(f"pZ{i}", [128, 512], F32) for i in range(3)]
    pW = [kb.psum(f"pW{i}", [128, 512], F32) for i in range(2)]
    OT = [kb.psum(f"sOT{i}", [128, 512], F32) for i in range(2)]
    negU = cbs(cb, C.NEGU)
    negones = cbs(cb, C.NEGONES)
    lt = cbs(cb, C.LT)
    u = 0

    def sb_ld(h_, g_, uu):
        q_, gg_ = QT[uu % 2], gt[uu % 2]
        kb.dma("sp", q_[:], qcT.ap[h_, :, g_ * 512:(g_ + 1) * 512], q_, reads=[qcT], writes=[q_])
        kb.dma("sp", gg_[:], gT.ap[h_ * 128:(h_ + 1) * 128, g_ * 512:(g_ + 1) * 512], gg_,
               reads=[gT], writes=[gg_])

    sb_ld(0, 0, 0)
    for h in range(8):
        k, v = KT[h % 2], VV[h % 2]
        kb.dma("sp", k[:], kcT.ap[h, :, :], k, reads=[kcT], writes=[k])
        vsrc = vc.ap[:, h * 128:(h + 1) * 128].rearrange("(b p) c -> p b c", p=128)
        for b0 in range(0, NB, 8):
            kb.dma("sp", v[:, b0:b0 + 8, :], vsrc[:, b0:b0 + 8, :], v, reads=[vc], writes=[v])
        for g in range(8):
            t0 = g * 512
            q = QT[u % 2]
            gg = gt[u % 2]
            ot = OT[u % 2]
            o = ob[u % 2]
            u += 1
            nkb = 4 * (g + 1)
            order = list(range(nkb - 1, -1, -1))

            def zmm(kbi, idx):
                c0 = max(0, kbi - 4 * g) * 128
                mm(kb, pZ[idx % 3][:, c0:512], k[:, kbi * 128:(kbi + 1) * 128], q[:, c0:512],
                   True, True, [k, q], [pZ[idx % 3]])

            def c0_of(kbi):
                return max(0, kbi - 4 * g) * 128

            def st_el(idx):
                kbi = order[idx]
                c0 = c0_of(kbi)
                z, e_, lp = pZ[idx % 3], E[idx % 2], LP[idx % 3]
                act(kb, e_[:, c0:512], z[:, c0:512], AF.Exp, [z], [e_])
                act(kb, lp[:, c0:512], e_[:, c0:512], AF.Ln, [e_], [lp], bias=1.0)
                if kbi >= 4 * g:
                    tt(kb, "pool", lp[:, c0:c0 + 128], lp[:, c0:c0 + 128], lt, ALU.mult,
                       [lp, cb], [lp])

            def st_w(idx):
                kbi = order[idx]
                c0 = c0_of(kbi)
                w, lp = pW[idx % 2], LP[idx % 3]
                mm(kb, w[:, c0:512], k[:, kbi * 128:(kbi + 1) * 128], q[:, c0:512], True, False,
                   [k, q], [w])
                mm(kb, w[:, c0:512], negU, lp[:, c0:512], False, idx == 0, [cb, lp], [w])
                if idx > 0:
                    mm(kb, w[:, c0:512], negones, RS[:, c0:512], False, True, [cb, RS], [w])
                if idx == 0:
                    kb.op("dve", lambda e: e.memset(RS[:], 0.0), writes=[RS])
                if idx + 1 < nkb:
                    tt(kb, "dve", RS[:, c0:512], RS[:, c0:512], lp[:, c0:512], ALU.add, [RS, lp],
                       [RS])

            def st_p(idx):
                kbi = order[idx]
                c0 = c0_of(kbi)
                w, pt_ = pW[idx % 2], PT[idx % 4]
                act(kb, pt_[:, c0:512], w[:, c0:512], AF.Exp, [w], [pt_])
                if kbi >= 4 * g:
                    tt(kb, "pool", pt_[:, c0:c0 + 128], pt_[:, c0:c0 + 128], lt, ALU.mult,
                       [pt_, cb], [pt_])

            def st_pv(idx):
                kbi = order[idx]
                c0 = c0_of(kbi)
                pt_ = PT[idx % 4]
                mm(kb, ot[:, c0:512], v[:, kbi, :], pt_[:, c0:512], idx == 0, idx == nkb - 1,
                   [v, pt_], [ot])

            zmm(order[0], 0)
            if nkb > 1:
                zmm(order[1], 1)
            for i in range(nkb + 3):
                if i < nkb:
                    st_el(i)
                if i + 2 < nkb:
                    zmm(order[i + 2], i + 2)
                if 0 <= i - 1 < nkb:
                    st_w(i - 1)
                if 0 <= i - 2 < nkb:
                    st_p(i - 2)
                if 0 <= i - 3 < nkb:
                    st_pv(i - 3)
            if g + 1 < 8:
                sb_ld(h, g + 1, u)
            elif h + 1 < 8:
                sb_ld(h + 1, 0, u)
            attn_epilogue(kb, ot, None, gg, None, o, ygT, h * 128, t0)
    kb.end_phase()


def phase_fox_prep(kb, flT, b_f, fA, fB):
    kb.begin_phase()
    fl = kb.sbuf("fl", [8, S], F32)
    bf = kb.sbuf("bf", [8, 2], F32)
    e_ = kb.sbuf("fe", [8, S], F32)
    ones = kb.sbuf("fones", [8, S], F32)
    cc = kb.sbuf("fcc", [8, S], F32)
    rr = kb.sbuf("frr", [8, S], F32)
    A = kb.sbuf("fAs", [8, 6, S], BF16)
    B = kb.sbuf("fBs", [8, 6, S], BF16)
    kb.dma("sp", fl[:], flT.ap[:, :], fl, reads=[flT], writes=[fl])
    kb.dma("sp", bf[:, 0:1], b_f.ap.rearrange("(h o) -> h o", o=1), bf, reads=[b_f], writes=[bf])
    kb.op("dve", lambda e: e.tensor_scalar(out=bf[:, 1:2], in0=bf[:, 0:1], scalar1=-1.0,
                                           scalar2=None, op0=ALU.mult), reads=[bf], writes=[bf])
    act(kb, e_[:], fl[:], AF.Exp, [fl, bf], [e_], scale=-1.0, bias=bf[:, 1:2])
    act(kb, e_[:], e_[:], AF.Ln, [e_], [e_], bias=1.0)
    kb.op("dve", lambda e: e.memset(ones[:], 1.0), writes=[ones])
    kb.op("dve", lambda e: e.tensor_tensor_scan(out=cc[:], data0=ones[:], data1=e_[:], initial=0.0,
                                                op0=ALU.mult, op1=ALU.add), reads=[ones, e_],
          writes=[cc])
    cur = cc
    for i in range(3):
        kb.op("dve", lambda e: e.tensor_copy(out=A[:, i, :], in_=cur[:]), reads=[cur], writes=[A])
        kb.op("dve", lambda e: e.tensor_scalar(out=B[:, 3 + i, :], in0=A[:, i, :], scalar1=-1.0,
                                               scalar2=None, op0=ALU.mult), reads=[A], writes=[B])
        if i < 2:
            nxt = rr if cur is cc else cc
            tt(kb, "dve", nxt[:], cur[:], A[:, i, :], ALU.subtract, [cur, A], [nxt])
            cur = nxt
    kb.op("dve", lambda e: e.memset(A[:, 3:6, :], 1.0), writes=[A])
    kb.op("dve", lambda e: e.memset(B[:, 0:3, :], 1.0), writes=[B])
    kb.dma("sp", fA.ap[:, :, :], A[:], A, reads=[A], writes=[fA])
    kb.dma("sp", fB.ap[:, :, :], B[:], B, reads=[B], writes=[fB])
    kb.end_phase()


def phase_fox(kb, cb, qdT, kdT, vd, fA, fB, gT, ygT):
    kb.begin_phase()
    KT = [kb.sbuf(f"fK{i}", [128, S], BF16) for i in range(2)]
    VV = [kb.sbuf(f"fV{i}", [128, NB, 128], BF16) for i in range(2)]
    AA = [kb.sbuf(f"fA{i}", [6, S], BF16) for i in range(2)]
    BB = [kb.sbuf(f"fB{i}", [6, S], BF16) for i in range(2)]
    QT = [kb.sbuf(f"fQ{i}", [128, 512], BF16) for i in range(2)]
    gt = [kb.sbuf(f"fg{i}", [128, 512], BF16) for i in range(2)]
    PT = [kb.sbuf(f"fP{i}", [128, 512], BF16) for i in range(4)]
    PS = [kb.sbuf(f"fPS{i}", [128, 512], F32) for i in range(2)]
    PH = [kb.sbuf(f"fPH{i}", [128, 512], BF16) for i in range(2)]
    PL = [kb.sbuf(f"fPL{i}", [128, 512], BF16) for i in range(2)]
    rden = [kb.sbuf(f"frden{i}", [128, 512], F32) for i in range(2)]
    ob = [kb.sbuf(f"fob{i}", [128, 512], BF16) for i in range(2)]
    pS = [kb.psum(f"fpS{i}", [128, 512], F32) for i in range(3)]
    OT = [kb.psum(f"fOT{i}", [128, 512], F32) for i in range(2)]
    DEN = [kb.psum(f"fDEN{i}", [128, 512], F32) for i in range(2)]
    ones = cbs(cb, C.ONES)
    le = cbs(cb, C.LE)
    u = 0
    pending = []

    def fx_ld(h_, g_, uu):
        q_, gg_ = QT[uu % 2], gt[uu % 2]
        kb.dma("sp", q_[:], qdT.ap[h_, :, g_ * 512:(g_ + 1) * 512], q_, reads=[qdT], writes=[q_])
        kb.dma("sp", gg_[:], gT.ap[1024 + h_ * 128:1024 + (h_ + 1) * 128, g_ * 512:(g_ + 1) * 512],
               gg_, reads=[gT], writes=[gg_])

    fx_ld(0, 0, 0)
    for h in range(8):
        k, v, A, B = KT[h % 2], VV[h % 2], AA[h % 2], BB[h % 2]
        kb.dma("sp", k[:], kdT.ap[h, :, :], k, reads=[kdT], writes=[k])
        kb.dma("sp", A[:], fA.ap[h, :, :], A, reads=[fA], writes=[A])
        kb.dma("sp", B[:], fB.ap[h, :, :], B, reads=[fB], writes=[B])
        vsrc = vd.ap[:, h * 128:(h + 1) * 128].rearrange("(b p) c -> p b c", p=128)
        for b0 in range(0, NB, 8):
            kb.dma("sp", v[:, b0:b0 + 8, :], vsrc[:, b0:b0 + 8, :], v, reads=[vd], writes=[v])
        for g in range(8):
            t0 = g * 512
            q = QT[u % 2]
            gg = gt[u % 2]
            ot = OT[u % 2]
            dn = DEN[u % 2]
            o = ob[u % 2]
            ps, ph, pl, rd = PS[u % 2], PH[u % 2], PL[u % 2], rden[u % 2]
            u += 1
            nkb = 4 * (g + 1)

            def qk(kbi):
                c0 = max(0, kbi - 4 * g) * 128
                p = pS[kbi % 3]
                mm(kb, p[:, c0:512], k[:, kbi * 128:(kbi + 1) * 128], q[:, c0:512], True, False,
                   [k, q], [p])
                mm(kb, p[:, c0:512], A[:, kbi * 128:(kbi + 1) * 128], B[:, t0 + c0:t0 + 512],
                   False, True, [A, B], [p])

            def pv(kbi):
                c0 = max(0, kbi - 4 * g) * 128
                pt_ = PT[kbi % 4]
                mm(kb, ot[:, c0:512], v[:, kbi, :], pt_[:, c0:512], kbi == 0, kbi == nkb - 1,
                   [v, pt_], [ot])

            qk(0)
            if nkb > 1:
                qk(1)
            while pending:
                pending.pop(0)()
            for kbi in range(nkb):
                if kbi + 2 < nkb:
                    qk(kbi + 2)
                c0 = max(0, kbi - 4 * g) * 128
                p = pS[kbi % 3]
                pt_ = PT[kbi % 4]
                act(kb, pt_[:, c0:512], p[:, c0:512], AF.Exp, [p], [pt_])
                if kbi >= 4 * g:
                    tt(kb, "pool", pt_[:, c0:c0 + 128], pt_[:, c0:c0 + 128], le, ALU.mult,
                       [pt_, cb], [pt_])
                if kbi == 0:
                    kb.op("dve", lambda e: e.tensor_copy(out=ps[:, :], in_=pt_[:, :]), reads=[pt_],
                          writes=[ps])
                else:
                    tt(kb, "dve", ps[:, c0:512], ps[:, c0:512], pt_[:, c0:512], ALU.add, [ps, pt_],
                       [ps])
                if kbi >= 1:
                    pv(kbi - 1)
            pv(nkb - 1)
            if g + 1 < 8:
                fx_ld(h, g + 1, u)
            elif h + 1 < 8:
                fx_ld(h + 1, 0, u)
            kb.op("dve", lambda e: e.tensor_copy(out=ph[:, :], in_=ps[:, :]), reads=[ps], writes=[ph])
            tt(kb, "dve", pl[:, :], ps[:, :], ph[:, :], ALU.subtract, [ps, ph], [pl])

            def epi(ot=ot, dn=dn, gg=gg, rd=rd, o=o, ph=ph, pl=pl, row0=1024 + h * 128, t0=t0):
                mm(kb, dn[:, :], ones, ph[:, :], True, False, [cb, ph], [dn])
                mm(kb, dn[:, :], ones, pl[:, :], False, True, [cb, pl], [dn])
                attn_epilogue(kb, ot, dn, gg, rd, o, ygT, row0, t0)
            pending.append(epi)
    while pending:
        pending.pop(0)()
    kb.end_phase()


def phase_outproj(kb, ygT, w_out, xres, xdst, norm_f):
    kb.begin_phase()
    W = kb.sbuf("oW", [128, 16, D], BF16)
    yg = [kb.sbuf(f"oyg{i}", [128, 16, 512], BF16) for i in range(2)]
    xr = [kb.sbuf(f"oxr{i}", [128, D], F32) for i in range(2)]
    xw = [kb.sbuf(f"oxw{i}", [128, D], F32) for i in range(2)]
    pp = [kb.psum(f"opp{i}", [128, 512], F32) for i in range(4)]
    if norm_f is not None:
        nf = kb.sbuf("onf", [128, D], F32)
        sq = kb.sbuf("osq", [128, D], BF16)
        stt = [kb.sbuf(f"ost{i}", [128, 4], F32) for i in range(2)]
        kb.dma("sp", nf[:], norm_f.ap.partition_broadcast(128), nf, reads=[norm_f], writes=[nf])
    wsrc = w_out.ap.rearrange("(k p) n -> p k n", p=128)
    for k0 in range(0, 16, 4):
        kb.dma("pool", W[:, k0:k0 + 4, :], wsrc[:, k0:k0 + 4, :], W, reads=[w_out], writes=[W])
    ysrc = ygT.ap.rearrange("(k p) t -> p k t", p=128)
    ti = 0
    for g in range(8):
        y = yg[g % 2]
        for k0 in range(0, 16, 8):
            kb.dma("sp", y[:, k0:k0 + 8, :], ysrc[:, k0:k0 + 8, g * 512:(g + 1) * 512], y,
                   reads=[ygT], writes=[y])
        for j in range(4):
            r0 = g * 512 + j * 128
            xi, xo = xr[ti % 2], xw[ti % 2]
            kb.dma("sp", xi[:], xres.ap[r0:r0 + 128, :], xi, reads=[xres], writes=[xi])
            for n in range(4):
                p = pp[n]
                for k in range(16):
                    mm(kb, p[:, :], y[:, k, j * 128:(j + 1) * 128], W[:, k, n * 512:(n + 1) * 512],
                       k == 0, k == 15, [y, W], [p])
                tt(kb, "dve", xo[:, n * 512:(n + 1) * 512], p[:, :], xi[:, n * 512:(n + 1) * 512],
                   ALU.add, [p, xi], [xo])
            if norm_f is None:
                kb.dma("sp", xdst.ap[r0:r0 + 128, :], xo[:], xo, reads=[xo], writes=[xdst])
            else:
                st_ = stt[ti % 2]
                act(kb, sq[:], xo[:], AF.Square, [xo], [sq, st_], accum_out=st_[:, 0:1])
                kb.op("dve", lambda e: e.tensor_scalar(out=st_[:, 1:2], in0=st_[:, 0:1],
                                                       scalar1=1.0 / D, scalar2=EPS, op0=ALU.mult,
                                                       op1=ALU.add), reads=[st_], writes=[st_])
                act(kb, st_[:, 2:3], st_[:, 1:2], AF.Ln, [st_], [st_])
                act(kb, st_[:, 3:4], st_[:, 2:3], AF.Exp, [st_], [st_], scale=-0.5)
                kb.op("dve", lambda e: e.scalar_tensor_tensor(
                    out=xi[:], in0=xo[:], scalar=st_[:, 3:4], in1=nf[:], op0=ALU.mult,
                    op1=ALU.mult), reads=[xo, st_, nf], writes=[xi])
                kb.dma("sp", xdst.ap[r0:r0 + 128, :], xi[:], xi, reads=[xi], writes=[xdst])
            ti += 1
    kb.end_phase()


def rope_segs(c0, n, half):
    segs = []
    for hs in range(0, n, 2 * half):
        segs.append((hs, c0 + hs + half, half))
        segs.append((hs + half, c0 + hs, half))
    return segs


def build(phases=None, debug=False):
    nc = bass.Bass("TRN2", target_bir_lowering=False)
    kb = KB(nc)
    P = (lambda p: True) if phases is None else (lambda p: p in phases)

    def ein(name, shape, dt=F32):
        return kb.dram(name, shape, dt, kind="ExternalInput")

    def scr(name, shape, dt):
        return kb.dram(name, shape, dt, kind="ExternalOutput")

    x = ein("x", [S, D])
    norm0 = ein("norm0", [D])
    w_in0 = ein("w_in0", [D, IN0])
    w_out0 = ein("w_out0", [D, D])
    norm1 = ein("norm1", [D])
    w_in1 = ein("w_in1", [D, IN1])
    b_f1 = ein("b_f1", [8])
    w_out1 = ein("w_out1", [D, D])
    norm_f = ein("norm_f", [D])
    cs128 = ein("cs128", [2, 128, S])
    cs64 = ein("cs64", [2, 128, S])
    cbd = ein("cb", [128, 7 * 128], BF16)
    pm32 = ein("pm32", [128, 2, 128], BF16)
    out = kb.dram("out", [S, D], F32, kind="ExternalOutput")

    qaT = scr("qaT", [10, 128, S], BF16)
    kaT = scr("kaT", [128, S], BF16)
    va = scr("va", [S, 128], BF16)
    iqT = scr("iqT", [8, 128, S], BF16)
    ikT = scr("ikT", [128, S], BF16)
    iw = scr("iw", [S, 16], F32)
    qbT = scr("qbT", [6, 128, S], BF16)
    kbT = scr("kbT", [6, 128, S], BF16)
    vb = scr("vb", [S, 768], BF16)
    g0T = scr("g0T", [D, S], BF16)
    yg0T = scr("yg0T", [D, S], BF16)
    x1 = scr("x1", [S, D], F32)
    qcT = scr("qcT", [8, 128, S], BF16)
    kcT = scr("kcT", [8, 128, S], BF16)
    vc = scr("vc", [S, 1024], BF16)
    qdT = scr("qdT", [8, 128, S], BF16)
    kdT = scr("kdT", [8, 128, S], BF16)
    vd = scr("vd", [S, 1024], BF16)
    flT = scr("flT", [8, S], F32)
    fA = scr("fA", [8, 6, S], BF16)
    fB = scr("fB", [8, 6, S], BF16)
    g1T = scr("g1T", [D, S], BF16)
    yg1T = scr("yg1T", [D, S], BF16)

    cb = kb.sbuf("cb_sb", [128, 7 * 128], BF16, glob=True)
    kb.dma("sp", cb[:], cbd.ap[:, :], cb, reads=[cbd], writes=[cb])

    def fm(c0, n, kind, dest, scale=1.0, cs=None, dup=False):
        half = 64 if cs == 128 else 32
        if dup:
            segA = [(0, c0, n), (n, c0, n)]
            segB = [(d, s, m) for (d, s, m) in rope_segs(c0, n, half)] + \
                   [(d + n, s, m) for (d, s, m) in rope_segs(c0, n, half)]
            n = 2 * n
        else:
            segA = [(0, c0, n)]
            segB = rope_segs(c0, n, half) if kind == "rope" else None
        return dict(segA=segA, segB=segB, ncols=n, kind=kind, scale=scale, dest=dest, cs=cs)

    def dest3(buf, h):
        return lambda tok0: (buf, buf.ap[h, :, tok0:tok0 + 512])

    def dest2(buf, r0, n=128):
        return lambda tok0: (buf, buf.ap[r0:r0 + n, tok0:tok0 + 512])

    def tdest(buf, c0, n):
        return lambda r0: (buf, buf.ap[r0:r0 + 128, c0:c0 + n])

    if P("in0"):
        ch = []
        for h in range(10):
            ch.append(fm(128 * h, 128, "rope", dest3(qaT, h), SC128, 128))
        ch.append(fm(1280, 128, "rope", dest2(kaT, 0), 1.0, 128))
        for c in range(8):
            ch.append(fm(1536 + 128 * c, 128, "rope", dest3(iqT, c), 1.0, 64))
        ch.append(fm(2560, 64, "rope", dest2(ikT, 0), 1.0, 64, dup=True))
        for h in range(6):
            ch.append(fm(2640 + 128 * h, 128, "rope", dest3(qbT, h), SC128, 128))
        for h in range(6):
            ch.append(fm(3408 + 128 * h, 128, "rope", dest3(kbT, h), 1.0, 128))
        for c in range(16):
            ch.append(fm(4944 + 128 * c, 128, "silu", dest2(g0T, 128 * c)))
        tm = [
            dict(seg=[(0, 1408, 128), (128, 4176, 384)], ncols=512,
                 dests=[(0, 128, tdest(va, 0, 128), "bf"), (128, 384, tdest(vb, 0, 384), "bf")]),
            dict(seg=[(0, 4560, 384), (384, 2624, 16)], ncols=400,
                 dests=[(0, 384, tdest(vb, 384, 384), "bf"), (384, 16, tdest(iw, 0, 16), "f32")]),
        ]
        phase_inproj(kb, cb, x, norm0, w_in0, ch, tm, cs128, cs64, pm32)
    if P("dsa"):
        phase_dsa(kb, cb, qaT, kaT, va, iqT, ikT, iw, g0T, yg0T)
    if P("dil"):
        phase_dilated(kb, cb, qbT, kbT, vb, g0T, yg0T)
    if P("out0"):
        phase_outproj(kb, yg0T, w_out0, x, x1, None)
    if P("in1"):
        ch = []
        for h in range(8):
            ch.append(fm(128 * h, 128, "copy", dest3(qcT, h), 1.0))
        for h in range(8):
            ch.append(fm(1024 + 128 * h, 128, "copy", dest3(kcT, h), SC128))
        for h in range(8):
            ch.append(fm(3072 + 128 * h, 128, "copy", dest3(qdT, h), 1.0))
        for h in range(8):
            ch.append(fm(4096 + 128 * h, 128, "copy", dest3(kdT, h), SC128))
        ch.append(fm(6144, 8, "copy32", dest2(flT, 0, 8)))
        for c in range(16):
            ch.append(fm(6152 + 128 * c, 128, "silu", dest2(g1T, 128 * c)))
        tm = []
        for i in range(2):
            tm.append(dict(seg=[(0, 2048 + 512 * i, 512)], ncols=512,
                           dests=[(0, 512, tdest(vc, 512 * i, 512), "bf")]))
        for i in range(2):
            tm.append(dict(seg=[(0, 5120 + 512 * i, 512)], ncols=512,
                           dests=[(0, 512, tdest(vd, 512 * i, 512), "bf")]))
        phase_inproj(kb, cb, x1, norm1, w_in1, ch, tm, None, None)
    if P("sb"):
        phase_sb(kb, cb, qcT, kcT, vc, g1T, yg1T)
    if P("fox"):
        phase_fox_prep(kb, flT, b_f1, fA, fB)
        phase_fox(kb, cb, qdT, kdT, vd, fA, fB, g1T, yg1T)
    if P("out1"):
        phase_outproj(kb, yg1T, w_out1, x1, out, norm_f)
    kb.barrier()
    return nc, kb


def host_consts():
    pos = np.arange(S, dtype=np.float32)

    def tables(hd, rows):
        half = hd // 2
        inv = (np.float32(10000.0) ** (-np.arange(half, dtype=np.float32) / np.float32(half))).astype(np.float32)
        ang = pos[None, :] * inv[:, None]
        cos = np.cos(ang).astype(np.float32)
        sin = np.sin(ang).astype(np.float32)
        c = np.zeros((2, rows, S), np.float32)
        for p in range(rows):
            d = p % hd
            c[0, p] = cos[d % half]
            c[1, p] = -sin[d % half] if d < half else sin[d % half]
        return c

    cs128 = tables(128, 128)
    cs64 = tables(64, 128)
    i = np.arange(128)[:, None]
    j = np.arange(128)[None, :]
    blocks = [
        (i == j), (i < j), (i >= j), (i <= j), np.ones((128, 128), bool),
    ]
    cbv = [b.astype(np.float32) for b in blocks]
    cbv.append(-(i >= j).astype(np.float32))
    cbv.append(-np.ones((128, 128), np.float32))
    cb = np.concatenate(cbv, axis=1).astype(ml_dtypes.bfloat16)
    pm = np.zeros((128, 2, 128), np.float32)
    pm_dt = ml_dtypes.bfloat16
    for m in range(128):
        pm[(m + 64) % 128, 0, m] = 1.0
        pm[(m // 64) * 64 + ((m % 64) + 32) % 64, 1, m] = 1.0
    return cs128, cs64, cb, pm.astype(pm_dt)


_CACHE = {}


def kernel(x, norm0, w_in0, w_out0, norm1, w_in1, b_f1, w_out1, norm_f):
    if "nc" not in _CACHE:
        _CACHE["nc"] = build()[0]
        _CACHE["consts"] = host_consts()
    nc = _CACHE["nc"]
    cs128, cs64, cb, pm = _CACHE["consts"]
    f = lambda a: np.ascontiguousarray(np.asarray(a, dtype=np.float32))
    shared = dict(norm0=f(norm0), w_in0=f(w_in0), w_out0=f(w_out0), norm1=f(norm1),
                  w_in1=f(w_in1), b_f1=f(b_f1), w_out1=f(w_out1), norm_f=f(norm_f),
                  cs128=cs128, cs64=cs64, cb=cb, pm32=pm)
    x = np.asarray(x, dtype=np.float32)
    in_maps = [dict(shared, x=np.ascontiguousarray(x[i])) for i in range(8)]
    res = run_bass_kernel_spmd(nc, in_maps, core_ids=list(range(8)))
    return np.stack([np.asarray(r["out"], dtype=np.float32) for r in res.results], axis=0)
```
